# Optimizing a Trainium2 kernel written in Bass

```python
import math
import jax, jax.numpy as jnp
from jax import lax
import numpy as np

D_MODEL = 2048
BATCH = 4
SEQ = 2048
DEPTH = 2

GRID_W = 64
HEAD_DIM = 128
N_MIX_HEADS = 12
N_KV_HEADS = 4
N_MEM_HEADS = 4
N_MEM = 256
NA_WIN_H = 8
NA_WIN_W = 16
Q_BLOCK = 128
D_FF = 4 * D_MODEL
ROPE_THETA = 10000.0
EPS = 1e-6
N_MIXERS = 2
N_A_LAYERS = (DEPTH + 1) // N_MIXERS
N_B_LAYERS = DEPTH // N_MIXERS
MIX_WIDTH = N_MIX_HEADS * HEAD_DIM
KV_WIDTH = N_KV_HEADS * HEAD_DIM
MEM_WIDTH = N_MEM_HEADS * HEAD_DIM
A_IN_WIDTH = 3 * MIX_WIDTH + MEM_WIDTH
B_IN_WIDTH = MIX_WIDTH + 2 * KV_WIDTH + MEM_WIDTH
CAT_WIDTH = MIX_WIDTH + MEM_WIDTH

kernel_name = "hybrid_natten_axialgqa_encoder"


def rms_norm(x, g):
    xf = x.astype(jnp.float32)
    y = xf * lax.rsqrt(jnp.mean(xf * xf, axis=-1, keepdims=True) + EPS)
    return (y * g.astype(jnp.float32)).astype(x.dtype)


def split_heads(t, n_heads):
    return t.reshape(t.shape[0], t.shape[1], n_heads, HEAD_DIM)


def _rotate(x, ang):
    x1, x2 = jnp.split(x, 2, axis=-1)
    c, s = jnp.cos(ang), jnp.sin(ang)
    return jnp.concatenate([x1 * c - x2 * s, x2 * c + x1 * s], axis=-1)


def axial_rope(x):
    S = x.shape[1]
    t = jnp.arange(S)
    row = (t // GRID_W).astype(jnp.float32)
    col = (t % GRID_W).astype(jnp.float32)
    half = HEAD_DIM // 2
    inv_freq = jnp.power(jnp.float32(ROPE_THETA),
                         -jnp.arange(0, half, 2, dtype=jnp.float32) / half)
    ang_r = (row[:, None] * inv_freq)[:, None, :]
    ang_c = (col[:, None] * inv_freq)[:, None, :]
    xf = x.astype(jnp.float32)
    out = jnp.concatenate([_rotate(xf[..., :half], ang_r),
                           _rotate(xf[..., half:], ang_c)], axis=-1)
    return out.astype(x.dtype)


def neighbourhood_attention(q, k, v, rpb):
    B, S, H, dh = q.shape
    rows = S // GRID_W
    kh = min(NA_WIN_H, rows)
    qg = q.reshape(B, rows, GRID_W, H, dh)
    kg = k.reshape(B, rows, GRID_W, H, dh)
    vg = v.reshape(B, rows, GRID_W, H, dh)
    cols = jnp.arange(GRID_W)
    c0 = jnp.clip(cols - NA_WIN_W // 2, 0, GRID_W - NA_WIN_W)
    col_mask = (cols[None, :] >= c0[:, None]) & (cols[None, :] < c0[:, None] + NA_WIN_W)
    dc_idx = jnp.clip(cols[None, :] - cols[:, None] + NA_WIN_W - 1, 0, 2 * NA_WIN_W - 2)
    scale = HEAD_DIM ** -0.5

    def one_row(r):
        r0 = jnp.clip(r - kh // 2, 0, rows - kh)
        q_r = lax.dynamic_index_in_dim(qg, r, axis=1, keepdims=False)
        k_r = lax.dynamic_slice_in_dim(kg, r0, kh, axis=1)
        v_r = lax.dynamic_slice_in_dim(vg, r0, kh, axis=1)
        dr_idx = r0 + jnp.arange(kh) - r + NA_WIN_H - 1
        bias = rpb[:, dr_idx][:, :, dc_idx]
        bias = bias.transpose(0, 2, 1, 3).astype(jnp.float32)
        s = jnp.einsum('bchd,bikhd->bhcik', q_r, k_r).astype(jnp.float32) * scale + bias[None]
        s = jnp.where(col_mask[None, None, :, None, :], s, -jnp.inf)
        p = jax.nn.softmax(s.reshape(B, H, GRID_W, kh * GRID_W), axis=-1)
        p = p.reshape(B, H, GRID_W, kh, GRID_W).astype(v.dtype)
        return jnp.einsum('bhcik,bikhd->bchd', p, v_r)

    out = lax.map(one_row, jnp.arange(rows))
    return out.transpose(1, 0, 2, 3, 4).reshape(B, S, H * dh)


def gqa_block_attention(q, k, v):
    B, S, H, dh = q.shape
    kvh = k.shape[2]
    g = H // kvh
    nb = S // Q_BLOCK
    qb = q.reshape(B, nb, Q_BLOCK, kvh, g, dh).transpose(1, 0, 3, 4, 2, 5)
    scale = HEAD_DIM ** -0.5

    def one_block(q_blk):
        s = jnp.einsum('bkgqd,bskd->bkgqs', q_blk, k).astype(jnp.float32) * scale
        p = jax.nn.softmax(s, axis=-1).astype(v.dtype)
        return jnp.einsum('bkgqs,bskd->bqkgd', p, v)

    out = lax.map(one_block, qb)
    return out.transpose(1, 0, 2, 3, 4, 5).reshape(B, S, H * dh)


def memory_attention(qm, km, vm):
    B, S, h, dh = qm.shape
    s = jnp.einsum('bshd,bmhd->bhsm', qm, km).astype(jnp.float32) * (HEAD_DIM ** -0.5)
    p = jax.nn.softmax(s, axis=-1).astype(vm.dtype)
    return jnp.einsum('bhsm,bmhd->bshd', p, vm).reshape(B, S, h * dh)


def setup_inputs(seed: int = 0) -> dict:
    key = jax.random.key(seed)
    ks = jax.random.split(key, 16)
    f32 = jnp.float32

    def dense(k, shape, fan_in):
        return jax.random.normal(k, shape, f32) * (fan_in ** -0.5)

    def gain(k, shape):
        return 1.0 + 0.02 * jax.random.normal(k, shape, f32)

    return {
        "x": jax.random.normal(ks[0], (BATCH, SEQ, D_MODEL), f32),
        "mem": jax.random.normal(ks[1], (BATCH, N_MEM, D_MODEL), f32),
        "mem_norm": gain(ks[2], (D_MODEL,)),
        "attn_norm": gain(ks[3], (DEPTH, D_MODEL)),
        "mlp_norm": gain(ks[4], (DEPTH, D_MODEL)),
        "a_w_in": dense(ks[5], (N_A_LAYERS, D_MODEL, A_IN_WIDTH), D_MODEL),
        "a_rpb": 0.1 * jax.random.normal(ks[6], (N_A_LAYERS, N_MIX_HEADS, 2 * NA_WIN_H - 1, 2 * NA_WIN_W - 1), f32),
        "b_w_in": dense(ks[7], (N_B_LAYERS, D_MODEL, B_IN_WIDTH), D_MODEL),
        "b_q_norm": gain(ks[8], (N_B_LAYERS, HEAD_DIM)),
        "b_k_norm": gain(ks[9], (N_B_LAYERS, HEAD_DIM)),
        "w_mem_kv": dense(ks[10], (DEPTH, D_MODEL, 2 * MEM_WIDTH), D_MODEL),
        "w_o": dense(ks[11], (DEPTH, CAT_WIDTH, D_MODEL), CAT_WIDTH),
        "w_up": dense(ks[12], (DEPTH, D_MODEL, D_FF), D_MODEL),
        "w_down": dense(ks[13], (DEPTH, D_FF, D_MODEL), D_FF),
        "final_norm": gain(ks[14], (D_MODEL,)),
    }


def reference(x, mem, mem_norm, attn_norm, mlp_norm, a_w_in, a_rpb, b_w_in, b_q_norm,
              b_k_norm, w_mem_kv, w_o, w_up, w_down, final_norm):
    m = rms_norm(mem, mem_norm)
    h = x
    for i in range(DEPTH):
        n = rms_norm(h, attn_norm[i])
        j = i // N_MIXERS
        if i % N_MIXERS == 0:
            z = n @ a_w_in[j]
            q, k, v, qm = jnp.split(z, [MIX_WIDTH, 2 * MIX_WIDTH, 3 * MIX_WIDTH], axis=-1)
            mix = neighbourhood_attention(split_heads(q, N_MIX_HEADS), split_heads(k, N_MIX_HEADS),
                                          split_heads(v, N_MIX_HEADS), a_rpb[j])
        else:
            z = n @ b_w_in[j]
            q, k, v, qm = jnp.split(z, [MIX_WIDTH, MIX_WIDTH + KV_WIDTH, MIX_WIDTH + 2 * KV_WIDTH], axis=-1)
            q = axial_rope(rms_norm(split_heads(q, N_MIX_HEADS), b_q_norm[j]))
            k = axial_rope(rms_norm(split_heads(k, N_KV_HEADS), b_k_norm[j]))
            mix = gqa_block_attention(q, k, split_heads(v, N_KV_HEADS))
        km, vm = jnp.split(m @ w_mem_kv[i], 2, axis=-1)
        cross = memory_attention(split_heads(qm, N_MEM_HEADS), split_heads(km, N_MEM_HEADS),
                                 split_heads(vm, N_MEM_HEADS))
        h = h + jnp.concatenate([mix, cross], axis=-1) @ w_o[i]
        n = rms_norm(h, mlp_norm[i])
        h = h + jnp.square(jax.nn.relu(n @ w_up[i])) @ w_down[i]
    return rms_norm(h, final_norm)
```

```python
import contextlib
import numpy as np
import ml_dtypes
import concourse.bass as bass
import concourse.mybir as mybir
from concourse.bass_utils import run_bass_kernel_spmd

F32 = mybir.dt.float32
BF16 = mybir.dt.bfloat16
ALU = mybir.AluOpType
AF = mybir.ActivationFunctionType

D = 2048
NTOK = 1024
NEXT = 1280
EPS = 1e-6
SCALE = 128 ** -0.5
MASKV = -30000.0

O_IDENT, O_GAINS, O_ONES, O_PERM, O_EPS = 0, 512, 1024, 1280, 1536
O_MT = 4096
O_KMT, O_VM = 12288, 14336
O_RING = 16384
O_QC = 49152
O_BIG = 81920
TOT = 208896


class Prog:
    ENGS = ("sync", "scalar", "vector", "gpsimd", "tensor")

    def __init__(self, nc, stack):
        self.nc = nc
        self.stack = stack
        self.ops = {e: [] for e in self.ENGS}
        self.sem = {}
        self.cnt = {}

    def _sem(self, key):
        if key not in self.sem:
            self.sem[key] = self.stack.enter_context(self.nc.semaphore(key))
            self.cnt[key] = 0
        return self.sem[key]

    def op(self, eng, fn, waits=(), sig=True):
        tok = None
        if sig:
            key = "e_" + eng
            self._sem(key)
            self.cnt[key] += 1
            tok = (key, self.cnt[key])
        self.ops[eng].append((fn, tuple(w for w in waits if w is not None), tok, 1))
        return tok

    def last(self, eng):
        key = "e_" + eng
        if key in self.cnt and self.cnt[key] > 0:
            return (key, self.cnt[key])
        return None

    def dma(self, eng, out, in_, semkey, waits=()):
        return self.custom(eng, lambda e, out=out, in_=in_: e.dma_start(out=out, in_=in_), semkey, waits)

    def custom(self, eng, fn, semkey, waits=(), inc=16):
        self._sem(semkey)
        self.cnt[semkey] += inc
        tok = (semkey, self.cnt[semkey])
        self.ops[eng].append((fn, tuple(w for w in waits if w is not None), tok, inc))
        return tok

    def wait_only(self, eng, waits):
        self.ops[eng].append((None, tuple(w for w in waits if w is not None), None, 0))

    def replay(self):
        with self.nc.Block() as block:
            for eng in self.ENGS:
                ops = self.ops[eng]
                if not ops:
                    continue

                def body(e, ops=ops):
                    seen = {}
                    for fn, waits, tok, inc in ops:
                        need = {}
                        for (k, v) in waits:
                            if seen.get(k, 0) < v:
                                need[k] = max(need.get(k, 0), v)
                        for k, v in need.items():
                            e.wait_ge(self.sem[k], v)
                            seen[k] = v
                        if fn is not None:
                            inst = fn(e)
                            if tok is not None:
                                inst.then_inc(self.sem[tok[0]], inc)

                getattr(block, eng)(body)


def build(mode):
    nc = bass.Bass("TRN2", target_bir_lowering=False)

    def din(name, shape, dt=F32):
        return nc.dram_tensor(name, shape, dt, kind="ExternalInput").ap()

    def dout(name, shape, dt=F32):
        return nc.dram_tensor(name, shape, dt, kind="ExternalOutput").ap()

    do1 = mode in ("s1", "fused")
    do2 = mode in ("s2", "fused")
    ident_d = din("ident", [128, 128])
    gains_d = din("gains", [128, 98])
    if do1:
        x_d = din("x_ext", [NEXT, D])
        bias_d = din("bias0", [12, 128, 1664])
        a_in = din("a_w_in", [D, 5120])
    mem_d = din("mem_b", [256, D])
    b_in = din("b_w_in", [D, 3072])
    perm_d = din("perm", [128, 128])
    cos_d = din("cosT", [128, NTOK])
    sin_d = din("sinT", [128, NTOK])
    wkv = din("w_mem_kv", [2 * D, 1024])
    wo = din("w_o", [2 * D, D])
    wup = din("w_up", [2 * D, 8192])
    wdn = din("w_down", [2 * 8192, D])
    if mode == "s1":
        h1_o = dout("h1", [NTOK, D])
        kv_own = dout("kv_own", [128, 8192], BF16)
    if mode == "s2":
        h1_i = din("h1", [NTOK, D])
        kv_full = din("kv_full", [256, 8192], BF16)
    if mode == "fused":
        kv_own = nc.dram_tensor("kv_own", [128, 8192], BF16, kind="Internal").ap()
        kv_full = nc.dram_tensor("kv_full", [256, 8192], BF16, kind="Internal").ap()
    if do2:
        out_d = dout("out", [NTOK, D])

    st = contextlib.ExitStack()
    with st:
        P = Prog(nc, st)
        arena = st.enter_context(nc.sbuf_tensor("arena", [128, TOT // 4], F32))
        abf = arena.bitcast(BF16)
        psum = st.enter_context(nc.psum_tensor("ps", [128, 4096], F32))

        def shp(ap, shape):
            if len(shape) == 1:
                return ap
            if len(shape) == 2:
                return ap.rearrange("p (a b) -> p a b", b=shape[1])
            return ap.rearrange("p (a b c) -> p a b c", b=shape[1], c=shape[2])

        def vf(off, *shape):
            n = int(np.prod(shape))
            return shp(arena[:, off // 4: off // 4 + n], shape)

        def vb(off, *shape):
            n = int(np.prod(shape))
            return shp(abf[:, off // 2: off // 2 + n], shape)

        ident = vf(O_IDENT, 128)
        gains = vf(O_GAINS, 98)
        ones = vb(O_ONES, 128)
        perm = vb(O_PERM, 128)
        epsT = vf(O_EPS, 1)
        scr = vf(2048, 512)
        mT = vb(O_MT, 16, 256)
        kmT = vb(O_KMT, 4, 256)
        vm = vb(O_VM, 2, 512)
        ring = [vb(O_RING + i * 16384, 16, 512) for i in range(2)]
        QC = vb(O_QC, 16, 1024)
        B = O_BIG

        class PSA:
            open = [False] * 8
            rel = [[] for _ in range(8)]
            relseq = list(range(8))
            seq = 8

            @classmethod
            def alloc(c):
                free = [b for b in range(8) if not c.open[b]]
                if not free:
                    raise RuntimeError("psum full")
                b = min(free, key=lambda x: c.relseq[x])
                c.open[b] = True
                return b

            @classmethod
            def alloc2(c):
                free = [b for b in range(0, 8, 2) if not c.open[b] and not c.open[b + 1]]
                if not free:
                    raise RuntimeError("psum full2")
                b = min(free, key=lambda x: max(c.relseq[x], c.relseq[x + 1]))
                c.open[b] = c.open[b + 1] = True
                return b

            @classmethod
            def release(c, b, toks):
                c.open[b] = False
                c.rel[b] = [t for t in toks if t is not None]
                c.relseq[b] = c.seq
                c.seq += 1

        def bank(b, n=512):
            return psum[:, b * 512: b * 512 + n]

        PEQ = []
        peq_busy = [False]

        def defer(n, fn):
            PEQ.append([n, fn])

        def pe_tick():
            if peq_busy[0]:
                return
            peq_busy[0] = True
            for ent in PEQ:
                ent[0] -= 1
            while PEQ and PEQ[0][0] <= 0:
                PEQ.pop(0)[1]()
            peq_busy[0] = False

        def pe_flush():
            peq_busy[0] = True
            while PEQ:
                PEQ.pop(0)[1]()
            peq_busy[0] = False

        def mm(out, pairs, waits):
            n = len(pairs)
            tok = None
            for i, (l, r) in enumerate(pairs):
                tok = P.op("tensor",
                           lambda e, l=l, r=r, i=i, out=out: e.matmul(out, lhsT=l, rhs=r, start=(i == 0), stop=(i == n - 1)),
                           waits=waits if i == 0 else (), sig=(i == n - 1))
                if i < n - 1:
                    pe_tick()
            return tok

        def bar():
            return [P.last(e) for e in ("tensor", "scalar", "vector")]

        def wblock(W, r0, c0):
            return W[r0:r0 + 2048, c0:c0 + 512].rearrange("(k p) n -> p k n", p=128)

        plan = []
        if do1:
            plan += [("aq%d" % i, a_in, 0, i * 512) for i in range(3)] + [("aqm", a_in, 0, 4608)]
            plan += [("kv0k", wkv, 0, 0), ("kv0v", wkv, 0, 512)]
            for g in range(3):
                plan += [("ak%d" % g, a_in, 0, 1536 + g * 512), ("av%d" % g, a_in, 0, 3072 + g * 512)]
            plan += [("wo0_%d" % i, wo, 0, i * 512) for i in range(4)]
            if mode == "fused":
                plan += [("kv1k", wkv, D, 0), ("kv1v", wkv, D, 512)]
            for kg in range(4):
                plan += [("up0_%d" % (kg * 4 + j), wup, 0, (kg * 4 + j) * 512) for j in range(4)]
                plan += [("dn0_%d_%d" % (kg, cb), wdn, kg * 2048, cb * 512) for cb in range(4)]
            plan += [("bv", b_in, 0, 2048), ("bk", b_in, 0, 1536)]
        if do2:
            if mode != "fused":
                plan += [("kv1k", wkv, D, 0), ("kv1v", wkv, D, 512)]
            plan += [("bq%d" % i, b_in, 0, i * 512) for i in range(3)] + [("bqm", b_in, 0, 2560)]
            plan += [("wo1_%d" % i, wo, D, i * 512) for i in range(4)]
            for kg in range(4):
                plan += [("up1_%d" % (kg * 4 + j), wup, D, (kg * 4 + j) * 512) for j in range(4)]
                plan += [("dn1_%d_%d" % (kg, cb), wdn, 8192 + kg * 2048, cb * 512) for cb in range(4)]

        class WS:
            nxt = 0
            cur = 0
            rel = {}
            loaded = {}

            @classmethod
            def pop(c, name):
                i = c.cur
                assert plan[i][0] == name, (plan[i][0], name)
                while c.nxt < len(plan) and c.nxt <= i + 1:
                    j = c.nxt
                    w = c.rel.get(j - 2, [])
                    assert j < 2 or (j - 2) in c.rel
                    if j < 2:
                        w = list(state["xdma"][:4])
                    _, W, r0, c0 = plan[j]
                    c.loaded[j] = P.dma("gpsimd", ring[j % 2], wblock(W, r0, c0), "w%d" % (j % 2), waits=w)
                    c.nxt += 1
                c.cur += 1
                return ring[i % 2], c.loaded[i], i

            @classmethod
            def release(c, i, toks):
                c.rel[i] = [t for t in toks if t is not None]

        t_ident = P.dma("sync", ident, ident_d, "c_id")
        t_gains = P.dma("sync", gains, gains_d, "c_g")
        t_perm = P.dma("gpsimd", perm, perm_d, "cstp")
        t_ones = P.op("vector", lambda e: e.memset(ones, 1.0))
        t_eps = P.op("vector", lambda e: e.memset(epsT, EPS))
        cst = [t_ident, t_gains, t_perm, t_ones, t_eps]

        sq = vb(B + 98304, 16, 256)
        rstd = vf(B + 106496, 256)
        tmpn = vf(B + 107520, 256)
        state = {"sq": [], "rstd": [], "tmpn": [], "xin_i": 0, "ev_i": 0, "xdma": []}

        def load_T(src, dstT, T, xin, xin_war, waits):
            toks = []
            for tt in range(T // 128):
                s = state["xin_i"] % len(xin)
                state["xin_i"] += 1
                tX = P.dma("sync", xin[s], src[tt * 128:(tt + 1) * 128, :], "x%d" % s, waits=xin_war[s])
                state["xdma"].append(tX)
                lastk = None
                for kq in range(4):
                    b = PSA.alloc()
                    pb = bank(b)
                    for i in range(4):
                        kc = kq * 4 + i
                        tk = P.op("tensor",
                                  lambda e, pb=pb, i=i, s=s, kc=kc: e.transpose(out=pb[:, i * 128:(i + 1) * 128], in_=xin[s][:, kc * 128:(kc + 1) * 128], identity=ident),
                                  waits=([tX, t_ident] + PSA.rel[b]) if i == 0 else (), sig=(i == 3))
                    state["ev_i"] += 1
                    dst = dstT[:, kq * 4:(kq + 1) * 4, tt * 128:(tt + 1) * 128]
                    src_ps = pb.rearrange("p (a b) -> p a b", b=128)
                    if state["ev_i"] % 2:
                        ev = P.op("vector", lambda e, dst=dst, src_ps=src_ps: e.tensor_copy(out=dst, in_=src_ps), waits=[tk] + list(waits))
                    else:
                        ev = P.op("scalar", lambda e, dst=dst, src_ps=src_ps: e.copy(out=dst, in_=src_ps), waits=[tk] + list(waits))
                    PSA.release(b, [ev])
                    toks.append(ev)
                    lastk = tk
                xin_war[s] = [lastk]
            return toks

        def norm_T(srcT, T, gcol, dst, src_waits, dst_waits, dmodel=2048):
            sqv = sq[:, :, :T]
            tsq = P.op("scalar", lambda e: e.activation(out=sqv, in_=srcT, func=AF.Square), waits=list(src_waits) + state["sq"])
            b = PSA.alloc()
            ps = bank(b, T)
            tss = mm(ps, [(ones, sq[:, kc, :T]) for kc in range(16)], [tsq, t_ones] + PSA.rel[b])
            state["sq"] = [tss]
            t1 = P.op("scalar", lambda e: e.activation(out=tmpn[:, :T], in_=ps, func=AF.Ln, bias=epsT, scale=1.0 / dmodel), waits=[tss, t_eps] + state["tmpn"])
            PSA.release(b, [t1])
            t2 = P.op("scalar", lambda e: e.activation(out=rstd[:, :T], in_=tmpn[:, :T], func=AF.Exp, scale=-0.5), waits=[t1] + state["rstd"])
            state["tmpn"] = [t2]
            toks = []
            for kc in range(16):
                toks.append(P.op("vector",
                                 lambda e, kc=kc: e.scalar_tensor_tensor(out=dst[:, kc, :], in0=srcT[:, kc, :], scalar=gains[:, gcol + kc:gcol + kc + 1], in1=rstd[:, :T], op0=ALU.mult, op1=ALU.mult),
                                 waits=[t2, t_gains] + (list(dst_waits) if kc == 0 else [])))
            state["rstd"] = [toks[-1]]
            return toks

        att = {"pT_war": [[] for _ in range(8)], "pT_i": 0, "NP": 2, "rl_war": [[], []], "rl_i": 0, "st_war": [[], [], []], "st_i": 0,
               "na_war": [[], [], []], "na_i": 0}

        def attn_unit(QT, tiles, out, NQ, q_waits, kv_waits, pT, rl, stmp=None, bias=None, bias_waits=(), split=False):
            nt = len(tiles)
            base_w = list(q_waits) + list(kv_waits) + [t_ones]
            ctx = {"tokPV": None}

            def open_acc():
                ctx["bO"] = PSA.alloc2()
                ctx["bL"] = ctx["bO"] + 1
                ctx["Oa"] = bank(ctx["bO"], NQ)
                ctx["La"] = bank(ctx["bL"], NQ)

            def issuePV(j, rhs, tP, slots):
                first = (j == 0)
                last = (j == nt - 1)
                Vj = tiles[j][1]
                Oa, La, bO, bL = ctx["Oa"], ctx["La"], ctx["bO"], ctx["bL"]
                P.op("tensor", lambda e, Vj=Vj, rhs=rhs: e.matmul(Oa, lhsT=Vj, rhs=rhs, start=first, stop=last),
                     waits=[tP] + (PSA.rel[bO] + PSA.rel[bL] if first else []), sig=False)
                ctx["tokPV"] = P.op("tensor", lambda e, rhs=rhs: e.matmul(La, lhsT=ones, rhs=rhs, start=first, stop=last), sig=True)
                for s_ in slots:
                    att["pT_war"][s_] = [ctx["tokPV"]]

            def finish_act():
                ri = att["rl_i"] % 2
                att["rl_i"] += 1
                ctx["ri"] = ri
                rv = rl[ri][:, :NQ]
                ctx["rv"] = rv
                if nt >= 16:
                    tR0 = P.op("vector", lambda e: e.reciprocal(out=rv, in_=ctx["La"]), waits=[ctx["tokPV"]] + att["rl_war"][ri])
                    ctx["tR0"] = tR0
                    ctx["tR"] = tR0
                else:
                    tR0 = P.op("scalar", lambda e: e.activation(out=rv, in_=ctx["La"], func=AF.Ln), waits=[ctx["tokPV"]] + att["rl_war"][ri])
                    ctx["tR0"] = tR0
                    ctx["tR"] = P.op("scalar", lambda e: e.activation(out=rv, in_=rv, func=AF.Exp, scale=-1.0), waits=[tR0])

            def finish_mul():
                Oa, bO, bL, rv, ri = ctx["Oa"], ctx["bO"], ctx["bL"], ctx["rv"], ctx["ri"]
                tO = P.op("vector", lambda e: e.tensor_tensor(out=out, in0=Oa, in1=rv, op=ALU.mult), waits=[ctx["tR"]])
                PSA.release(bO, [tO])
                PSA.release(bL, [ctx["tR0"]])
                att["rl_war"][ri] = [tO]
                return tO

            def finish():
                finish_act()
                return finish_mul()

            if bias is None:
                if not split:
                    open_acc()
                assert NQ == 512 and nt % 2 == 0
                npairs = nt // 2
                q = []

                def issueS2(p):
                    b2 = PSA.alloc2()
                    tS = None
                    for k in range(2):
                        KTj = tiles[2 * p + k][0]
                        Sj = psum[:, (b2 + k) * 512:(b2 + k + 1) * 512]
                        tS = P.op("tensor", lambda e, KTj=KTj, Sj=Sj: e.matmul(Sj, lhsT=KTj, rhs=QT, start=True, stop=True),
                                  waits=(base_w + PSA.rel[b2] + PSA.rel[b2 + 1]) if k == 0 else (), sig=(k == 1))
                    half = att["pT_i"] % att["NP"]
                    att["pT_i"] += 1
                    src = psum[:, b2 * 512: b2 * 512 + 1024]
                    dst = pT_flat[:, half * 1024: half * 1024 + 1024]
                    tP = P.op("scalar", lambda e: e.activation(out=dst, in_=src, func=AF.Exp, scale=SCALE),
                              waits=[tS] + att["pT_war"][2 * half] + att["pT_war"][2 * half + 1])
                    PSA.release(b2, [tP])
                    PSA.release(b2 + 1, [tP])
                    q.append((tP, half))

                LOOKP = max(2, att["NP"] - 1)
                for p in range(min(LOOKP, npairs)):
                    issueS2(p)
                if split:
                    assert npairs == 1

                    def pv_only():
                        open_acc()
                        tP, half = q[0]
                        for k in range(2):
                            issuePV(k, pT_flat[:, half * 1024 + k * 512: half * 1024 + (k + 1) * 512], tP, [2 * half, 2 * half + 1])
                        return finish()

                    return pv_only
                for p in range(npairs):
                    tP, half = q[p]
                    for k in range(2):
                        issuePV(2 * p + k, pT_flat[:, half * 1024 + k * 512: half * 1024 + (k + 1) * 512], tP, [2 * half, 2 * half + 1])
                    if p + LOOKP < npairs:
                        issueS2(p + LOOKP)
                return finish()

            b2 = PSA.alloc2()
            S = psum[:, b2 * 512: b2 * 512 + nt * 128]
            tS = None
            for j in range(nt):
                KTj = tiles[j][0]
                Sj = psum[:, b2 * 512 + j * 128: b2 * 512 + (j + 1) * 128]
                tS = P.op("tensor", lambda e, KTj=KTj, Sj=Sj: e.matmul(Sj, lhsT=KTj, rhs=QT, start=True, stop=True),
                          waits=(base_w + PSA.rel[b2] + PSA.rel[b2 + 1]) if j == 0 else (), sig=(j == nt - 1))
            si = att["st_i"] % 3
            att["st_i"] += 1
            sv = stmp[si][:, :nt * 128]
            tB = P.op("vector", lambda e: e.scalar_tensor_tensor(out=sv, in0=S, scalar=SCALE, in1=bias, op0=ALU.mult, op1=ALU.add),
                      waits=[tS] + list(bias_waits) + att["st_war"][si])
            PSA.release(b2, [tB])
            PSA.release(b2 + 1, [tB])
            third = att["na_i"] % 3
            att["na_i"] += 1
            pfull = pT_na[:, third * 1024: third * 1024 + nt * 128]
            tP = P.op("scalar", lambda e: e.activation(out=pfull, in_=sv, func=AF.Exp),
                      waits=[tB] + att["na_war"][third])
            att["st_war"][si] = [tP]

            def pv_stage():
                open_acc()
                for j in range(nt):
                    issuePV(j, pT_na[:, third * 1024 + j * 128: third * 1024 + (j + 1) * 128], tP, [])
                att["na_war"][third] = [ctx["tokPV"]]
                finish_act()
                return finish_mul

            return pv_stage

        def evac_copy(eng, dst, ps, waits):
            if eng == "scalar":
                return P.op("scalar", lambda e: e.copy(out=dst, in_=ps), waits=waits)
            return P.op("vector", lambda e: e.tensor_copy(out=dst, in_=ps), waits=waits)

        def fpat(blk, btok, c, actT, t0, n, act_waits):
            b = PSA.alloc()
            ps = bank(b, n)
            tS = mm(ps, [(blk[:, kc, c * 128:(c + 1) * 128], actT[:, kc, t0:t0 + n]) for kc in range(16)],
                    [btok] + list(act_waits) + PSA.rel[b])
            return b, ps, tS

        def tpat(blk, btok, actT, t0, act_waits):
            b = PSA.alloc()
            ps = bank(b, 512)
            tS = mm(ps, [(actT[:, kc, t0:t0 + 128], blk[:, kc, :]) for kc in range(16)],
                    [btok] + list(act_waits) + PSA.rel[b])
            return b, ps, tS

        def mem_kv(layer, mT_waits, dst_waits):
            blk, btok, bi = WS.pop("kv%dk" % layer)
            toks = []
            last = None
            for c in range(4):
                b, ps, tS = fpat(blk, btok, c, mT, 0, 256, mT_waits)
                ev = evac_copy("scalar", kmT[:, c, :], ps, [tS] + list(dst_waits))
                PSA.release(b, [ev])
                toks.append(ev)
                last = tS
            WS.release(bi, [last])
            blk, btok, bi = WS.pop("kv%dv" % layer)
            for t in range(2):
                b, ps, tS = tpat(blk, btok, mT, t * 128, mT_waits)
                ev = evac_copy("vector", vm[:, t, :], ps, [tS] + list(dst_waits))
                PSA.release(b, [ev])
                toks.append(ev)
                last = tS
            WS.release(bi, [last])
            return toks

        def mem_attn(q_waits, kv_waits, pT, rl):
            toks = []
            pend = None
            for j in range(4):
                for tg in range(2):
                    QT = QC[:, 12 + j, tg * 512:(tg + 1) * 512]
                    tiles = [(kmT[:, j, t * 128:(t + 1) * 128], vm[:, t, j * 128:(j + 1) * 128]) for t in range(2)]
                    st = attn_unit(QT, tiles, QT, 512, q_waits, kv_waits, pT, rl, split=True)
                    if pend is not None:
                        toks.append(pend())
                    pend = st
            toks.append(pend())
            return toks

        def wo_phase(layer, hT, cat_waits, h_waits):
            toks = []
            for cb in range(4):
                blk, btok, bi = WS.pop("wo%d_%d" % (layer, cb))
                last = None
                for c in range(4):
                    for tg in range(2):
                        b, ps, tS = fpat(blk, btok, c, QC, tg * 512, 512, cat_waits)
                        hv = hT[:, cb * 4 + c, tg * 512:(tg + 1) * 512]
                        ev = P.op("vector", lambda e, hv=hv, ps=ps: e.tensor_tensor(out=hv, in0=ps, in1=hv, op=ALU.add), waits=[tS] + list(h_waits))
                        PSA.release(b, [ev])
                        toks.append(ev)
                        last = tS
                WS.release(bi, [last])
            return toks

        def mlp_phase(layer, hT, nT):
            aT = QC
            rt = [vf(B + 108544, 512), vf(B + 110592, 512)]
            rt_war = [[], []]
            ri = 0
            h_toks = []
            a_war = bar()
            for kg in range(4):
                a_toks = []
                for j in range(4):
                    blk, btok, bi = WS.pop("up%d_%d" % (layer, kg * 4 + j))
                    last = None
                    for tg in range(2):
                        for c in range(4):
                            b, ps, tS = fpat(blk, btok, c, nT, tg * 512, 512, nwaits(tg * 512, 512))
                            s = ri % 2
                            ri += 1
                            rv = rt[s]
                            t1 = P.op("scalar", lambda e, rv=rv, ps=ps: e.activation(out=rv, in_=ps, func=AF.Square), waits=[tS] + rt_war[s])
                            av = aT[:, j * 4 + c, tg * 512:(tg + 1) * 512]
                            t2 = P.op("vector", lambda e, rv=rv, ps=ps, av=av: e.scalar_tensor_tensor(out=av, in0=ps, scalar=0.0, in1=rv, op0=ALU.is_gt, op1=ALU.mult),
                                      waits=[t1] + a_war)
                            rt_war[s] = [t2]
                            PSA.release(b, [t2])
                            a_toks.append(t2)
                            last = tS
                    WS.release(bi, [last])
                lastdn = None
                for cb in range(4):
                    blk, btok, bi = WS.pop("dn%d_%d_%d" % (layer, kg, cb))
                    last = None
                    for c in range(4):
                        for tg in range(2):
                            b, ps, tS = fpat(blk, btok, c, aT, tg * 512, 512, a_toks)
                            hv = hT[:, cb * 4 + c, tg * 512:(tg + 1) * 512]
                            ev = P.op("vector", lambda e, hv=hv, ps=ps: e.tensor_tensor(out=hv, in0=ps, in1=hv, op=ALU.add), waits=[tS])
                            PSA.release(b, [ev])
                            h_toks.append(ev)
                            last = tS
                    WS.release(bi, [last])
                    lastdn = last
                a_war = [lastdn]
            return h_toks

        def emit_out(srcT_of, dst, waits_of, otile, ot_war):
            dts = []
            for tt in range(8):
                srcT = srcT_of(tt)
                s = tt % 2
                evs = []
                for kq in range(4):
                    b = PSA.alloc()
                    pb = bank(b)
                    tk = None
                    for i in range(4):
                        kc = kq * 4 + i
                        sv = srcT[:, kc, :]
                        tk = P.op("tensor", lambda e, pb=pb, i=i, sv=sv: e.transpose(out=pb[:, i * 128:(i + 1) * 128], in_=sv, identity=ident),
                                  waits=(list(waits_of(tt)) + [t_ident] + PSA.rel[b]) if i == 0 else (), sig=(i == 3))
                    dstv = otile[s][:, kq * 512:(kq + 1) * 512]
                    ev = evac_copy("scalar" if kq % 2 else "vector", dstv, pb, [tk] + ot_war[s])
                    PSA.release(b, [ev])
                    evs.append(ev)
                td = P.dma("sync", dst[tt * 128:(tt + 1) * 128, :], otile[s], "o%d" % s, waits=evs)
                ot_war[s] = [td]
                dts.append(td)
            return dts

        final_waits = []
        TN = {"g": []}

        def nwaits(t0, n):
            out = []
            for g in range(t0 // 256, (t0 + n - 1) // 256 + 1):
                out += TN["g"][g]
            return out

        def nall():
            out = []
            for g in TN["g"]:
                out += g
            return out

        xinA = [vf(B + 40960, 2048), vf(B + 49152, 2048)]
        if do1:
            xinA += [vf(O_QC + 16384, 2048), vf(O_QC + 24576, 2048)]
        xTa = vf(B + 57344, 16, 256)
        xw = [[] for _ in xinA]
        xTa_war = []

        def mem_norm():
            tl = load_T(mem_d, xTa, 256, xinA, xw, xTa_war)
            return norm_T(xTa, 256, 0, mT, tl, [])

        if not do1:
            t_mT = mem_norm()

        if do1:
            nT0 = vb(B, 16, NEXT)
            TN["g"] = []
            xTb = [xTa, vf(B + 73728, 16, 256)]
            xT_war = [[], []]
            srcs = [x_d[gi * 256:(gi + 1) * 256, :] for gi in range(5)] + [mem_d]
            loads = {}

            def do_load(i):
                loads[i] = load_T(srcs[i], xTb[i % 2], 256, xinA, xw, xT_war[i % 2])

            do_load(0)
            for i in range(6):
                if i + 1 < 6:
                    do_load(i + 1)
                if i < 5:
                    tn = norm_T(xTb[i % 2], 256, 16, nT0[:, :, i * 256:(i + 1) * 256], loads[i], [])
                    TN["g"].append(tn)
                else:
                    tn = norm_T(xTb[i % 2], 256, 0, mT, loads[i], [])
                    t_mT = tn
                xT_war[i % 2] = tn
            t_q = []
            x_all = []
            for i in range(6):
                x_all += loads[i]
            for qi in range(4):
                blk, btok, bi = WS.pop("aq%d" % qi if qi < 3 else "aqm")
                last = None
                for tg in range(2):
                    for c in range(4):
                        b, ps, tS = fpat(blk, btok, c, nT0, tg * 512, 512, nwaits(tg * 512, 512))
                        ev = evac_copy("scalar", QC[:, qi * 4 + c, tg * 512:(tg + 1) * 512], ps, [tS] + (x_all if qi >= 2 else []))
                        PSA.release(b, [ev])
                        t_q.append(ev)
                        last = tS
                WS.release(bi, [last])
            t_kvm = mem_kv(0, t_mT, [])
            KTg = vb(B + 40960, 4, NEXT)
            Vg = vb(B + 51200, 10, 512)
            biasb = [vf(B + 61440, 1664), vf(B + 68096, 1664)]
            pT_flat = vb(B + 74752, 2048)
            pT = [pT_flat[:, i * 512:(i + 1) * 512] for i in range(4)]
            stmp = [vf(B + 78848, 640), vf(B + 81408, 640), vf(B + 94208, 640)]
            pT_na = vb(B + 88064, 3072)
            rl = [vf(B + 83968, 512), vf(B + 86016, 512)]
            kv_war = bar()
            bias_war = [list(kv_war), list(kv_war)]
            t_cat = mem_attn(t_q, t_kvm, pT, rl)
            for g in range(3):
                blk, btok, bi = WS.pop("ak%d" % g)
                t_k = []
                last = None
                for c in range(4):
                    for (t0, n) in ((0, 512), (512, 512), (1024, 256)):
                        b, ps, tS = fpat(blk, btok, c, nT0, t0, n, nwaits(t0, n))
                        ev = evac_copy("scalar", KTg[:, c, t0:t0 + n], ps, [tS] + kv_war)
                        PSA.release(b, [ev])
                        t_k.append(ev)
                        last = tS
                WS.release(bi, [last])
                blk, btok, bi = WS.pop("av%d" % g)
                for t in range(10):
                    b, ps, tS = tpat(blk, btok, nT0, t * 128, nwaits(t * 128, 128))
                    ev = evac_copy("vector" if t % 2 else "scalar", Vg[:, t, :], ps, [tS] + kv_war)
                    PSA.release(b, [ev])
                    t_k.append(ev)
                    last = tS
                WS.release(bi, [last])
                lastatt = None
                pend = []
                mul_q = []
                for hh in range(4):
                    h = g * 4 + hh
                    s = h % 2
                    t_b = P.dma("sync", biasb[s], bias_d[h], "bias%d" % s, waits=bias_war[s])
                    for m in range(8):
                        if m < 2:
                            tl_, boff = [0, 1, 2, 3], m * 512
                        else:
                            tl_, boff = list(range(m - 2, m + 3)), 1024
                        nt = len(tl_)
                        tiles = [(KTg[:, hh, t * 128:(t + 1) * 128], Vg[:, t, hh * 128:(hh + 1) * 128]) for t in tl_]
                        QT = QC[:, h, m * 128:(m + 1) * 128]
                        if len(pend) >= 3:
                            mul_q.append(pend.pop(0)())
                        if len(mul_q) >= 2:
                            lastatt = mul_q.pop(0)()
                            t_cat.append(lastatt)
                        pvs = attn_unit(QT, tiles, QT, 128, t_q, t_k, pT, rl, stmp=stmp,
                                        bias=biasb[s][:, boff:boff + nt * 128], bias_waits=[t_b])
                        pend.append(pvs)
                    bias_war[s] = [P.last("vector")]
                while pend:
                    mul_q.append(pend.pop(0)())
                while mul_q:
                    lastatt = mul_q.pop(0)()
                    t_cat.append(lastatt)
                kv_war = [P.last("tensor"), lastatt]
            hT = vf(B, 16, NTOK)
            xinC = [vf(B + 65536, 2048), vf(B + 73728, 2048)]
            bw = bar()
            xw = [list(bw), list(bw)]
            t_h = []
            for gi in range(4):
                t_h += load_T(x_d[gi * 256:(gi + 1) * 256, :], hT[:, :, gi * 256:(gi + 1) * 256], 256, xinC, xw, bw)
            t_h = wo_phase(0, hT, t_cat, t_h)
            nT = vb(B + 65536, 16, NTOK)
            nw = bar()
            TN["g"] = []
            for gi in range(4):
                TN["g"].append(norm_T(hT[:, :, gi * 256:(gi + 1) * 256], 256, 32, nT[:, :, gi * 256:(gi + 1) * 256], t_h, nw))
            if mode == "fused":
                t_kvm1 = mem_kv(1, t_mT, nw)
            t_h = mlp_phase(0, hT, nT)
            nw = bar()
            TN["g"] = []
            for gi in range(4):
                TN["g"].append(norm_T(hT[:, :, gi * 256:(gi + 1) * 256], 256, 48, nT[:, :, gi * 256:(gi + 1) * 256], t_h, nw))

        if mode == "s2":
            hT = vf(B, 16, NTOK)
            nT = vb(B + 65536, 16, NTOK)
            xinC = [vf(B + 65536, 2048), vf(B + 73728, 2048)]
            bw = bar()
            xw = [list(bw), list(bw)]
            t_h = []
            for gi in range(4):
                t_h += load_T(h1_i[gi * 256:(gi + 1) * 256, :], hT[:, :, gi * 256:(gi + 1) * 256], 256, xinC, xw, bw)
            nw = bar()
            TN["g"] = []
            for gi in range(4):
                TN["g"].append(norm_T(hT[:, :, gi * 256:(gi + 1) * 256], 256, 48, nT[:, :, gi * 256:(gi + 1) * 256], t_h, nw))

        cosT = vf(B + 108544, NTOK)
        sinT = vf(B + 112640, NTOK)
        sqq = vb(B + 116736, 512)
        rstq = vf(B + 117760, 512)
        tq = vf(B + 119808, 512)
        qh = vb(B + 121856, 512)
        t1b = vf(B + 122880, 512)
        t2b = vf(B + 124928, 512)
        qk = {"sqq": [], "rstq": [], "tq": [], "qh": [], "t1": [], "t2": []}
        bw = bar()
        t_cos = P.dma("sync", cosT, cos_d, "c_cos", waits=bw)
        t_sin = P.dma("sync", sinT, sin_d, "c_sin", waits=bw)

        def qk_post(b, ps, tS, gcol, t0, dst, dst_waits, done):
            t1 = P.op("scalar", lambda e: e.activation(out=sqq, in_=ps, func=AF.Square), waits=[tS] + qk["sqq"])
            res = {}

            def stage2():
                b2 = PSA.alloc()
                ps2 = bank(b2)
                t2 = mm(ps2, [(ones, sqq)], [t1, t_ones] + PSA.rel[b2])
                qk["sqq"] = [t2]
                t3 = P.op("scalar", lambda e: e.activation(out=tq, in_=ps2, func=AF.Ln, bias=epsT, scale=1.0 / 128), waits=[t2, t_eps] + qk["tq"])
                PSA.release(b2, [t3])
                t4 = P.op("scalar", lambda e: e.activation(out=rstq, in_=tq, func=AF.Exp, scale=-0.5), waits=[t3] + qk["rstq"])
                qk["tq"] = [t4]
                t5 = P.op("vector", lambda e: e.scalar_tensor_tensor(out=qh, in0=ps, scalar=gains[:, gcol:gcol + 1], in1=rstq, op0=ALU.mult, op1=ALU.mult),
                          waits=[t4, t_gains] + qk["qh"])
                PSA.release(b, [t5])
                qk["rstq"] = [t5]
                res["t5"] = t5

            def stage3():
                t5 = res["t5"]
                b3 = PSA.alloc()
                ps3 = bank(b3)
                t6 = mm(ps3, [(perm, qh)], [t5, t_perm] + PSA.rel[b3])
                cv = cosT[:, t0:t0 + 512]
                sv = sinT[:, t0:t0 + 512]
                t7 = P.op("vector", lambda e: e.tensor_tensor(out=t1b, in0=qh, in1=cv, op=ALU.mult), waits=[t5, t_cos] + qk["t1"])
                t8 = P.op("vector", lambda e: e.tensor_tensor(out=t2b, in0=ps3, in1=sv, op=ALU.mult), waits=[t6, t_sin] + qk["t2"])
                PSA.release(b3, [t8])
                t9 = P.op("vector", lambda e: e.tensor_tensor(out=dst, in0=t1b, in1=t2b, op=ALU.add), waits=[t7, t8] + list(dst_waits))
                qk["qh"] = [t6, t7]
                qk["t1"] = [t9]
                qk["t2"] = [t9]
                done(t9)

            defer(4, stage2)
            defer(20, stage3)

        if do1:
            kst = [vb(B + 98304, 1024), vb(B + 100352, 1024)]
            vst = [vb(B + 102400, 512), vb(B + 103424, 512)]
            kst_war = [nall(), nall()]
            vst_war = [nall(), nall()]
            kv_dmas = []
            blk, btok, bi = WS.pop("bv")
            last = None
            for t in range(8):
                s = t % 2
                b, ps, tS = tpat(blk, btok, nT, t * 128, nwaits(t * 128, 128))
                ev = evac_copy("scalar", vst[s], ps, [tS] + vst_war[s])
                PSA.release(b, [ev])
                td = P.dma("sync", kv_own[:, 4096 + t * 512: 4096 + (t + 1) * 512], vst[s], "kvo_v%d" % s, waits=[ev])
                vst_war[s] = [td]
                kv_dmas.append(td)
                last = tS
            WS.release(bi, [last])
            blk, btok, bi = WS.pop("bk")
            last = None
            for c in range(4):
                s = c % 2
                t9s = []

                def kdone(t9, c=c, s=s, t9s=t9s):
                    t9s.append(t9)
                    if len(t9s) == 2:
                        td = P.dma("sync", kv_own[:, c * 1024:(c + 1) * 1024], kst[s], "kvo_k%d" % s, waits=t9s)
                        kst_war[s] = [td]
                        kv_dmas.append(td)

                if c >= 2:
                    pe_flush()
                for tg in range(2):
                    b, ps, tS = fpat(blk, btok, c, nT, tg * 512, 512, nall())
                    qk_post(b, ps, tS, 97, tg * 512, kst[s][:, tg * 512:(tg + 1) * 512], list(kst_war[s]), kdone)
                    last = tS
            WS.release(bi, [last])
            pe_flush()
            final_waits += kv_dmas

        if mode == "s1":
            otile = [vf(B + 81920, 2048), vf(B + 90112, 2048)]
            bw = bar()
            ow = [list(bw), list(bw)]
            final_waits += emit_out(lambda tt: hT[:, :, tt * 128:(tt + 1) * 128], h1_o, lambda tt: t_h, otile, ow)

        kvfull_waits = []

        if do2:
            t_kvm = t_kvm1 if mode == "fused" else mem_kv(1, t_mT, bar())
            t_q = []
            q_war = bar()
            for qi in range(4):
                blk, btok, bi = WS.pop("bq%d" % qi if qi < 3 else "bqm")
                if qi == 0 and mode == "fused":
                    t_cc = P.custom("gpsimd",
                                    lambda e: e.collective_compute("AllGather", ALU.bypass, replica_groups=[[0, 1], [2, 3], [4, 5], [6, 7]],
                                                                   ins=[kv_own.opt()], outs=[kv_full.opt()]),
                                    "cc", waits=kv_dmas, inc=1)
                    kvfull_waits.append(t_cc)
                last = None
                for c in range(4):
                    for tg in range(2):
                        b, ps, tS = fpat(blk, btok, c, nT, tg * 512, 512, nall())
                        dst = QC[:, qi * 4 + c, tg * 512:(tg + 1) * 512]
                        if qi < 3:
                            qk_post(b, ps, tS, 96, tg * 512, dst, q_war, t_q.append)
                        else:
                            ev = evac_copy("scalar", dst, ps, [tS] + q_war)
                            PSA.release(b, [ev])
                            t_q.append(ev)
                        last = tS
                WS.release(bi, [last])
            pe_flush()
            KTf = vb(B + 65536, 4, 2048)
            Vf = vb(B + 81920, 16, 512)
            pT_flat = vb(B + 98304, 4096)
            pT = [pT_flat[:, i * 512:(i + 1) * 512] for i in range(8)]
            rl = [vf(B + 106496, 512), vf(B + 108544, 512)]
            bw = bar()
            if mode == "fused":
                bw = bw + kv_dmas
            att["NP"] = 4
            att["pT_war"] = [list(bw) for _ in range(8)]
            att["rl_war"] = [list(bw), list(bw)]
            t_kv = []
            for r in range(2):
                t_kv.append(P.dma("sync", KTf[:, :, r * 1024:(r + 1) * 1024],
                                  kv_full[r * 128:(r + 1) * 128, 0:4096].rearrange("p (h t) -> p h t", t=1024), "kvl", waits=bw + kvfull_waits))
                t_kv.append(P.dma("sync", Vf[:, r * 8:(r + 1) * 8, :],
                                  kv_full[r * 128:(r + 1) * 128, 4096:8192].rearrange("p (t n) -> p t n", n=512), "kvl", waits=bw + kvfull_waits))
            t_cat = mem_attn(t_q, t_kvm, pT, rl)
            for h in range(12):
                kvh = h // 3
                for tg in range(2):
                    QT = QC[:, h, tg * 512:(tg + 1) * 512]
                    tiles = [(KTf[:, kvh, kt * 128:(kt + 1) * 128], Vf[:, kt, kvh * 128:(kvh + 1) * 128]) for kt in range(16)]
                    t_cat.append(attn_unit(QT, tiles, QT, 512, t_q, t_kv, pT, rl))
            t_h = wo_phase(1, hT, t_cat, t_h)
            nw = bar()
            TN["g"] = []
            for gi in range(4):
                TN["g"].append(norm_T(hT[:, :, gi * 256:(gi + 1) * 256], 256, 64, nT[:, :, gi * 256:(gi + 1) * 256], t_h, nw))
            t_h = mlp_phase(1, hT, nT)
            yTs = [vf(B + 65536, 16, 256), vf(B + 81920, 16, 256)]
            otile = [vf(O_QC, 2048), vf(O_QC + 8192, 2048), vf(O_QC + 16384, 2048), vf(O_QC + 24576, 2048)]
            bw = bar()
            ow = [list(bw) for _ in range(4)]
            y_wars = [list(bw), list(bw)]
            tys = {}

            def fin_norm(gi):
                tys[gi] = norm_T(hT[:, :, gi * 256:(gi + 1) * 256], 256, 80, yTs[gi % 2], t_h, y_wars[gi % 2])

            fin_norm(0)
            for gi in range(4):
                yT = yTs[gi % 2]
                if gi + 1 < 4 and gi >= 1:
                    pass
                if gi + 1 < 4 and gi == 0:
                    fin_norm(1)
                ty = tys[gi]
                dts = []
                for tt in range(2):
                    srcT = yT[:, :, tt * 128:(tt + 1) * 128]
                    s = (gi * 2 + tt) % 4
                    evs = []
                    lastk = None
                    for kq in range(4):
                        b = PSA.alloc()
                        pb = bank(b)
                        tk = None
                        for i in range(4):
                            kc = kq * 4 + i
                            sv = srcT[:, kc, :]
                            tk = P.op("tensor", lambda e, pb=pb, i=i, sv=sv: e.transpose(out=pb[:, i * 128:(i + 1) * 128], in_=sv, identity=ident),
                                      waits=(list(ty) + [t_ident] + PSA.rel[b]) if i == 0 else (), sig=(i == 3))
                        dstv = otile[s][:, kq * 512:(kq + 1) * 512]
                        ev = evac_copy("scalar" if kq % 2 else "vector", dstv, pb, [tk] + ow[s])
                        PSA.release(b, [ev])
                        evs.append(ev)
                        lastk = tk
                    row = gi * 256 + tt * 128
                    td = P.dma("sync", out_d[row:row + 128, :], otile[s], "o%d" % s, waits=evs)
                    ow[s] = [td]
                    final_waits.append(td)
                y_wars[gi % 2] = [lastk]
                if gi + 2 < 4:
                    fin_norm(gi + 2)

        P.wait_only("sync", final_waits)
        P.replay()
    return nc


def _true_row(l, hf):
    return l if hf == 0 else 31 - l


def _bias_tables(rpb, hf):
    units = [(0, [0, 1, 2, 3]), (1, [0, 1, 2, 3]), (2, [0, 1, 2, 3, 4])]
    p = np.arange(128)
    ki, kc = p // 64, p % 64
    qi, qc = p // 64, p % 64
    cols = []
    for m, tl in units:
        for t in tl:
            kr = np.array([_true_row(2 * t + a, hf) for a in ki])[:, None]
            qr = np.array([_true_row(2 * m + a, hf) for a in qi])[None, :]
            r0 = np.clip(qr - 4, 0, 24)
            vr = (kr >= r0) & (kr < r0 + 8)
            c0 = np.clip(qc - 8, 0, 48)[None, :]
            vc = (kc[:, None] >= c0) & (kc[:, None] < c0 + 16)
            dr = np.clip(kr - qr + 7, 0, 14)
            dc = np.clip(kc[:, None] - qc[None, :] + 15, 0, 30)
            valid = vr & vc
            g = rpb[:, dr, dc]
            cols.append(np.where(valid[None], g, np.float32(MASKV)).astype(np.float32))
    return np.ascontiguousarray(np.concatenate(cols, axis=2))


def _rope_tables(hf):
    t = np.arange(NTOK)
    row = np.array([_true_row(l, hf) for l in (t // 64)], dtype=np.float32)
    col = (t % 64).astype(np.float32)
    inv = np.power(np.float32(10000.0), -np.arange(0, 64, 2, dtype=np.float32) / np.float32(64)).astype(np.float32)
    d = np.arange(128)
    f = d % 32
    pos = np.where((d < 64)[:, None], row[None, :], col[None, :]).astype(np.float32)
    ang = (pos * inv[f][:, None]).astype(np.float32)
    cosT = np.cos(ang).astype(np.float32)
    sgn = np.where((d % 64) < 32, -1.0, 1.0).astype(np.float32)[:, None]
    sinT = (np.sin(ang).astype(np.float32) * sgn).astype(np.float32)
    return np.ascontiguousarray(cosT), np.ascontiguousarray(sinT)


def _fm(vec):
    return np.asarray(vec, dtype=np.float32).reshape(-1, 128).T


_CACHE = {}


def _get_nc(mode):
    if mode not in _CACHE:
        _CACHE[mode] = build(mode)
    return _CACHE[mode]


def kernel(x, mem, mem_norm, attn_norm, mlp_norm, a_w_in, a_rpb, b_w_in, b_q_norm, b_k_norm,
           w_mem_kv, w_o, w_up, w_down, final_norm, _mode="fused"):
    x = np.asarray(x, dtype=np.float32)
    mem = np.asarray(mem, dtype=np.float32)
    gains = np.concatenate([_fm(mem_norm), _fm(attn_norm[0]), _fm(mlp_norm[0]), _fm(attn_norm[1]), _fm(mlp_norm[1]),
                            _fm(final_norm), np.asarray(b_q_norm[0], np.float32)[:, None], np.asarray(b_k_norm[0], np.float32)[:, None]], axis=1)
    gains = np.ascontiguousarray(gains.astype(np.float32))
    d = np.arange(128)
    partner = np.where((d % 64) < 32, d + 32, d - 32)
    perm = np.zeros((128, 128), np.float32)
    perm[partner, d] = 1.0
    ident = np.eye(128, dtype=np.float32)
    rpb = np.asarray(a_rpb[0], np.float32)
    bias = [_bias_tables(rpb, 0), _bias_tables(rpb, 1)]
    rope = [_rope_tables(0), _rope_tables(1)]
    common = {
        "ident": ident, "gains": gains, "perm": perm,
        "b_w_in": np.ascontiguousarray(np.asarray(b_w_in[0], np.float32)),
        "w_mem_kv": np.asarray(w_mem_kv, np.float32).reshape(2 * D, 1024),
        "w_o": np.asarray(w_o, np.float32).reshape(2 * D, D),
        "w_up": np.asarray(w_up, np.float32).reshape(2 * D, 8192),
        "w_down": np.asarray(w_down, np.float32).reshape(2 * 8192, D),
    }
    a_in = np.ascontiguousarray(np.asarray(a_w_in[0], np.float32))
    maps1 = []
    for c in range(8):
        b, hf = c // 2, c % 2
        xb = x[b].reshape(32, 64, D)
        if hf:
            xb = xb[::-1]
        m = dict(common)
        m["x_ext"] = np.ascontiguousarray(xb[:20].reshape(NEXT, D))
        m["mem_b"] = np.ascontiguousarray(mem[b])
        m["bias0"] = bias[hf]
        m["a_w_in"] = a_in
        m["cosT"], m["sinT"] = rope[hf]
        maps1.append(m)
    if _mode == "fused":
        res = run_bass_kernel_spmd(_get_nc("fused"), maps1, core_ids=list(range(8)))
        outs = [r["out"] for r in res.results]
    else:
        res1 = run_bass_kernel_spmd(_get_nc("s1"), maps1, core_ids=list(range(8)))
        maps2 = []
        for c in range(8):
            b, hf = c // 2, c % 2
            m = dict(common)
            m["mem_b"] = maps1[c]["mem_b"]
            m["cosT"], m["sinT"] = rope[hf]
            m["h1"] = np.asarray(res1.results[c]["h1"])
            own = np.asarray(res1.results[c]["kv_own"])
            oth = np.asarray(res1.results[c ^ 1]["kv_own"])
            pair = [own, oth] if hf == 0 else [oth, own]
            m["kv_full"] = np.ascontiguousarray(np.concatenate(pair, axis=0))
            maps2.append(m)
        res2 = run_bass_kernel_spmd(_get_nc("s2"), maps2, core_ids=list(range(8)))
        outs = [r["out"] for r in res2.results]
    out = np.empty((4, 2048, D), np.float32)
    for c in range(8):
        b, hf = c // 2, c % 2
        ob = np.asarray(outs[c], np.float32).reshape(16, 64, D)
        if hf:
            ob = ob[::-1]
        out[b, hf * 1024:(hf + 1) * 1024] = ob.reshape(NTOK, D)
    return out
```

```python
import contextlib
import numpy as np
import ml_dtypes
import concourse.bass as bass
import concourse.mybir as mybir
from concourse.bass_utils import run_bass_kernel_spmd

F32 = mybir.dt.float32
BF16 = mybir.dt.bfloat16
ALU = mybir.AluOpType
AF = mybir.ActivationFunctionType

D = 2048
NTOK = 1024
NEXT = 1280
EPS = 1e-6
SCALE = 128 ** -0.5
MASKV = -30000.0

O_IDENT, O_GAINS, O_ONES, O_PERM, O_EPS = 0, 512, 1024, 1280, 1536
O_MT = 4096
O_KMT, O_VM = 12288, 14336
O_RING = 16384
O_QC = 49152
O_BIG = 81920
TOT = 208896


class Prog:
    ENGS = ("sync", "scalar", "vector", "gpsimd", "tensor")

    def __init__(self, nc, stack):
        self.nc = nc
        self.stack = stack
        self.ops = {e: [] for e in self.ENGS}
        self.sem = {}
        self.cnt = {}

    def _sem(self, key):
        if key not in self.sem:
            self.sem[key] = self.stack.enter_context(self.nc.semaphore(key))
            self.cnt[key] = 0
        return self.sem[key]

    def op(self, eng, fn, waits=(), sig=True):
        tok = None
        if sig:
            key = "e_" + eng
            self._sem(key)
            self.cnt[key] += 1
            tok = (key, self.cnt[key])
        self.ops[eng].append((fn, tuple(w for w in waits if w is not None), tok, 1))
        return tok

    def last(self, eng):
        key = "e_" + eng
        if key in self.cnt and self.cnt[key] > 0:
            return (key, self.cnt[key])
        return None

    def dma(self, eng, out, in_, semkey, waits=()):
        return self.custom(eng, lambda e, out=out, in_=in_: e.dma_start(out=out, in_=in_), semkey, waits)

    def custom(self, eng, fn, semkey, waits=(), inc=16):
        self._sem(semkey)
        self.cnt[semkey] += inc
        tok = (semkey, self.cnt[semkey])
        self.ops[eng].append((fn, tuple(w for w in waits if w is not None), tok, inc))
        return tok

    def wait_only(self, eng, waits):
        self.ops[eng].append((None, tuple(w for w in waits if w is not None), None, 0))

    def replay(self):
        with self.nc.Block() as block:
            for eng in self.ENGS:
                ops = self.ops[eng]
                if not ops:
                    continue

                def body(e, ops=ops):
                    seen = {}
                    for fn, waits, tok, inc in ops:
                        need = {}
                        for (k, v) in waits:
                            if seen.get(k, 0) < v:
                                need[k] = max(need.get(k, 0), v)
                        for k, v in need.items():
                            e.wait_ge(self.sem[k], v)
                            seen[k] = v
                        if fn is not None:
                            inst = fn(e)
                            if tok is not None:
                                inst.then_inc(self.sem[tok[0]], inc)

                getattr(block, eng)(body)


def build(mode):
    nc = bass.Bass("TRN2", target_bir_lowering=False)

    def din(name, shape, dt=F32):
        return nc.dram_tensor(name, shape, dt, kind="ExternalInput").ap()

    def dout(name, shape, dt=F32):
        return nc.dram_tensor(name, shape, dt, kind="ExternalOutput").ap()

    do1 = mode in ("s1", "fused")
    do2 = mode in ("s2", "fused")
    ident_d = din("ident", [128, 128])
    gains_d = din("gains", [128, 98])
    if do1:
        x_d = din("x_ext", [NEXT, D])
        bias_d = din("bias0", [12, 128, 1664])
        a_in = din("a_w_in", [D, 5120])
    mem_d = din("mem_b", [256, D])
    b_in = din("b_w_in", [D, 3072])
    perm_d = din("perm", [128, 128])
    cos_d = din("cosT", [128, NTOK])
    sin_d = din("sinT", [128, NTOK])
    wkv = din("w_mem_kv", [2 * D, 1024])
    wo = din("w_o", [2 * D, D])
    wup = din("w_up", [2 * D, 8192])
    wdn = din("w_down", [2 * 8192, D])
    if mode == "s1":
        h1_o = dout("h1", [NTOK, D])
        kv_own = dout("kv_own", [128, 8192], BF16)
    if mode == "s2":
        h1_i = din("h1", [NTOK, D])
        kv_full = din("kv_full", [256, 8192], BF16)
    if mode == "fused":
        kv_own = nc.dram_tensor("kv_own", [128, 8192], BF16, kind="Internal").ap()
        kv_full = nc.dram_tensor("kv_full", [256, 8192], BF16, kind="Internal").ap()
    if do2:
        out_d = dout("out", [NTOK, D])

    st = contextlib.ExitStack()
    with st:
        P = Prog(nc, st)
        arena = st.enter_context(nc.sbuf_tensor("arena", [128, TOT // 4], F32))
        abf = arena.bitcast(BF16)
        psum = st.enter_context(nc.psum_tensor("ps", [128, 4096], F32))

        def shp(ap, shape):
            if len(shape) == 1:
                return ap
            if len(shape) == 2:
                return ap.rearrange("p (a b) -> p a b", b=shape[1])
            return ap.rearrange("p (a b c) -> p a b c", b=shape[1], c=shape[2])

        def vf(off, *shape):
            n = int(np.prod(shape))
            return shp(arena[:, off // 4: off // 4 + n], shape)

        def vb(off, *shape):
            n = int(np.prod(shape))
            return shp(abf[:, off // 2: off // 2 + n], shape)

        ident = vf(O_IDENT, 128)
        gains = vf(O_GAINS, 98)
        ones = vb(O_ONES, 128)
        perm = vb(O_PERM, 128)
        epsT = vf(O_EPS, 1)
        scr = vf(2048, 512)
        mT = vb(O_MT, 16, 256)
        kmT = vb(O_KMT, 4, 256)
        vm = vb(O_VM, 2, 512)
        ring = [vb(O_RING + i * 16384, 16, 512) for i in range(2)]
        QC = vb(O_QC, 16, 1024)
        B = O_BIG

        class PSA:
            open = [False] * 8
            rel = [[] for _ in range(8)]
            relseq = list(range(8))
            seq = 8

            @classmethod
            def alloc(c):
                free = [b for b in range(8) if not c.open[b]]
                if not free:
                    raise RuntimeError("psum full")
                b = min(free, key=lambda x: c.relseq[x])
                c.open[b] = True
                return b

            @classmethod
            def alloc2(c):
                free = [b for b in range(0, 8, 2) if not c.open[b] and not c.open[b + 1]]
                if not free:
                    raise RuntimeError("psum full2")
                b = min(free, key=lambda x: max(c.relseq[x], c.relseq[x + 1]))
                c.open[b] = c.open[b + 1] = True
                return b

            @classmethod
            def release(c, b, toks):
                c.open[b] = False
                c.rel[b] = [t for t in toks if t is not None]
                c.relseq[b] = c.seq
                c.seq += 1

        def bank(b, n=512):
            return psum[:, b * 512: b * 512 + n]

        PEQ = []
        peq_busy = [False]

        def defer(n, fn):
            PEQ.append([n, fn])

        def pe_tick():
            if peq_busy[0]:
                return
            peq_busy[0] = True
            for ent in PEQ:
                ent[0] -= 1
            while PEQ and PEQ[0][0] <= 0:
                PEQ.pop(0)[1]()
            peq_busy[0] = False

        def pe_flush():
            peq_busy[0] = True
            while PEQ:
                PEQ.pop(0)[1]()
            peq_busy[0] = False

        def mm(out, pairs, waits):
            n = len(pairs)
            tok = None
            for i, (l, r) in enumerate(pairs):
                tok = P.op("tensor",
                           lambda e, l=l, r=r, i=i, out=out: e.matmul(out, lhsT=l, rhs=r, start=(i == 0), stop=(i == n - 1)),
                           waits=waits if i == 0 else (), sig=(i == n - 1))
                if i < n - 1:
                    pe_tick()
            return tok

        def bar():
            return [P.last(e) for e in ("tensor", "scalar", "vector")]

        def wblock(W, r0, c0):
            return W[r0:r0 + 2048, c0:c0 + 512].rearrange("(k p) n -> p k n", p=128)

        plan = []
        if do1:
            plan += [("aq%d" % i, a_in, 0, i * 512) for i in range(3)] + [("aqm", a_in, 0, 4608)]
            plan += [("kv0k", wkv, 0, 0), ("kv0v", wkv, 0, 512)]
            for g in range(3):
                plan += [("ak%d" % g, a_in, 0, 1536 + g * 512), ("av%d" % g, a_in, 0, 3072 + g * 512)]
            plan += [("wo0_%d" % i, wo, 0, i * 512) for i in range(4)]
            if mode == "fused":
                plan += [("kv1k", wkv, D, 0), ("kv1v", wkv, D, 512)]
            for kg in range(4):
                plan += [("up0_%d" % (kg * 4 + j), wup, 0, (kg * 4 + j) * 512) for j in range(4)]
                plan += [("dn0_%d_%d" % (kg, cb), wdn, kg * 2048, cb * 512) for cb in range(4)]
            plan += [("bv", b_in, 0, 2048), ("bk", b_in, 0, 1536)]
        if do2:
            if mode != "fused":
                plan += [("kv1k", wkv, D, 0), ("kv1v", wkv, D, 512)]
            plan += [("bq%d" % i, b_in, 0, i * 512) for i in range(3)] + [("bqm", b_in, 0, 2560)]
            plan += [("wo1_%d" % i, wo, D, i * 512) for i in range(4)]
            for kg in range(4):
                plan += [("up1_%d" % (kg * 4 + j), wup, D, (kg * 4 + j) * 512) for j in range(4)]
                plan += [("dn1_%d_%d" % (kg, cb), wdn, 8192 + kg * 2048, cb * 512) for cb in range(4)]

        class WS:
            nxt = 0
            cur = 0
            rel = {}
            loaded = {}

            @classmethod
            def pop(c, name):
                i = c.cur
                assert plan[i][0] == name, (plan[i][0], name)
                while c.nxt < len(plan) and c.nxt <= i + 1:
                    j = c.nxt
                    w = c.rel.get(j - 2, [])
                    assert j < 2 or (j - 2) in c.rel
                    if j < 2:
                        w = list(state["xdma"][:4])
                    _, W, r0, c0 = plan[j]
                    c.loaded[j] = P.dma("gpsimd", ring[j % 2], wblock(W, r0, c0), "w%d" % (j % 2), waits=w)
                    c.nxt += 1
                c.cur += 1
                return ring[i % 2], c.loaded[i], i

            @classmethod
            def release(c, i, toks):
                c.rel[i] = [t for t in toks if t is not None]

        t_ident = P.dma("sync", ident, ident_d, "c_id")
        t_gains = P.dma("sync", gains, gains_d, "c_g")
        t_perm = P.dma("gpsimd", perm, perm_d, "cstp")
        t_ones = P.op("vector", lambda e: e.memset(ones, 1.0))
        t_eps = P.op("vector", lambda e: e.memset(epsT, EPS))
        cst = [t_ident, t_gains, t_perm, t_ones, t_eps]

        sq = vb(B + 98304, 16, 256)
        rstd = vf(B + 106496, 256)
        tmpn = vf(B + 107520, 256)
        state = {"sq": [], "rstd": [], "tmpn": [], "xin_i": 0, "ev_i": 0, "xdma": []}

        def load_T(src, dstT, T, xin, xin_war, waits):
            toks = []
            for tt in range(T // 128):
                s = state["xin_i"] % len(xin)
                state["xin_i"] += 1
                tX = P.dma("sync", xin[s], src[tt * 128:(tt + 1) * 128, :], "x%d" % s, waits=xin_war[s])
                state["xdma"].append(tX)
                lastk = None
                for kq in range(4):
                    b = PSA.alloc()
                    pb = bank(b)
                    for i in range(4):
                        kc = kq * 4 + i
                        tk = P.op("tensor",
                                  lambda e, pb=pb, i=i, s=s, kc=kc: e.transpose(out=pb[:, i * 128:(i + 1) * 128], in_=xin[s][:, kc * 128:(kc + 1) * 128], identity=ident),
                                  waits=([tX, t_ident] + PSA.rel[b]) if i == 0 else (), sig=(i == 3))
                    state["ev_i"] += 1
                    dst = dstT[:, kq * 4:(kq + 1) * 4, tt * 128:(tt + 1) * 128]
                    src_ps = pb.rearrange("p (a b) -> p a b", b=128)
                    if state["ev_i"] % 2:
                        ev = P.op("vector", lambda e, dst=dst, src_ps=src_ps: e.tensor_copy(out=dst, in_=src_ps), waits=[tk] + list(waits))
                    else:
                        ev = P.op("scalar", lambda e, dst=dst, src_ps=src_ps: e.copy(out=dst, in_=src_ps), waits=[tk] + list(waits))
                    PSA.release(b, [ev])
                    toks.append(ev)
                    lastk = tk
                xin_war[s] = [lastk]
            return toks

        def norm_T(srcT, T, gcol, dst, src_waits, dst_waits, dmodel=2048):
            sqv = sq[:, :, :T]
            tsq = P.op("scalar", lambda e: e.activation(out=sqv, in_=srcT, func=AF.Square), waits=list(src_waits) + state["sq"])
            b = PSA.alloc()
            ps = bank(b, T)
            tss = mm(ps, [(ones, sq[:, kc, :T]) for kc in range(16)], [tsq, t_ones] + PSA.rel[b])
            state["sq"] = [tss]
            t1 = P.op("scalar", lambda e: e.activation(out=tmpn[:, :T], in_=ps, func=AF.Ln, bias=epsT, scale=1.0 / dmodel), waits=[tss, t_eps] + state["tmpn"])
            PSA.release(b, [t1])
            t2 = P.op("scalar", lambda e: e.activation(out=rstd[:, :T], in_=tmpn[:, :T], func=AF.Exp, scale=-0.5), waits=[t1] + state["rstd"])
            state["tmpn"] = [t2]
            toks = []
            for kc in range(16):
                toks.append(P.op("vector",
                                 lambda e, kc=kc: e.scalar_tensor_tensor(out=dst[:, kc, :], in0=srcT[:, kc, :], scalar=gains[:, gcol + kc:gcol + kc + 1], in1=rstd[:, :T], op0=ALU.mult, op1=ALU.mult),
                                 waits=[t2, t_gains] + (list(dst_waits) if kc == 0 else [])))
            state["rstd"] = [toks[-1]]
            return toks

        att = {"pT_war": [[] for _ in range(8)], "pT_i": 0, "NP": 2, "rl_war": [[], []], "rl_i": 0, "st_war": [[], [], []], "st_i": 0,
               "na_war": [[], [], []], "na_i": 0}

        def attn_unit(QT, tiles, out, NQ, q_waits, kv_waits, pT, rl, stmp=None, bias=None, bias_waits=(), split=False):
            nt = len(tiles)
            base_w = list(q_waits) + list(kv_waits) + [t_ones]
            ctx = {"tokPV": None}

            def open_acc():
                ctx["bO"] = PSA.alloc2()
                ctx["bL"] = ctx["bO"] + 1
                ctx["Oa"] = bank(ctx["bO"], NQ)
                ctx["La"] = bank(ctx["bL"], NQ)

            def issuePV(j, rhs, tP, slots):
                first = (j == 0)
                last = (j == nt - 1)
                Vj = tiles[j][1]
                Oa, La, bO, bL = ctx["Oa"], ctx["La"], ctx["bO"], ctx["bL"]
                P.op("tensor", lambda e, Vj=Vj, rhs=rhs: e.matmul(Oa, lhsT=Vj, rhs=rhs, start=first, stop=last),
                     waits=[tP] + (PSA.rel[bO] + PSA.rel[bL] if first else []), sig=False)
                ctx["tokPV"] = P.op("tensor", lambda e, rhs=rhs: e.matmul(La, lhsT=ones, rhs=rhs, start=first, stop=last), sig=True)
                for s_ in slots:
                    att["pT_war"][s_] = [ctx["tokPV"]]

            def finish_act():
                ri = att["rl_i"] % 2
                att["rl_i"] += 1
                ctx["ri"] = ri
                rv = rl[ri][:, :NQ]
                ctx["rv"] = rv
                if nt >= 16:
                    tR0 = P.op("vector", lambda e: e.reciprocal(out=rv, in_=ctx["La"]), waits=[ctx["tokPV"]] + att["rl_war"][ri])
                    ctx["tR0"] = tR0
                    ctx["tR"] = tR0
                else:
                    tR0 = P.op("scalar", lambda e: e.activation(out=rv, in_=ctx["La"], func=AF.Ln), waits=[ctx["tokPV"]] + att["rl_war"][ri])
                    ctx["tR0"] = tR0
                    ctx["tR"] = P.op("scalar", lambda e: e.activation(out=rv, in_=rv, func=AF.Exp, scale=-1.0), waits=[tR0])

            def finish_mul():
                Oa, bO, bL, rv, ri = ctx["Oa"], ctx["bO"], ctx["bL"], ctx["rv"], ctx["ri"]
                tO = P.op("vector", lambda e: e.tensor_tensor(out=out, in0=Oa, in1=rv, op=ALU.mult), waits=[ctx["tR"]])
                PSA.release(bO, [tO])
                PSA.release(bL, [ctx["tR0"]])
                att["rl_war"][ri] = [tO]
                return tO

            def finish():
                finish_act()
                return finish_mul()

            if bias is None:
                if not split:
                    open_acc()
                assert NQ == 512 and nt % 2 == 0
                npairs = nt // 2
                q = []

                def issueS2(p):
                    b2 = PSA.alloc2()
                    tS = None
                    for k in range(2):
                        KTj = tiles[2 * p + k][0]
                        Sj = psum[:, (b2 + k) * 512:(b2 + k + 1) * 512]
                        tS = P.op("tensor", lambda e, KTj=KTj, Sj=Sj: e.matmul(Sj, lhsT=KTj, rhs=QT, start=True, stop=True),
                                  waits=(base_w + PSA.rel[b2] + PSA.rel[b2 + 1]) if k == 0 else (), sig=(k == 1))
                    half = att["pT_i"] % att["NP"]
                    att["pT_i"] += 1
                    src = psum[:, b2 * 512: b2 * 512 + 1024]
                    dst = pT_flat[:, half * 1024: half * 1024 + 1024]
                    tP = P.op("scalar", lambda e: e.activation(out=dst, in_=src, func=AF.Exp, scale=SCALE),
                              waits=[tS] + att["pT_war"][2 * half] + att["pT_war"][2 * half + 1])
                    PSA.release(b2, [tP])
                    PSA.release(b2 + 1, [tP])
                    q.append((tP, half))

                LOOKP = 2
                for p in range(min(LOOKP, npairs)):
                    issueS2(p)
                if split:
                    assert npairs == 1

                    def pv_only():
                        open_acc()
                        tP, half = q[0]
                        for k in range(2):
                            issuePV(k, pT_flat[:, half * 1024 + k * 512: half * 1024 + (k + 1) * 512], tP, [2 * half, 2 * half + 1])
                        return finish()

                    return pv_only
                for p in range(npairs):
                    tP, half = q[p]
                    for k in range(2):
                        issuePV(2 * p + k, pT_flat[:, half * 1024 + k * 512: half * 1024 + (k + 1) * 512], tP, [2 * half, 2 * half + 1])
                    if p + LOOKP < npairs:
                        issueS2(p + LOOKP)
                return finish()

            b2 = PSA.alloc2()
            S = psum[:, b2 * 512: b2 * 512 + nt * 128]
            tS = None
            for j in range(nt):
                KTj = tiles[j][0]
                Sj = psum[:, b2 * 512 + j * 128: b2 * 512 + (j + 1) * 128]
                tS = P.op("tensor", lambda e, KTj=KTj, Sj=Sj: e.matmul(Sj, lhsT=KTj, rhs=QT, start=True, stop=True),
                          waits=(base_w + PSA.rel[b2] + PSA.rel[b2 + 1]) if j == 0 else (), sig=(j == nt - 1))
            si = att["st_i"] % 3
            att["st_i"] += 1
            sv = stmp[si][:, :nt * 128]
            tB = P.op("vector", lambda e: e.scalar_tensor_tensor(out=sv, in0=S, scalar=SCALE, in1=bias, op0=ALU.mult, op1=ALU.add),
                      waits=[tS] + list(bias_waits) + att["st_war"][si])
            PSA.release(b2, [tB])
            PSA.release(b2 + 1, [tB])
            third = att["na_i"] % 3
            att["na_i"] += 1
            pfull = pT_na[:, third * 1024: third * 1024 + nt * 128]
            tP = P.op("scalar", lambda e: e.activation(out=pfull, in_=sv, func=AF.Exp),
                      waits=[tB] + att["na_war"][third])
            att["st_war"][si] = [tP]

            def pv_stage():
                open_acc()
                for j in range(nt):
                    issuePV(j, pT_na[:, third * 1024 + j * 128: third * 1024 + (j + 1) * 128], tP, [])
                att["na_war"][third] = [ctx["tokPV"]]
                finish_act()
                return finish_mul

            return pv_stage

        def evac_copy(eng, dst, ps, waits):
            if eng == "scalar":
                return P.op("scalar", lambda e: e.copy(out=dst, in_=ps), waits=waits)
            return P.op("vector", lambda e: e.tensor_copy(out=dst, in_=ps), waits=waits)

        def fpat(blk, btok, c, actT, t0, n, act_waits):
            b = PSA.alloc()
            ps = bank(b, n)
            tS = mm(ps, [(blk[:, kc, c * 128:(c + 1) * 128], actT[:, kc, t0:t0 + n]) for kc in range(16)],
                    [btok] + list(act_waits) + PSA.rel[b])
            return b, ps, tS

        def tpat(blk, btok, actT, t0, act_waits):
            b = PSA.alloc()
            ps = bank(b, 512)
            tS = mm(ps, [(actT[:, kc, t0:t0 + 128], blk[:, kc, :]) for kc in range(16)],
                    [btok] + list(act_waits) + PSA.rel[b])
            return b, ps, tS

        def mem_kv(layer, mT_waits, dst_waits):
            blk, btok, bi = WS.pop("kv%dk" % layer)
            toks = []
            last = None
            for c in range(4):
                b, ps, tS = fpat(blk, btok, c, mT, 0, 256, mT_waits)
                ev = evac_copy("scalar", kmT[:, c, :], ps, [tS] + list(dst_waits))
                PSA.release(b, [ev])
                toks.append(ev)
                last = tS
            WS.release(bi, [last])
            blk, btok, bi = WS.pop("kv%dv" % layer)
            for t in range(2):
                b, ps, tS = tpat(blk, btok, mT, t * 128, mT_waits)
                ev = evac_copy("vector", vm[:, t, :], ps, [tS] + list(dst_waits))
                PSA.release(b, [ev])
                toks.append(ev)
                last = tS
            WS.release(bi, [last])
            return toks

        def mem_attn(q_waits, kv_waits, pT, rl):
            toks = []
            pend = None
            for j in range(4):
                for tg in range(2):
                    QT = QC[:, 12 + j, tg * 512:(tg + 1) * 512]
                    tiles = [(kmT[:, j, t * 128:(t + 1) * 128], vm[:, t, j * 128:(j + 1) * 128]) for t in range(2)]
                    st = attn_unit(QT, tiles, QT, 512, q_waits, kv_waits, pT, rl, split=True)
                    if pend is not None:
                        toks.append(pend())
                    pend = st
            toks.append(pend())
            return toks

        def wo_phase(layer, hT, cat_waits, h_waits):
            toks = []
            for cb in range(4):
                blk, btok, bi = WS.pop("wo%d_%d" % (layer, cb))
                last = None
                for c in range(4):
                    for tg in range(2):
                        b, ps, tS = fpat(blk, btok, c, QC, tg * 512, 512, cat_waits)
                        hv = hT[:, cb * 4 + c, tg * 512:(tg + 1) * 512]
                        ev = P.op("vector", lambda e, hv=hv, ps=ps: e.tensor_tensor(out=hv, in0=ps, in1=hv, op=ALU.add), waits=[tS] + list(h_waits))
                        PSA.release(b, [ev])
                        toks.append(ev)
                        last = tS
                WS.release(bi, [last])
            return toks

        def mlp_phase(layer, hT, nT):
            aT = QC
            rt = [vf(B + 108544, 512), vf(B + 110592, 512)]
            rt_war = [[], []]
            ri = 0
            h_toks = []
            a_war = bar()
            for kg in range(4):
                a_toks = []
                for j in range(4):
                    blk, btok, bi = WS.pop("up%d_%d" % (layer, kg * 4 + j))
                    last = None
                    for tg in range(2):
                        for c in range(4):
                            b, ps, tS = fpat(blk, btok, c, nT, tg * 512, 512, nwaits(tg * 512, 512))
                            s = ri % 2
                            ri += 1
                            rv = rt[s]
                            t1 = P.op("scalar", lambda e, rv=rv, ps=ps: e.activation(out=rv, in_=ps, func=AF.Square), waits=[tS] + rt_war[s])
                            av = aT[:, j * 4 + c, tg * 512:(tg + 1) * 512]
                            t2 = P.op("vector", lambda e, rv=rv, ps=ps, av=av: e.scalar_tensor_tensor(out=av, in0=ps, scalar=0.0, in1=rv, op0=ALU.is_gt, op1=ALU.mult),
                                      waits=[t1] + a_war)
                            rt_war[s] = [t2]
                            PSA.release(b, [t2])
                            a_toks.append(t2)
                            last = tS
                    WS.release(bi, [last])
                lastdn = None
                for cb in range(4):
                    blk, btok, bi = WS.pop("dn%d_%d_%d" % (layer, kg, cb))
                    last = None
                    for c in range(4):
                        for tg in range(2):
                            b, ps, tS = fpat(blk, btok, c, aT, tg * 512, 512, a_toks)
                            hv = hT[:, cb * 4 + c, tg * 512:(tg + 1) * 512]
                            ev = P.op("vector", lambda e, hv=hv, ps=ps: e.tensor_tensor(out=hv, in0=ps, in1=hv, op=ALU.add), waits=[tS])
                            PSA.release(b, [ev])
                            h_toks.append(ev)
                            last = tS
                    WS.release(bi, [last])
                    lastdn = last
                a_war = [lastdn]
            return h_toks

        def emit_out(srcT_of, dst, waits_of, otile, ot_war):
            dts = []
            for tt in range(8):
                srcT = srcT_of(tt)
                s = tt % 2
                evs = []
                for kq in range(4):
                    b = PSA.alloc()
                    pb = bank(b)
                    tk = None
                    for i in range(4):
                        kc = kq * 4 + i
                        sv = srcT[:, kc, :]
                        tk = P.op("tensor", lambda e, pb=pb, i=i, sv=sv: e.transpose(out=pb[:, i * 128:(i + 1) * 128], in_=sv, identity=ident),
                                  waits=(list(waits_of(tt)) + [t_ident] + PSA.rel[b]) if i == 0 else (), sig=(i == 3))
                    dstv = otile[s][:, kq * 512:(kq + 1) * 512]
                    ev = evac_copy("scalar" if kq % 2 else "vector", dstv, pb, [tk] + ot_war[s])
                    PSA.release(b, [ev])
                    evs.append(ev)
                td = P.dma("sync", dst[tt * 128:(tt + 1) * 128, :], otile[s], "o%d" % s, waits=evs)
                ot_war[s] = [td]
                dts.append(td)
            return dts

        final_waits = []
        TN = {"g": []}

        def nwaits(t0, n):
            out = []
            for g in range(t0 // 256, (t0 + n - 1) // 256 + 1):
                out += TN["g"][g]
            return out

        def nall():
            out = []
            for g in TN["g"]:
                out += g
            return out

        xinA = [vf(B + 40960, 2048), vf(B + 49152, 2048)]
        if do1:
            xinA += [vf(O_QC + 16384, 2048), vf(O_QC + 24576, 2048)]
        xTa = vf(B + 57344, 16, 256)
        xw = [[] for _ in xinA]
        xTa_war = []

        def mem_norm():
            tl = load_T(mem_d, xTa, 256, xinA, xw, xTa_war)
            return norm_T(xTa, 256, 0, mT, tl, [])

        if not do1:
            t_mT = mem_norm()

        if do1:
            nT0 = vb(B, 16, NEXT)
            TN["g"] = []
            xTb = [xTa, vf(B + 73728, 16, 256)]
            xT_war = [[], []]
            srcs = [x_d[gi * 256:(gi + 1) * 256, :] for gi in range(5)] + [mem_d]
            loads = {}

            def do_load(i):
                loads[i] = load_T(srcs[i], xTb[i % 2], 256, xinA, xw, xT_war[i % 2])

            do_load(0)
            for i in range(6):
                if i + 1 < 6:
                    do_load(i + 1)
                if i < 5:
                    tn = norm_T(xTb[i % 2], 256, 16, nT0[:, :, i * 256:(i + 1) * 256], loads[i], [])
                    TN["g"].append(tn)
                else:
                    tn = norm_T(xTb[i % 2], 256, 0, mT, loads[i], [])
                    t_mT = tn
                xT_war[i % 2] = tn
            t_q = []
            x_all = []
            for i in range(6):
                x_all += loads[i]
            for qi in range(4):
                blk, btok, bi = WS.pop("aq%d" % qi if qi < 3 else "aqm")
                last = None
                for tg in range(2):
                    for c in range(4):
                        b, ps, tS = fpat(blk, btok, c, nT0, tg * 512, 512, nwaits(tg * 512, 512))
                        ev = evac_copy("scalar", QC[:, qi * 4 + c, tg * 512:(tg + 1) * 512], ps, [tS] + (x_all if qi >= 2 else []))
                        PSA.release(b, [ev])
                        t_q.append(ev)
                        last = tS
                WS.release(bi, [last])
            t_kvm = mem_kv(0, t_mT, [])
            KTg = vb(B + 40960, 4, NEXT)
            Vg = vb(B + 51200, 10, 512)
            biasb = [vf(B + 61440, 1664), vf(B + 68096, 1664)]
            pT_flat = vb(B + 74752, 2048)
            pT = [pT_flat[:, i * 512:(i + 1) * 512] for i in range(4)]
            stmp = [vf(B + 78848, 640), vf(B + 81408, 640), vf(B + 94208, 640)]
            pT_na = vb(B + 88064, 3072)
            rl = [vf(B + 83968, 512), vf(B + 86016, 512)]
            kv_war = bar()
            bias_war = [list(kv_war), list(kv_war)]
            t_cat = mem_attn(t_q, t_kvm, pT, rl)
            for g in range(3):
                blk, btok, bi = WS.pop("ak%d" % g)
                t_k = []
                last = None
                for c in range(4):
                    for (t0, n) in ((0, 512), (512, 512), (1024, 256)):
                        b, ps, tS = fpat(blk, btok, c, nT0, t0, n, nwaits(t0, n))
                        ev = evac_copy("scalar", KTg[:, c, t0:t0 + n], ps, [tS] + kv_war)
                        PSA.release(b, [ev])
                        t_k.append(ev)
                        last = tS
                WS.release(bi, [last])
                blk, btok, bi = WS.pop("av%d" % g)
                for t in range(10):
                    b, ps, tS = tpat(blk, btok, nT0, t * 128, nwaits(t * 128, 128))
                    ev = evac_copy("vector" if t % 2 else "scalar", Vg[:, t, :], ps, [tS] + kv_war)
                    PSA.release(b, [ev])
                    t_k.append(ev)
                    last = tS
                WS.release(bi, [last])
                lastatt = None
                pend = []
                mul_q = []
                for hh in range(4):
                    h = g * 4 + hh
                    s = h % 2
                    t_b = P.dma("sync", biasb[s], bias_d[h], "bias%d" % s, waits=bias_war[s])
                    for m in range(8):
                        if m < 2:
                            tl_, boff = [0, 1, 2, 3], m * 512
                        else:
                            tl_, boff = list(range(m - 2, m + 3)), 1024
                        nt = len(tl_)
                        tiles = [(KTg[:, hh, t * 128:(t + 1) * 128], Vg[:, t, hh * 128:(hh + 1) * 128]) for t in tl_]
                        QT = QC[:, h, m * 128:(m + 1) * 128]
                        if len(pend) >= 3:
                            mul_q.append(pend.pop(0)())
                        if len(mul_q) >= 2:
                            lastatt = mul_q.pop(0)()
                            t_cat.append(lastatt)
                        pvs = attn_unit(QT, tiles, QT, 128, t_q, t_k, pT, rl, stmp=stmp,
                                        bias=biasb[s][:, boff:boff + nt * 128], bias_waits=[t_b])
                        pend.append(pvs)
                    bias_war[s] = [P.last("vector")]
                while pend:
                    mul_q.append(pend.pop(0)())
                while mul_q:
                    lastatt = mul_q.pop(0)()
                    t_cat.append(lastatt)
                kv_war = [P.last("tensor"), lastatt]
            hT = vf(B, 16, NTOK)
            xinC = [vf(B + 65536, 2048), vf(B + 73728, 2048)]
            bw = bar()
            xw = [list(bw), list(bw)]
            t_h = []
            for gi in range(4):
                t_h += load_T(x_d[gi * 256:(gi + 1) * 256, :], hT[:, :, gi * 256:(gi + 1) * 256], 256, xinC, xw, bw)
            t_h = wo_phase(0, hT, t_cat, t_h)
            nT = vb(B + 65536, 16, NTOK)
            nw = bar()
            TN["g"] = []
            for gi in range(4):
                TN["g"].append(norm_T(hT[:, :, gi * 256:(gi + 1) * 256], 256, 32, nT[:, :, gi * 256:(gi + 1) * 256], t_h, nw))
            if mode == "fused":
                t_kvm1 = mem_kv(1, t_mT, nw)
            t_h = mlp_phase(0, hT, nT)
            nw = bar()
            TN["g"] = []
            for gi in range(4):
                TN["g"].append(norm_T(hT[:, :, gi * 256:(gi + 1) * 256], 256, 48, nT[:, :, gi * 256:(gi + 1) * 256], t_h, nw))

        if mode == "s2":
            hT = vf(B, 16, NTOK)
            nT = vb(B + 65536, 16, NTOK)
            xinC = [vf(B + 65536, 2048), vf(B + 73728, 2048)]
            bw = bar()
            xw = [list(bw), list(bw)]
            t_h = []
            for gi in range(4):
                t_h += load_T(h1_i[gi * 256:(gi + 1) * 256, :], hT[:, :, gi * 256:(gi + 1) * 256], 256, xinC, xw, bw)
            nw = bar()
            TN["g"] = []
            for gi in range(4):
                TN["g"].append(norm_T(hT[:, :, gi * 256:(gi + 1) * 256], 256, 48, nT[:, :, gi * 256:(gi + 1) * 256], t_h, nw))

        cosT = vf(B + 108544, NTOK)
        sinT = vf(B + 112640, NTOK)
        sqq = vb(B + 116736, 512)
        rstq = vf(B + 117760, 512)
        tq = vf(B + 119808, 512)
        qh = vb(B + 121856, 512)
        t1b = vf(B + 122880, 512)
        t2b = vf(B + 124928, 512)
        qk = {"sqq": [], "rstq": [], "tq": [], "qh": [], "t1": [], "t2": []}
        bw = bar()
        t_cos = P.dma("sync", cosT, cos_d, "c_cos", waits=bw)
        t_sin = P.dma("sync", sinT, sin_d, "c_sin", waits=bw)

        def qk_post(b, ps, tS, gcol, t0, dst, dst_waits, done):
            t1 = P.op("scalar", lambda e: e.activation(out=sqq, in_=ps, func=AF.Square), waits=[tS] + qk["sqq"])
            res = {}

            def stage2():
                b2 = PSA.alloc()
                ps2 = bank(b2)
                t2 = mm(ps2, [(ones, sqq)], [t1, t_ones] + PSA.rel[b2])
                qk["sqq"] = [t2]
                t3 = P.op("scalar", lambda e: e.activation(out=tq, in_=ps2, func=AF.Ln, bias=epsT, scale=1.0 / 128), waits=[t2, t_eps] + qk["tq"])
                PSA.release(b2, [t3])
                t4 = P.op("scalar", lambda e: e.activation(out=rstq, in_=tq, func=AF.Exp, scale=-0.5), waits=[t3] + qk["rstq"])
                qk["tq"] = [t4]
                t5 = P.op("vector", lambda e: e.scalar_tensor_tensor(out=qh, in0=ps, scalar=gains[:, gcol:gcol + 1], in1=rstq, op0=ALU.mult, op1=ALU.mult),
                          waits=[t4, t_gains] + qk["qh"])
                PSA.release(b, [t5])
                qk["rstq"] = [t5]
                res["t5"] = t5

            def stage3():
                t5 = res["t5"]
                b3 = PSA.alloc()
                ps3 = bank(b3)
                t6 = mm(ps3, [(perm, qh)], [t5, t_perm] + PSA.rel[b3])
                cv = cosT[:, t0:t0 + 512]
                sv = sinT[:, t0:t0 + 512]
                t7 = P.op("vector", lambda e: e.tensor_tensor(out=t1b, in0=qh, in1=cv, op=ALU.mult), waits=[t5, t_cos] + qk["t1"])
                t8 = P.op("vector", lambda e: e.tensor_tensor(out=t2b, in0=ps3, in1=sv, op=ALU.mult), waits=[t6, t_sin] + qk["t2"])
                PSA.release(b3, [t8])
                t9 = P.op("vector", lambda e: e.tensor_tensor(out=dst, in0=t1b, in1=t2b, op=ALU.add), waits=[t7, t8] + list(dst_waits))
                qk["qh"] = [t6, t7]
                qk["t1"] = [t9]
                qk["t2"] = [t9]
                done(t9)

            defer(4, stage2)
            defer(20, stage3)

        if do1:
            kst = [vb(B + 98304, 1024), vb(B + 100352, 1024)]
            vst = [vb(B + 102400, 512), vb(B + 103424, 512)]
            kst_war = [nall(), nall()]
            vst_war = [nall(), nall()]
            kv_dmas = []
            blk, btok, bi = WS.pop("bv")
            last = None
            for t in range(8):
                s = t % 2
                b, ps, tS = tpat(blk, btok, nT, t * 128, nwaits(t * 128, 128))
                ev = evac_copy("scalar", vst[s], ps, [tS] + vst_war[s])
                PSA.release(b, [ev])
                td = P.dma("sync", kv_own[:, 4096 + t * 512: 4096 + (t + 1) * 512], vst[s], "kvo_v%d" % s, waits=[ev])
                vst_war[s] = [td]
                kv_dmas.append(td)
                last = tS
            WS.release(bi, [last])
            blk, btok, bi = WS.pop("bk")
            last = None
            for c in range(4):
                s = c % 2
                t9s = []

                def kdone(t9, c=c, s=s, t9s=t9s):
                    t9s.append(t9)
                    if len(t9s) == 2:
                        td = P.dma("sync", kv_own[:, c * 1024:(c + 1) * 1024], kst[s], "kvo_k%d" % s, waits=t9s)
                        kst_war[s] = [td]
                        kv_dmas.append(td)

                if c >= 2:
                    pe_flush()
                for tg in range(2):
                    b, ps, tS = fpat(blk, btok, c, nT, tg * 512, 512, nall())
                    qk_post(b, ps, tS, 97, tg * 512, kst[s][:, tg * 512:(tg + 1) * 512], list(kst_war[s]), kdone)
                    last = tS
            WS.release(bi, [last])
            pe_flush()
            final_waits += kv_dmas

        if mode == "s1":
            otile = [vf(B + 81920, 2048), vf(B + 90112, 2048)]
            bw = bar()
            ow = [list(bw), list(bw)]
            final_waits += emit_out(lambda tt: hT[:, :, tt * 128:(tt + 1) * 128], h1_o, lambda tt: t_h, otile, ow)

        kvfull_waits = []

        if do2:
            t_kvm = t_kvm1 if mode == "fused" else mem_kv(1, t_mT, bar())
            t_q = []
            q_war = bar()
            for qi in range(4):
                blk, btok, bi = WS.pop("bq%d" % qi if qi < 3 else "bqm")
                if qi == 0 and mode == "fused":
                    t_cc = P.custom("gpsimd",
                                    lambda e: e.collective_compute("AllGather", ALU.bypass, replica_groups=[[0, 1], [2, 3], [4, 5], [6, 7]],
                                                                   ins=[kv_own.opt()], outs=[kv_full.opt()]),
                                    "cc", waits=kv_dmas, inc=1)
                    kvfull_waits.append(t_cc)
                last = None
                for c in range(4):
                    for tg in range(2):
                        b, ps, tS = fpat(blk, btok, c, nT, tg * 512, 512, nall())
                        dst = QC[:, qi * 4 + c, tg * 512:(tg + 1) * 512]
                        if qi < 3:
                            qk_post(b, ps, tS, 96, tg * 512, dst, q_war, t_q.append)
                        else:
                            ev = evac_copy("scalar", dst, ps, [tS] + q_war)
                            PSA.release(b, [ev])
                            t_q.append(ev)
                        last = tS
                WS.release(bi, [last])
            pe_flush()
            KTf = vb(B + 65536, 4, 2048)
            Vf = vb(B + 81920, 16, 512)
            pT_flat = vb(B + 98304, 4096)
            pT = [pT_flat[:, i * 512:(i + 1) * 512] for i in range(8)]
            rl = [vf(B + 106496, 512), vf(B + 108544, 512)]
            bw = bar()
            if mode == "fused":
                bw = bw + kv_dmas
            att["NP"] = 4
            att["pT_war"] = [list(bw) for _ in range(8)]
            att["rl_war"] = [list(bw), list(bw)]
            t_kv = []
            for r in range(2):
                t_kv.append(P.dma("sync", KTf[:, :, r * 1024:(r + 1) * 1024],
                                  kv_full[r * 128:(r + 1) * 128, 0:4096].rearrange("p (h t) -> p h t", t=1024), "kvl", waits=bw + kvfull_waits))
                t_kv.append(P.dma("sync", Vf[:, r * 8:(r + 1) * 8, :],
                                  kv_full[r * 128:(r + 1) * 128, 4096:8192].rearrange("p (t n) -> p t n", n=512), "kvl", waits=bw + kvfull_waits))
            t_cat = mem_attn(t_q, t_kvm, pT, rl)
            for h in range(12):
                kvh = h // 3
                for tg in range(2):
                    QT = QC[:, h, tg * 512:(tg + 1) * 512]
                    tiles = [(KTf[:, kvh, kt * 128:(kt + 1) * 128], Vf[:, kt, kvh * 128:(kvh + 1) * 128]) for kt in range(16)]
                    t_cat.append(attn_unit(QT, tiles, QT, 512, t_q, t_kv, pT, rl))
            t_h = wo_phase(1, hT, t_cat, t_h)
            nw = bar()
            TN["g"] = []
            for gi in range(4):
                TN["g"].append(norm_T(hT[:, :, gi * 256:(gi + 1) * 256], 256, 64, nT[:, :, gi * 256:(gi + 1) * 256], t_h, nw))
            t_h = mlp_phase(1, hT, nT)
            yTs = [vf(B + 65536, 16, 256), vf(B + 81920, 16, 256)]
            otile = [vf(O_QC, 2048), vf(O_QC + 8192, 2048), vf(O_QC + 16384, 2048), vf(O_QC + 24576, 2048)]
            bw = bar()
            ow = [list(bw) for _ in range(4)]
            y_wars = [list(bw), list(bw)]
            tys = {}

            def fin_norm(gi):
                tys[gi] = norm_T(hT[:, :, gi * 256:(gi + 1) * 256], 256, 80, yTs[gi % 2], t_h, y_wars[gi % 2])

            fin_norm(0)
            for gi in range(4):
                yT = yTs[gi % 2]
                if gi + 1 < 4 and gi >= 1:
                    pass
                if gi + 1 < 4 and gi == 0:
                    fin_norm(1)
                ty = tys[gi]
                dts = []
                for tt in range(2):
                    srcT = yT[:, :, tt * 128:(tt + 1) * 128]
                    s = (gi * 2 + tt) % 4
                    evs = []
                    lastk = None
                    for kq in range(4):
                        b = PSA.alloc()
                        pb = bank(b)
                        tk = None
                        for i in range(4):
                            kc = kq * 4 + i
                            sv = srcT[:, kc, :]
                            tk = P.op("tensor", lambda e, pb=pb, i=i, sv=sv: e.transpose(out=pb[:, i * 128:(i + 1) * 128], in_=sv, identity=ident),
                                      waits=(list(ty) + [t_ident] + PSA.rel[b]) if i == 0 else (), sig=(i == 3))
                        dstv = otile[s][:, kq * 512:(kq + 1) * 512]
                        ev = evac_copy("scalar" if kq % 2 else "vector", dstv, pb, [tk] + ow[s])
                        PSA.release(b, [ev])
                        evs.append(ev)
                        lastk = tk
                    row = gi * 256 + tt * 128
                    td = P.dma("sync", out_d[row:row + 128, :], otile[s], "o%d" % s, waits=evs)
                    ow[s] = [td]
                    final_waits.append(td)
                y_wars[gi % 2] = [lastk]
                if gi + 2 < 4:
                    fin_norm(gi + 2)

        P.wait_only("sync", final_waits)
        P.replay()
    return nc


def _true_row(l, hf):
    return l if hf == 0 else 31 - l


def _bias_tables(rpb, hf):
    units = [(0, [0, 1, 2, 3]), (1, [0, 1, 2, 3]), (2, [0, 1, 2, 3, 4])]
    p = np.arange(128)
    ki, kc = p // 64, p % 64
    qi, qc = p // 64, p % 64
    cols = []
    for m, tl in units:
        for t in tl:
            kr = np.array([_true_row(2 * t + a, hf) for a in ki])[:, None]
            qr = np.array([_true_row(2 * m + a, hf) for a in qi])[None, :]
            r0 = np.clip(qr - 4, 0, 24)
            vr = (kr >= r0) & (kr < r0 + 8)
            c0 = np.clip(qc - 8, 0, 48)[None, :]
            vc = (kc[:, None] >= c0) & (kc[:, None] < c0 + 16)
            dr = np.clip(kr - qr + 7, 0, 14)
            dc = np.clip(kc[:, None] - qc[None, :] + 15, 0, 30)
            valid = vr & vc
            g = rpb[:, dr, dc]
            cols.append(np.where(valid[None], g, np.float32(MASKV)).astype(np.float32))
    return np.ascontiguousarray(np.concatenate(cols, axis=2))


def _rope_tables(hf):
    t = np.arange(NTOK)
    row = np.array([_true_row(l, hf) for l in (t // 64)], dtype=np.float32)
    col = (t % 64).astype(np.float32)
    inv = np.power(np.float32(10000.0), -np.arange(0, 64, 2, dtype=np.float32) / np.float32(64)).astype(np.float32)
    d = np.arange(128)
    f = d % 32
    pos = np.where((d < 64)[:, None], row[None, :], col[None, :]).astype(np.float32)
    ang = (pos * inv[f][:, None]).astype(np.float32)
    cosT = np.cos(ang).astype(np.float32)
    sgn = np.where((d % 64) < 32, -1.0, 1.0).astype(np.float32)[:, None]
    sinT = (np.sin(ang).astype(np.float32) * sgn).astype(np.float32)
    return np.ascontiguousarray(cosT), np.ascontiguousarray(sinT)


def _fm(vec):
    return np.asarray(vec, dtype=np.float32).reshape(-1, 128).T


_CACHE = {}


def _get_nc(mode):
    if mode not in _CACHE:
        _CACHE[mode] = build(mode)
    return _CACHE[mode]


def kernel(x, mem, mem_norm, attn_norm, mlp_norm, a_w_in, a_rpb, b_w_in, b_q_norm, b_k_norm,
           w_mem_kv, w_o, w_up, w_down, final_norm, _mode="fused"):
    x = np.asarray(x, dtype=np.float32)
    mem = np.asarray(mem, dtype=np.float32)
    gains = np.concatenate([_fm(mem_norm), _fm(attn_norm[0]), _fm(mlp_norm[0]), _fm(attn_norm[1]), _fm(mlp_norm[1]),
                            _fm(final_norm), np.asarray(b_q_norm[0], np.float32)[:, None], np.asarray(b_k_norm[0], np.float32)[:, None]], axis=1)
    gains = np.ascontiguousarray(gains.astype(np.float32))
    d = np.arange(128)
    partner = np.where((d % 64) < 32, d + 32, d - 32)
    perm = np.zeros((128, 128), np.float32)
    perm[partner, d] = 1.0
    ident = np.eye(128, dtype=np.float32)
    rpb = np.asarray(a_rpb[0], np.float32)
    bias = [_bias_tables(rpb, 0), _bias_tables(rpb, 1)]
    rope = [_rope_tables(0), _rope_tables(1)]
    common = {
        "ident": ident, "gains": gains, "perm": perm,
        "b_w_in": np.ascontiguousarray(np.asarray(b_w_in[0], np.float32)),
        "w_mem_kv": np.asarray(w_mem_kv, np.float32).reshape(2 * D, 1024),
        "w_o": np.asarray(w_o, np.float32).reshape(2 * D, D),
        "w_up": np.asarray(w_up, np.float32).reshape(2 * D, 8192),
        "w_down": np.asarray(w_down, np.float32).reshape(2 * 8192, D),
    }
    a_in = np.ascontiguousarray(np.asarray(a_w_in[0], np.float32))
    maps1 = []
    for c in range(8):
        b, hf = c // 2, c % 2
        xb = x[b].reshape(32, 64, D)
        if hf:
            xb = xb[::-1]
        m = dict(common)
        m["x_ext"] = np.ascontiguousarray(xb[:20].reshape(NEXT, D))
        m["mem_b"] = np.ascontiguousarray(mem[b])
        m["bias0"] = bias[hf]
        m["a_w_in"] = a_in
        m["cosT"], m["sinT"] = rope[hf]
        maps1.append(m)
    if _mode == "fused":
        res = run_bass_kernel_spmd(_get_nc("fused"), maps1, core_ids=list(range(8)))
        outs = [r["out"] for r in res.results]
    else:
        res1 = run_bass_kernel_spmd(_get_nc("s1"), maps1, core_ids=list(range(8)))
        maps2 = []
        for c in range(8):
            b, hf = c // 2, c % 2
            m = dict(common)
            m["mem_b"] = maps1[c]["mem_b"]
            m["cosT"], m["sinT"] = rope[hf]
            m["h1"] = np.asarray(res1.results[c]["h1"])
            own = np.asarray(res1.results[c]["kv_own"])
            oth = np.asarray(res1.results[c ^ 1]["kv_own"])
            pair = [own, oth] if hf == 0 else [oth, own]
            m["kv_full"] = np.ascontiguousarray(np.concatenate(pair, axis=0))
            maps2.append(m)
        res2 = run_bass_kernel_spmd(_get_nc("s2"), maps2, core_ids=list(range(8)))
        outs = [r["out"] for r in res2.results]
    out = np.empty((4, 2048, D), np.float32)
    for c in range(8):
        b, hf = c // 2, c % 2
        ob = np.asarray(outs[c], np.float32).reshape(16, 64, D)
        if hf:
            ob = ob[::-1]
        out[b, hf * 1024:(hf + 1) * 1024] = ob.reshape(NTOK, D)
    return out
```

```python
import contextlib
import numpy as np
import ml_dtypes
import concourse.bass as bass
import concourse.mybir as mybir
from concourse.bass_utils import run_bass_kernel_spmd

F32 = mybir.dt.float32
BF16 = mybir.dt.bfloat16
ALU = mybir.AluOpType
AF = mybir.ActivationFunctionType

D = 2048
NTOK = 1024
NEXT = 1280
EPS = 1e-6
SCALE = 128 ** -0.5
MASKV = -30000.0

O_IDENT, O_GAINS, O_ONES, O_PERM, O_EPS = 0, 512, 1024, 1280, 1536
O_MT = 4096
O_KMT, O_VM = 12288, 14336
O_RING = 16384
O_QC = 49152
O_BIG = 81920
TOT = 208896


class Prog:
    ENGS = ("sync", "scalar", "vector", "gpsimd", "tensor")

    def __init__(self, nc, stack):
        self.nc = nc
        self.stack = stack
        self.ops = {e: [] for e in self.ENGS}
        self.sem = {}
        self.cnt = {}

    def _sem(self, key):
        if key not in self.sem:
            self.sem[key] = self.stack.enter_context(self.nc.semaphore(key))
            self.cnt[key] = 0
        return self.sem[key]

    def op(self, eng, fn, waits=(), sig=True):
        tok = None
        if sig:
            key = "e_" + eng
            self._sem(key)
            self.cnt[key] += 1
            tok = (key, self.cnt[key])
        self.ops[eng].append((fn, tuple(w for w in waits if w is not None), tok, 1))
        return tok

    def last(self, eng):
        key = "e_" + eng
        if key in self.cnt and self.cnt[key] > 0:
            return (key, self.cnt[key])
        return None

    def dma(self, eng, out, in_, semkey, waits=()):
        return self.custom(eng, lambda e, out=out, in_=in_: e.dma_start(out=out, in_=in_), semkey, waits)

    def custom(self, eng, fn, semkey, waits=(), inc=16):
        self._sem(semkey)
        self.cnt[semkey] += inc
        tok = (semkey, self.cnt[semkey])
        self.ops[eng].append((fn, tuple(w for w in waits if w is not None), tok, inc))
        return tok

    def wait_only(self, eng, waits):
        self.ops[eng].append((None, tuple(w for w in waits if w is not None), None, 0))

    def replay(self):
        with self.nc.Block() as block:
            for eng in self.ENGS:
                ops = self.ops[eng]
                if not ops:
                    continue

                def body(e, ops=ops):
                    seen = {}
                    for fn, waits, tok, inc in ops:
                        need = {}
                        for (k, v) in waits:
                            if seen.get(k, 0) < v:
                                need[k] = max(need.get(k, 0), v)
                        for k, v in need.items():
                            e.wait_ge(self.sem[k], v)
                            seen[k] = v
                        if fn is not None:
                            inst = fn(e)
                            if tok is not None:
                                inst.then_inc(self.sem[tok[0]], inc)

                getattr(block, eng)(body)


def build(mode):
    nc = bass.Bass("TRN2", target_bir_lowering=False)

    def din(name, shape, dt=F32):
        return nc.dram_tensor(name, shape, dt, kind="ExternalInput").ap()

    def dout(name, shape, dt=F32):
        return nc.dram_tensor(name, shape, dt, kind="ExternalOutput").ap()

    do1 = mode in ("s1", "fused")
    do2 = mode in ("s2", "fused")
    ident_d = din("ident", [128, 128])
    gains_d = din("gains", [128, 98])
    if do1:
        x_d = din("x_ext", [NEXT, D])
        bias_d = din("bias0", [12, 128, 1664])
        a_in = din("a_w_in", [D, 5120])
    mem_d = din("mem_b", [256, D])
    b_in = din("b_w_in", [D, 3072])
    perm_d = din("perm", [128, 128])
    cos_d = din("cosT", [128, NTOK])
    sin_d = din("sinT", [128, NTOK])
    wkv = din("w_mem_kv", [2 * D, 1024])
    wo = din("w_o", [2 * D, D])
    wup = din("w_up", [2 * D, 8192])
    wdn = din("w_down", [2 * 8192, D])
    if mode == "s1":
        h1_o = dout("h1", [NTOK, D])
        kv_own = dout("kv_own", [128, 8192], BF16)
    if mode == "s2":
        h1_i = din("h1", [NTOK, D])
        kv_full = din("kv_full", [256, 8192], BF16)
    if mode == "fused":
        kv_own = nc.dram_tensor("kv_own", [128, 8192], BF16, kind="Internal").ap()
        kv_full = nc.dram_tensor("kv_full", [256, 8192], BF16, kind="Internal").ap()
    if do2:
        out_d = dout("out", [NTOK, D])

    st = contextlib.ExitStack()
    with st:
        P = Prog(nc, st)
        arena = st.enter_context(nc.sbuf_tensor("arena", [128, TOT // 4], F32))
        abf = arena.bitcast(BF16)
        psum = st.enter_context(nc.psum_tensor("ps", [128, 4096], F32))

        def shp(ap, shape):
            if len(shape) == 1:
                return ap
            if len(shape) == 2:
                return ap.rearrange("p (a b) -> p a b", b=shape[1])
            return ap.rearrange("p (a b c) -> p a b c", b=shape[1], c=shape[2])

        def vf(off, *shape):
            n = int(np.prod(shape))
            return shp(arena[:, off // 4: off // 4 + n], shape)

        def vb(off, *shape):
            n = int(np.prod(shape))
            return shp(abf[:, off // 2: off // 2 + n], shape)

        ident = vf(O_IDENT, 128)
        gains = vf(O_GAINS, 98)
        ones = vb(O_ONES, 128)
        perm = vb(O_PERM, 128)
        epsT = vf(O_EPS, 1)
        scr = vf(2048, 512)
        mT = vb(O_MT, 16, 256)
        kmT = vb(O_KMT, 4, 256)
        vm = vb(O_VM, 2, 512)
        ring = [vb(O_RING + i * 16384, 16, 512) for i in range(2)]
        QC = vb(O_QC, 16, 1024)
        B = O_BIG

        class PSA:
            open = [False] * 8
            rel = [[] for _ in range(8)]
            relseq = list(range(8))
            seq = 8

            @classmethod
            def alloc(c):
                free = [b for b in range(8) if not c.open[b]]
                if not free:
                    raise RuntimeError("psum full")
                b = min(free, key=lambda x: c.relseq[x])
                c.open[b] = True
                return b

            @classmethod
            def alloc2(c):
                free = [b for b in range(0, 8, 2) if not c.open[b] and not c.open[b + 1]]
                if not free:
                    raise RuntimeError("psum full2")
                b = min(free, key=lambda x: max(c.relseq[x], c.relseq[x + 1]))
                c.open[b] = c.open[b + 1] = True
                return b

            @classmethod
            def release(c, b, toks):
                c.open[b] = False
                c.rel[b] = [t for t in toks if t is not None]
                c.relseq[b] = c.seq
                c.seq += 1

        def bank(b, n=512):
            return psum[:, b * 512: b * 512 + n]

        PEQ = []
        peq_busy = [False]

        def defer(n, fn):
            PEQ.append([n, fn])

        def pe_tick():
            if peq_busy[0]:
                return
            peq_busy[0] = True
            for ent in PEQ:
                ent[0] -= 1
            while PEQ and PEQ[0][0] <= 0:
                PEQ.pop(0)[1]()
            peq_busy[0] = False

        def pe_flush():
            peq_busy[0] = True
            while PEQ:
                PEQ.pop(0)[1]()
            peq_busy[0] = False

        def mm(out, pairs, waits):
            n = len(pairs)
            tok = None
            for i, (l, r) in enumerate(pairs):
                tok = P.op("tensor",
                           lambda e, l=l, r=r, i=i, out=out: e.matmul(out, lhsT=l, rhs=r, start=(i == 0), stop=(i == n - 1)),
                           waits=waits if i == 0 else (), sig=(i == n - 1))
                if i < n - 1:
                    pe_tick()
            return tok

        def bar():
            return [P.last(e) for e in ("tensor", "scalar", "vector")]

        def wblock(W, r0, c0):
            return W[r0:r0 + 2048, c0:c0 + 512].rearrange("(k p) n -> p k n", p=128)

        plan = []
        if do1:
            plan += [("aq%d" % i, a_in, 0, i * 512) for i in range(3)] + [("aqm", a_in, 0, 4608)]
            plan += [("kv0k", wkv, 0, 0), ("kv0v", wkv, 0, 512)]
            for g in range(3):
                plan += [("ak%d" % g, a_in, 0, 1536 + g * 512), ("av%d" % g, a_in, 0, 3072 + g * 512)]
            plan += [("wo0_%d" % i, wo, 0, i * 512) for i in range(4)]
            if mode == "fused":
                plan += [("kv1k", wkv, D, 0), ("kv1v", wkv, D, 512)]
            for kg in range(4):
                plan += [("up0_%d" % (kg * 4 + j), wup, 0, (kg * 4 + j) * 512) for j in range(4)]
                plan += [("dn0_%d_%d" % (kg, cb), wdn, kg * 2048, cb * 512) for cb in range(4)]
            plan += [("bv", b_in, 0, 2048), ("bk", b_in, 0, 1536)]
        if do2:
            if mode != "fused":
                plan += [("kv1k", wkv, D, 0), ("kv1v", wkv, D, 512)]
            plan += [("bq%d" % i, b_in, 0, i * 512) for i in range(3)] + [("bqm", b_in, 0, 2560)]
            plan += [("wo1_%d" % i, wo, D, i * 512) for i in range(4)]
            for kg in range(4):
                plan += [("up1_%d" % (kg * 4 + j), wup, D, (kg * 4 + j) * 512) for j in range(4)]
                plan += [("dn1_%d_%d" % (kg, cb), wdn, 8192 + kg * 2048, cb * 512) for cb in range(4)]

        class WS:
            nxt = 0
            cur = 0
            rel = {}
            loaded = {}

            @classmethod
            def pop(c, name):
                i = c.cur
                assert plan[i][0] == name, (plan[i][0], name)
                while c.nxt < len(plan) and c.nxt <= i + 1:
                    j = c.nxt
                    w = c.rel.get(j - 2, [])
                    assert j < 2 or (j - 2) in c.rel
                    if j < 2:
                        w = list(state["xdma"][:4])
                    _, W, r0, c0 = plan[j]
                    c.loaded[j] = P.dma("gpsimd", ring[j % 2], wblock(W, r0, c0), "w%d" % (j % 2), waits=w)
                    c.nxt += 1
                c.cur += 1
                return ring[i % 2], c.loaded[i], i

            @classmethod
            def release(c, i, toks):
                c.rel[i] = [t for t in toks if t is not None]

        t_ident = P.dma("sync", ident, ident_d, "c_id")
        t_gains = P.dma("sync", gains, gains_d, "c_g")
        t_perm = P.dma("gpsimd", perm, perm_d, "cstp")
        t_ones = P.op("vector", lambda e: e.memset(ones, 1.0))
        t_eps = P.op("vector", lambda e: e.memset(epsT, EPS))
        cst = [t_ident, t_gains, t_perm, t_ones, t_eps]

        sq = vb(B + 98304, 16, 256)
        rstd = vf(B + 106496, 256)
        tmpn = vf(B + 107520, 256)
        state = {"sq": [], "rstd": [], "tmpn": [], "xin_i": 0, "ev_i": 0, "xdma": []}

        def load_T(src, dstT, T, xin, xin_war, waits):
            toks = []
            for tt in range(T // 128):
                s = state["xin_i"] % len(xin)
                state["xin_i"] += 1
                tX = P.dma("sync", xin[s], src[tt * 128:(tt + 1) * 128, :], "x%d" % s, waits=xin_war[s])
                state["xdma"].append(tX)
                lastk = None
                for kq in range(4):
                    b = PSA.alloc()
                    pb = bank(b)
                    for i in range(4):
                        kc = kq * 4 + i
                        tk = P.op("tensor",
                                  lambda e, pb=pb, i=i, s=s, kc=kc: e.transpose(out=pb[:, i * 128:(i + 1) * 128], in_=xin[s][:, kc * 128:(kc + 1) * 128], identity=ident),
                                  waits=([tX, t_ident] + PSA.rel[b]) if i == 0 else (), sig=(i == 3))
                    state["ev_i"] += 1
                    dst = dstT[:, kq * 4:(kq + 1) * 4, tt * 128:(tt + 1) * 128]
                    src_ps = pb.rearrange("p (a b) -> p a b", b=128)
                    if state["ev_i"] % 2:
                        ev = P.op("vector", lambda e, dst=dst, src_ps=src_ps: e.tensor_copy(out=dst, in_=src_ps), waits=[tk] + list(waits))
                    else:
                        ev = P.op("scalar", lambda e, dst=dst, src_ps=src_ps: e.copy(out=dst, in_=src_ps), waits=[tk] + list(waits))
                    PSA.release(b, [ev])
                    toks.append(ev)
                    lastk = tk
                xin_war[s] = [lastk]
            return toks

        def norm_T(srcT, T, gcol, dst, src_waits, dst_waits, dmodel=2048):
            sqv = sq[:, :, :T]
            tsq = P.op("scalar", lambda e: e.activation(out=sqv, in_=srcT, func=AF.Square), waits=list(src_waits) + state["sq"])
            b = PSA.alloc()
            ps = bank(b, T)
            tss = mm(ps, [(ones, sq[:, kc, :T]) for kc in range(16)], [tsq, t_ones] + PSA.rel[b])
            state["sq"] = [tss]
            t1 = P.op("scalar", lambda e: e.activation(out=tmpn[:, :T], in_=ps, func=AF.Ln, bias=epsT, scale=1.0 / dmodel), waits=[tss, t_eps] + state["tmpn"])
            PSA.release(b, [t1])
            t2 = P.op("scalar", lambda e: e.activation(out=rstd[:, :T], in_=tmpn[:, :T], func=AF.Exp, scale=-0.5), waits=[t1] + state["rstd"])
            state["tmpn"] = [t2]
            toks = []
            for kc in range(16):
                toks.append(P.op("vector",
                                 lambda e, kc=kc: e.scalar_tensor_tensor(out=dst[:, kc, :], in0=srcT[:, kc, :], scalar=gains[:, gcol + kc:gcol + kc + 1], in1=rstd[:, :T], op0=ALU.mult, op1=ALU.mult),
                                 waits=[t2, t_gains] + (list(dst_waits) if kc == 0 else [])))
            state["rstd"] = [toks[-1]]
            return toks

        att = {"pT_war": [[] for _ in range(8)], "pT_i": 0, "NP": 2, "rl_war": [[], []], "rl_i": 0, "st_war": [[], [], []], "st_i": 0,
               "na_war": [[], [], []], "na_i": 0}

        def attn_unit(QT, tiles, out, NQ, q_waits, kv_waits, pT, rl, stmp=None, bias=None, bias_waits=(), split=False):
            nt = len(tiles)
            base_w = list(q_waits) + list(kv_waits) + [t_ones]
            ctx = {"tokPV": None}

            def open_acc():
                ctx["bO"] = PSA.alloc2()
                ctx["bL"] = ctx["bO"] + 1
                ctx["Oa"] = bank(ctx["bO"], NQ)
                ctx["La"] = bank(ctx["bL"], NQ)

            def issuePV(j, rhs, tP, slots):
                first = (j == 0)
                last = (j == nt - 1)
                Vj = tiles[j][1]
                Oa, La, bO, bL = ctx["Oa"], ctx["La"], ctx["bO"], ctx["bL"]
                P.op("tensor", lambda e, Vj=Vj, rhs=rhs: e.matmul(Oa, lhsT=Vj, rhs=rhs, start=first, stop=last),
                     waits=[tP] + (PSA.rel[bO] + PSA.rel[bL] if first else []), sig=False)
                ctx["tokPV"] = P.op("tensor", lambda e, rhs=rhs: e.matmul(La, lhsT=ones, rhs=rhs, start=first, stop=last), sig=True)
                for s_ in slots:
                    att["pT_war"][s_] = [ctx["tokPV"]]

            def finish_act():
                ri = att["rl_i"] % 2
                att["rl_i"] += 1
                ctx["ri"] = ri
                rv = rl[ri][:, :NQ]
                ctx["rv"] = rv
                tR0 = P.op("scalar", lambda e: e.activation(out=rv, in_=ctx["La"], func=AF.Ln), waits=[ctx["tokPV"]] + att["rl_war"][ri])
                ctx["tR0"] = tR0
                ctx["tR"] = P.op("scalar", lambda e: e.activation(out=rv, in_=rv, func=AF.Exp, scale=-1.0), waits=[tR0])

            def finish_mul():
                Oa, bO, bL, rv, ri = ctx["Oa"], ctx["bO"], ctx["bL"], ctx["rv"], ctx["ri"]
                tO = P.op("vector", lambda e: e.tensor_tensor(out=out, in0=Oa, in1=rv, op=ALU.mult), waits=[ctx["tR"]])
                PSA.release(bO, [tO])
                PSA.release(bL, [ctx["tR0"]])
                att["rl_war"][ri] = [tO]
                return tO

            def finish():
                finish_act()
                return finish_mul()

            if bias is None:
                if not split:
                    open_acc()
                assert NQ == 512 and nt % 2 == 0
                npairs = nt // 2
                q = []

                def issueS2(p):
                    b2 = PSA.alloc2()
                    tS = None
                    for k in range(2):
                        KTj = tiles[2 * p + k][0]
                        Sj = psum[:, (b2 + k) * 512:(b2 + k + 1) * 512]
                        tS = P.op("tensor", lambda e, KTj=KTj, Sj=Sj: e.matmul(Sj, lhsT=KTj, rhs=QT, start=True, stop=True),
                                  waits=(base_w + PSA.rel[b2] + PSA.rel[b2 + 1]) if k == 0 else (), sig=(k == 1))
                    half = att["pT_i"] % att["NP"]
                    att["pT_i"] += 1
                    src = psum[:, b2 * 512: b2 * 512 + 1024]
                    dst = pT_flat[:, half * 1024: half * 1024 + 1024]
                    tP = P.op("scalar", lambda e: e.activation(out=dst, in_=src, func=AF.Exp, scale=SCALE),
                              waits=[tS] + att["pT_war"][2 * half] + att["pT_war"][2 * half + 1])
                    PSA.release(b2, [tP])
                    PSA.release(b2 + 1, [tP])
                    q.append((tP, half))

                LOOKP = max(2, att["NP"] - 1)
                for p in range(min(LOOKP, npairs)):
                    issueS2(p)
                if split:
                    assert npairs == 1

                    def pv_only():
                        open_acc()
                        tP, half = q[0]
                        for k in range(2):
                            issuePV(k, pT_flat[:, half * 1024 + k * 512: half * 1024 + (k + 1) * 512], tP, [2 * half, 2 * half + 1])
                        return finish()

                    return pv_only
                for p in range(npairs):
                    tP, half = q[p]
                    for k in range(2):
                        issuePV(2 * p + k, pT_flat[:, half * 1024 + k * 512: half * 1024 + (k + 1) * 512], tP, [2 * half, 2 * half + 1])
                    if p + LOOKP < npairs:
                        issueS2(p + LOOKP)
                return finish()

            b2 = PSA.alloc2()
            S = psum[:, b2 * 512: b2 * 512 + nt * 128]
            tS = None
            for j in range(nt):
                KTj = tiles[j][0]
                Sj = psum[:, b2 * 512 + j * 128: b2 * 512 + (j + 1) * 128]
                tS = P.op("tensor", lambda e, KTj=KTj, Sj=Sj: e.matmul(Sj, lhsT=KTj, rhs=QT, start=True, stop=True),
                          waits=(base_w + PSA.rel[b2] + PSA.rel[b2 + 1]) if j == 0 else (), sig=(j == nt - 1))
            si = att["st_i"] % 3
            att["st_i"] += 1
            sv = stmp[si][:, :nt * 128]
            tB = P.op("vector", lambda e: e.scalar_tensor_tensor(out=sv, in0=S, scalar=SCALE, in1=bias, op0=ALU.mult, op1=ALU.add),
                      waits=[tS] + list(bias_waits) + att["st_war"][si])
            PSA.release(b2, [tB])
            PSA.release(b2 + 1, [tB])
            third = att["na_i"] % 3
            att["na_i"] += 1
            pfull = pT_na[:, third * 1024: third * 1024 + nt * 128]
            tP = P.op("scalar", lambda e: e.activation(out=pfull, in_=sv, func=AF.Exp),
                      waits=[tB] + att["na_war"][third])
            att["st_war"][si] = [tP]

            def pv_stage():
                open_acc()
                for j in range(nt):
                    issuePV(j, pT_na[:, third * 1024 + j * 128: third * 1024 + (j + 1) * 128], tP, [])
                att["na_war"][third] = [ctx["tokPV"]]
                finish_act()
                return finish_mul

            return pv_stage

        def evac_copy(eng, dst, ps, waits):
            if eng == "scalar":
                return P.op("scalar", lambda e: e.copy(out=dst, in_=ps), waits=waits)
            return P.op("vector", lambda e: e.tensor_copy(out=dst, in_=ps), waits=waits)

        def fpat(blk, btok, c, actT, t0, n, act_waits):
            b = PSA.alloc()
            ps = bank(b, n)
            tS = mm(ps, [(blk[:, kc, c * 128:(c + 1) * 128], actT[:, kc, t0:t0 + n]) for kc in range(16)],
                    [btok] + list(act_waits) + PSA.rel[b])
            return b, ps, tS

        def tpat(blk, btok, actT, t0, act_waits):
            b = PSA.alloc()
            ps = bank(b, 512)
            tS = mm(ps, [(actT[:, kc, t0:t0 + 128], blk[:, kc, :]) for kc in range(16)],
                    [btok] + list(act_waits) + PSA.rel[b])
            return b, ps, tS

        def mem_kv(layer, mT_waits, dst_waits):
            blk, btok, bi = WS.pop("kv%dk" % layer)
            toks = []
            last = None
            for c in range(4):
                b, ps, tS = fpat(blk, btok, c, mT, 0, 256, mT_waits)
                ev = evac_copy("scalar", kmT[:, c, :], ps, [tS] + list(dst_waits))
                PSA.release(b, [ev])
                toks.append(ev)
                last = tS
            WS.release(bi, [last])
            blk, btok, bi = WS.pop("kv%dv" % layer)
            for t in range(2):
                b, ps, tS = tpat(blk, btok, mT, t * 128, mT_waits)
                ev = evac_copy("vector", vm[:, t, :], ps, [tS] + list(dst_waits))
                PSA.release(b, [ev])
                toks.append(ev)
                last = tS
            WS.release(bi, [last])
            return toks

        def mem_attn(q_waits, kv_waits, pT, rl):
            toks = []
            pend = None
            for j in range(4):
                for tg in range(2):
                    QT = QC[:, 12 + j, tg * 512:(tg + 1) * 512]
                    tiles = [(kmT[:, j, t * 128:(t + 1) * 128], vm[:, t, j * 128:(j + 1) * 128]) for t in range(2)]
                    st = attn_unit(QT, tiles, QT, 512, q_waits, kv_waits, pT, rl, split=True)
                    if pend is not None:
                        toks.append(pend())
                    pend = st
            toks.append(pend())
            return toks

        def wo_phase(layer, hT, cat_waits, h_waits):
            toks = []
            for cb in range(4):
                blk, btok, bi = WS.pop("wo%d_%d" % (layer, cb))
                last = None
                for c in range(4):
                    for tg in range(2):
                        b, ps, tS = fpat(blk, btok, c, QC, tg * 512, 512, cat_waits)
                        hv = hT[:, cb * 4 + c, tg * 512:(tg + 1) * 512]
                        ev = P.op("vector", lambda e, hv=hv, ps=ps: e.tensor_tensor(out=hv, in0=ps, in1=hv, op=ALU.add), waits=[tS] + list(h_waits))
                        PSA.release(b, [ev])
                        toks.append(ev)
                        last = tS
                WS.release(bi, [last])
            return toks

        def mlp_phase(layer, hT, nT):
            aT = QC
            rt = [vf(B + 108544, 512), vf(B + 110592, 512)]
            rt_war = [[], []]
            ri = 0
            h_toks = []
            a_war = bar()
            for kg in range(4):
                a_toks = []
                for j in range(4):
                    blk, btok, bi = WS.pop("up%d_%d" % (layer, kg * 4 + j))
                    last = None
                    for tg in range(2):
                        for c in range(4):
                            b, ps, tS = fpat(blk, btok, c, nT, tg * 512, 512, nwaits(tg * 512, 512))
                            s = ri % 2
                            ri += 1
                            rv = rt[s]
                            t1 = P.op("scalar", lambda e, rv=rv, ps=ps: e.activation(out=rv, in_=ps, func=AF.Square), waits=[tS] + rt_war[s])
                            av = aT[:, j * 4 + c, tg * 512:(tg + 1) * 512]
                            t2 = P.op("vector", lambda e, rv=rv, ps=ps, av=av: e.scalar_tensor_tensor(out=av, in0=ps, scalar=0.0, in1=rv, op0=ALU.is_gt, op1=ALU.mult),
                                      waits=[t1] + a_war)
                            rt_war[s] = [t2]
                            PSA.release(b, [t2])
                            a_toks.append(t2)
                            last = tS
                    WS.release(bi, [last])
                lastdn = None
                for cb in range(4):
                    blk, btok, bi = WS.pop("dn%d_%d_%d" % (layer, kg, cb))
                    last = None
                    for c in range(4):
                        for tg in range(2):
                            b, ps, tS = fpat(blk, btok, c, aT, tg * 512, 512, a_toks)
                            hv = hT[:, cb * 4 + c, tg * 512:(tg + 1) * 512]
                            ev = P.op("vector", lambda e, hv=hv, ps=ps: e.tensor_tensor(out=hv, in0=ps, in1=hv, op=ALU.add), waits=[tS])
                            PSA.release(b, [ev])
                            h_toks.append(ev)
                            last = tS
                    WS.release(bi, [last])
                    lastdn = last
                a_war = [lastdn]
            return h_toks

        def emit_out(srcT_of, dst, waits_of, otile, ot_war):
            dts = []
            for tt in range(8):
                srcT = srcT_of(tt)
                s = tt % 2
                evs = []
                for kq in range(4):
                    b = PSA.alloc()
                    pb = bank(b)
                    tk = None
                    for i in range(4):
                        kc = kq * 4 + i
                        sv = srcT[:, kc, :]
                        tk = P.op("tensor", lambda e, pb=pb, i=i, sv=sv: e.transpose(out=pb[:, i * 128:(i + 1) * 128], in_=sv, identity=ident),
                                  waits=(list(waits_of(tt)) + [t_ident] + PSA.rel[b]) if i == 0 else (), sig=(i == 3))
                    dstv = otile[s][:, kq * 512:(kq + 1) * 512]
                    ev = evac_copy("scalar" if kq % 2 else "vector", dstv, pb, [tk] + ot_war[s])
                    PSA.release(b, [ev])
                    evs.append(ev)
                td = P.dma("sync", dst[tt * 128:(tt + 1) * 128, :], otile[s], "o%d" % s, waits=evs)
                ot_war[s] = [td]
                dts.append(td)
            return dts

        final_waits = []
        TN = {"g": []}

        def nwaits(t0, n):
            out = []
            for g in range(t0 // 256, (t0 + n - 1) // 256 + 1):
                out += TN["g"][g]
            return out

        def nall():
            out = []
            for g in TN["g"]:
                out += g
            return out

        xinA = [vf(B + 40960, 2048), vf(B + 49152, 2048)]
        if do1:
            xinA += [vf(O_QC + 16384, 2048), vf(O_QC + 24576, 2048)]
        xTa = vf(B + 57344, 16, 256)
        xw = [[] for _ in xinA]
        xTa_war = []

        def mem_norm():
            tl = load_T(mem_d, xTa, 256, xinA, xw, xTa_war)
            return norm_T(xTa, 256, 0, mT, tl, [])

        if not do1:
            t_mT = mem_norm()

        if do1:
            nT0 = vb(B, 16, NEXT)
            TN["g"] = []
            xTb = [xTa, vf(B + 73728, 16, 256)]
            xT_war = [[], []]
            srcs = [x_d[gi * 256:(gi + 1) * 256, :] for gi in range(5)] + [mem_d]
            loads = {}

            def do_load(i):
                loads[i] = load_T(srcs[i], xTb[i % 2], 256, xinA, xw, xT_war[i % 2])

            t_q = []
            qblk = {}
            qlast = {}

            def q_part(qi, tg, extra):
                if qi not in qblk:
                    qblk[qi] = WS.pop("aq%d" % qi if qi < 3 else "aqm")
                blk, btok, bi = qblk[qi]
                for c in range(4):
                    b, ps, tS = fpat(blk, btok, c, nT0, tg * 512, 512, nwaits(tg * 512, 512))
                    ev = evac_copy("scalar", QC[:, qi * 4 + c, tg * 512:(tg + 1) * 512], ps, [tS] + extra)
                    PSA.release(b, [ev])
                    t_q.append(ev)
                    qlast[qi] = tS
                if tg == 1:
                    WS.release(bi, [qlast[qi]])

            do_load(0)
            for i in range(6):
                if i + 1 < 6:
                    do_load(i + 1)
                if i < 5:
                    tn = norm_T(xTb[i % 2], 256, 16, nT0[:, :, i * 256:(i + 1) * 256], loads[i], [])
                    TN["g"].append(tn)
                else:
                    tn = norm_T(xTb[i % 2], 256, 0, mT, loads[i], [])
                    t_mT = tn
                xT_war[i % 2] = tn
                if i == 1:
                    q_part(0, 0, [])
                if i == 3:
                    q_part(0, 1, [])
            x_all = []
            for i in range(6):
                x_all += loads[i]
            for qi in range(1, 4):
                for tg in range(2):
                    q_part(qi, tg, x_all if qi >= 2 else [])
            t_kvm = mem_kv(0, t_mT, [])
            KTg = vb(B + 40960, 4, NEXT)
            Vg = vb(B + 51200, 10, 512)
            biasb = [vf(B + 61440, 1664), vf(B + 68096, 1664)]
            pT_flat = vb(B + 74752, 2048)
            pT = [pT_flat[:, i * 512:(i + 1) * 512] for i in range(4)]
            stmp = [vf(B + 78848, 640), vf(B + 81408, 640), vf(B + 94208, 640)]
            pT_na = vb(B + 88064, 3072)
            rl = [vf(B + 83968, 512), vf(B + 86016, 512)]
            kv_war = bar()
            bias_war = [list(kv_war), list(kv_war)]
            t_cat = mem_attn(t_q, t_kvm, pT, rl)
            for g in range(3):
                blk, btok, bi = WS.pop("ak%d" % g)
                t_k = []
                last = None
                for c in range(4):
                    for (t0, n) in ((0, 512), (512, 512), (1024, 256)):
                        b, ps, tS = fpat(blk, btok, c, nT0, t0, n, nwaits(t0, n))
                        ev = evac_copy("scalar", KTg[:, c, t0:t0 + n], ps, [tS] + kv_war)
                        PSA.release(b, [ev])
                        t_k.append(ev)
                        last = tS
                WS.release(bi, [last])
                blk, btok, bi = WS.pop("av%d" % g)
                for t in range(10):
                    b, ps, tS = tpat(blk, btok, nT0, t * 128, nwaits(t * 128, 128))
                    ev = evac_copy("vector" if t % 2 else "scalar", Vg[:, t, :], ps, [tS] + kv_war)
                    PSA.release(b, [ev])
                    t_k.append(ev)
                    last = tS
                WS.release(bi, [last])
                lastatt = None
                pend = []
                mul_q = []
                for hh in range(4):
                    h = g * 4 + hh
                    s = h % 2
                    t_b = P.dma("sync", biasb[s], bias_d[h], "bias%d" % s, waits=bias_war[s])
                    for m in range(8):
                        if m < 2:
                            tl_, boff = [0, 1, 2, 3], m * 512
                        else:
                            tl_, boff = list(range(m - 2, m + 3)), 1024
                        nt = len(tl_)
                        tiles = [(KTg[:, hh, t * 128:(t + 1) * 128], Vg[:, t, hh * 128:(hh + 1) * 128]) for t in tl_]
                        QT = QC[:, h, m * 128:(m + 1) * 128]
                        if len(pend) >= 3:
                            mul_q.append(pend.pop(0)())
                        if len(mul_q) >= 2:
                            lastatt = mul_q.pop(0)()
                            t_cat.append(lastatt)
                        pvs = attn_unit(QT, tiles, QT, 128, t_q, t_k, pT, rl, stmp=stmp,
                                        bias=biasb[s][:, boff:boff + nt * 128], bias_waits=[t_b])
                        pend.append(pvs)
                    bias_war[s] = [P.last("vector")]
                while pend:
                    mul_q.append(pend.pop(0)())
                while mul_q:
                    lastatt = mul_q.pop(0)()
                    t_cat.append(lastatt)
                kv_war = [P.last("tensor"), lastatt]
            hT = vf(B, 16, NTOK)
            xinC = [vf(B + 65536, 2048), vf(B + 73728, 2048)]
            bw = bar()
            xw = [list(bw), list(bw)]
            t_h = []
            for gi in range(4):
                t_h += load_T(x_d[gi * 256:(gi + 1) * 256, :], hT[:, :, gi * 256:(gi + 1) * 256], 256, xinC, xw, bw)
            t_h = wo_phase(0, hT, t_cat, t_h)
            nT = vb(B + 65536, 16, NTOK)
            nw = bar()
            TN["g"] = []
            for gi in range(4):
                TN["g"].append(norm_T(hT[:, :, gi * 256:(gi + 1) * 256], 256, 32, nT[:, :, gi * 256:(gi + 1) * 256], t_h, nw))
            if mode == "fused":
                t_kvm1 = mem_kv(1, t_mT, nw)
            t_h = mlp_phase(0, hT, nT)
            nw = bar()
            TN["g"] = []
            for gi in range(4):
                TN["g"].append(norm_T(hT[:, :, gi * 256:(gi + 1) * 256], 256, 48, nT[:, :, gi * 256:(gi + 1) * 256], t_h, nw))

        if mode == "s2":
            hT = vf(B, 16, NTOK)
            nT = vb(B + 65536, 16, NTOK)
            xinC = [vf(B + 65536, 2048), vf(B + 73728, 2048)]
            bw = bar()
            xw = [list(bw), list(bw)]
            t_h = []
            for gi in range(4):
                t_h += load_T(h1_i[gi * 256:(gi + 1) * 256, :], hT[:, :, gi * 256:(gi + 1) * 256], 256, xinC, xw, bw)
            nw = bar()
            TN["g"] = []
            for gi in range(4):
                TN["g"].append(norm_T(hT[:, :, gi * 256:(gi + 1) * 256], 256, 48, nT[:, :, gi * 256:(gi + 1) * 256], t_h, nw))

        cosT = vf(B + 108544, NTOK)
        sinT = vf(B + 112640, NTOK)
        sqq = vb(B + 116736, 512)
        rstq = vf(B + 117760, 512)
        tq = vf(B + 119808, 512)
        qh = vb(B + 121856, 512)
        t1b = vf(B + 122880, 512)
        t2b = vf(B + 124928, 512)
        qk = {"sqq": [], "rstq": [], "tq": [], "qh": [], "t1": [], "t2": []}
        bw = bar()
        t_cos = P.dma("sync", cosT, cos_d, "c_cos", waits=bw)
        t_sin = P.dma("sync", sinT, sin_d, "c_sin", waits=bw)

        def qk_post(b, ps, tS, gcol, t0, dst, dst_waits, done):
            t1 = P.op("scalar", lambda e: e.activation(out=sqq, in_=ps, func=AF.Square), waits=[tS] + qk["sqq"])
            res = {}

            def stage2():
                b2 = PSA.alloc()
                ps2 = bank(b2)
                t2 = mm(ps2, [(ones, sqq)], [t1, t_ones] + PSA.rel[b2])
                qk["sqq"] = [t2]
                t3 = P.op("scalar", lambda e: e.activation(out=tq, in_=ps2, func=AF.Ln, bias=epsT, scale=1.0 / 128), waits=[t2, t_eps] + qk["tq"])
                PSA.release(b2, [t3])
                t4 = P.op("scalar", lambda e: e.activation(out=rstq, in_=tq, func=AF.Exp, scale=-0.5), waits=[t3] + qk["rstq"])
                qk["tq"] = [t4]
                t5 = P.op("vector", lambda e: e.scalar_tensor_tensor(out=qh, in0=ps, scalar=gains[:, gcol:gcol + 1], in1=rstq, op0=ALU.mult, op1=ALU.mult),
                          waits=[t4, t_gains] + qk["qh"])
                PSA.release(b, [t5])
                qk["rstq"] = [t5]
                res["t5"] = t5

            def stage3():
                t5 = res["t5"]
                b3 = PSA.alloc()
                ps3 = bank(b3)
                t6 = mm(ps3, [(perm, qh)], [t5, t_perm] + PSA.rel[b3])
                cv = cosT[:, t0:t0 + 512]
                sv = sinT[:, t0:t0 + 512]
                t7 = P.op("vector", lambda e: e.tensor_tensor(out=t1b, in0=qh, in1=cv, op=ALU.mult), waits=[t5, t_cos] + qk["t1"])
                t8 = P.op("vector", lambda e: e.tensor_tensor(out=t2b, in0=ps3, in1=sv, op=ALU.mult), waits=[t6, t_sin] + qk["t2"])
                PSA.release(b3, [t8])
                t9 = P.op("vector", lambda e: e.tensor_tensor(out=dst, in0=t1b, in1=t2b, op=ALU.add), waits=[t7, t8] + list(dst_waits))
                qk["qh"] = [t6, t7]
                qk["t1"] = [t9]
                qk["t2"] = [t9]
                done(t9)

            defer(4, stage2)
            defer(20, stage3)

        if do1:
            kst = [vb(B + 98304, 1024), vb(B + 100352, 1024)]
            vst = [vb(B + 102400, 512), vb(B + 103424, 512)]
            kst_war = [nall(), nall()]
            vst_war = [nall(), nall()]
            kv_dmas = []
            blk, btok, bi = WS.pop("bv")
            last = None
            for t in range(8):
                s = t % 2
                b, ps, tS = tpat(blk, btok, nT, t * 128, nwaits(t * 128, 128))
                ev = evac_copy("scalar", vst[s], ps, [tS] + vst_war[s])
                PSA.release(b, [ev])
                td = P.dma("sync", kv_own[:, 4096 + t * 512: 4096 + (t + 1) * 512], vst[s], "kvo_v%d" % s, waits=[ev])
                vst_war[s] = [td]
                kv_dmas.append(td)
                last = tS
            WS.release(bi, [last])
            blk, btok, bi = WS.pop("bk")
            last = None
            for c in range(4):
                s = c % 2
                t9s = []

                def kdone(t9, c=c, s=s, t9s=t9s):
                    t9s.append(t9)
                    if len(t9s) == 2:
                        td = P.dma("sync", kv_own[:, c * 1024:(c + 1) * 1024], kst[s], "kvo_k%d" % s, waits=t9s)
                        kst_war[s] = [td]
                        kv_dmas.append(td)

                if c >= 2:
                    pe_flush()
                for tg in range(2):
                    b, ps, tS = fpat(blk, btok, c, nT, tg * 512, 512, nall())
                    qk_post(b, ps, tS, 97, tg * 512, kst[s][:, tg * 512:(tg + 1) * 512], list(kst_war[s]), kdone)
                    last = tS
            WS.release(bi, [last])
            pe_flush()
            final_waits += kv_dmas

        if mode == "s1":
            otile = [vf(B + 81920, 2048), vf(B + 90112, 2048)]
            bw = bar()
            ow = [list(bw), list(bw)]
            final_waits += emit_out(lambda tt: hT[:, :, tt * 128:(tt + 1) * 128], h1_o, lambda tt: t_h, otile, ow)

        kvfull_waits = []

        if do2:
            t_kvm = t_kvm1 if mode == "fused" else mem_kv(1, t_mT, bar())
            t_q = []
            q_war = bar()
            for qi in range(4):
                blk, btok, bi = WS.pop("bq%d" % qi if qi < 3 else "bqm")
                if qi == 0 and mode == "fused":
                    t_cc = P.custom("gpsimd",
                                    lambda e: e.collective_compute("AllGather", ALU.bypass, replica_groups=[[0, 1], [2, 3], [4, 5], [6, 7]],
                                                                   ins=[kv_own.opt()], outs=[kv_full.opt()]),
                                    "cc", waits=kv_dmas, inc=1)
                    kvfull_waits.append(t_cc)
                last = None
                for c in range(4):
                    for tg in range(2):
                        b, ps, tS = fpat(blk, btok, c, nT, tg * 512, 512, nall())
                        dst = QC[:, qi * 4 + c, tg * 512:(tg + 1) * 512]
                        if qi < 3:
                            qk_post(b, ps, tS, 96, tg * 512, dst, q_war, t_q.append)
                        else:
                            ev = evac_copy("scalar", dst, ps, [tS] + q_war)
                            PSA.release(b, [ev])
                            t_q.append(ev)
                        last = tS
                WS.release(bi, [last])
            pe_flush()
            KTf = vb(B + 65536, 4, 2048)
            Vf = vb(B + 81920, 16, 512)
            pT_flat = vb(B + 98304, 4096)
            pT = [pT_flat[:, i * 512:(i + 1) * 512] for i in range(8)]
            rl = [vf(B + 106496, 512), vf(B + 108544, 512)]
            bw = bar()
            if mode == "fused":
                bw = bw + kv_dmas
            att["NP"] = 4
            att["pT_war"] = [list(bw) for _ in range(8)]
            att["rl_war"] = [list(bw), list(bw)]
            t_kv = []
            for r in range(2):
                t_kv.append(P.dma("sync", KTf[:, :, r * 1024:(r + 1) * 1024],
                                  kv_full[r * 128:(r + 1) * 128, 0:4096].rearrange("p (h t) -> p h t", t=1024), "kvl", waits=bw + kvfull_waits))
                t_kv.append(P.dma("sync", Vf[:, r * 8:(r + 1) * 8, :],
                                  kv_full[r * 128:(r + 1) * 128, 4096:8192].rearrange("p (t n) -> p t n", n=512), "kvl", waits=bw + kvfull_waits))
            t_cat = mem_attn(t_q, t_kvm, pT, rl)
            for h in range(12):
                kvh = h // 3
                for tg in range(2):
                    QT = QC[:, h, tg * 512:(tg + 1) * 512]
                    tiles = [(KTf[:, kvh, kt * 128:(kt + 1) * 128], Vf[:, kt, kvh * 128:(kvh + 1) * 128]) for kt in range(16)]
                    t_cat.append(attn_unit(QT, tiles, QT, 512, t_q, t_kv, pT, rl))
            t_h = wo_phase(1, hT, t_cat, t_h)
            nw = bar()
            TN["g"] = []
            for gi in range(4):
                TN["g"].append(norm_T(hT[:, :, gi * 256:(gi + 1) * 256], 256, 64, nT[:, :, gi * 256:(gi + 1) * 256], t_h, nw))
            t_h = mlp_phase(1, hT, nT)
            yTs = [vf(B + 65536, 16, 256), vf(B + 81920, 16, 256)]
            otile = [vf(O_QC, 2048), vf(O_QC + 8192, 2048), vf(O_QC + 16384, 2048), vf(O_QC + 24576, 2048)]
            bw = bar()
            ow = [list(bw) for _ in range(4)]
            y_wars = [list(bw), list(bw)]
            tys = {}

            def fin_norm(gi):
                tys[gi] = norm_T(hT[:, :, gi * 256:(gi + 1) * 256], 256, 80, yTs[gi % 2], t_h, y_wars[gi % 2])

            fin_norm(0)
            for gi in range(4):
                yT = yTs[gi % 2]
                if gi + 1 < 4 and gi >= 1:
                    pass
                if gi + 1 < 4 and gi == 0:
                    fin_norm(1)
                ty = tys[gi]
                dts = []
                for tt in range(2):
                    srcT = yT[:, :, tt * 128:(tt + 1) * 128]
                    s = (gi * 2 + tt) % 4
                    evs = []
                    lastk = None
                    for kq in range(4):
                        b = PSA.alloc()
                        pb = bank(b)
                        tk = None
                        for i in range(4):
                            kc = kq * 4 + i
                            sv = srcT[:, kc, :]
                            tk = P.op("tensor", lambda e, pb=pb, i=i, sv=sv: e.transpose(out=pb[:, i * 128:(i + 1) * 128], in_=sv, identity=ident),
                                      waits=(list(ty) + [t_ident] + PSA.rel[b]) if i == 0 else (), sig=(i == 3))
                        dstv = otile[s][:, kq * 512:(kq + 1) * 512]
                        ev = evac_copy("scalar" if kq % 2 else "vector", dstv, pb, [tk] + ow[s])
                        PSA.release(b, [ev])
                        evs.append(ev)
                        lastk = tk
                    row = gi * 256 + tt * 128
                    td = P.dma("sync", out_d[row:row + 128, :], otile[s], "o%d" % s, waits=evs)
                    ow[s] = [td]
                    final_waits.append(td)
                y_wars[gi % 2] = [lastk]
                if gi + 2 < 4:
                    fin_norm(gi + 2)

        P.wait_only("sync", final_waits)
        P.replay()
    return nc


def _true_row(l, hf):
    return l if hf == 0 else 31 - l


def _bias_tables(rpb, hf):
    units = [(0, [0, 1, 2, 3]), (1, [0, 1, 2, 3]), (2, [0, 1, 2, 3, 4])]
    p = np.arange(128)
    ki, kc = p // 64, p % 64
    qi, qc = p // 64, p % 64
    cols = []
    for m, tl in units:
        for t in tl:
            kr = np.array([_true_row(2 * t + a, hf) for a in ki])[:, None]
            qr = np.array([_true_row(2 * m + a, hf) for a in qi])[None, :]
            r0 = np.clip(qr - 4, 0, 24)
            vr = (kr >= r0) & (kr < r0 + 8)
            c0 = np.clip(qc - 8, 0, 48)[None, :]
            vc = (kc[:, None] >= c0) & (kc[:, None] < c0 + 16)
            dr = np.clip(kr - qr + 7, 0, 14)
            dc = np.clip(kc[:, None] - qc[None, :] + 15, 0, 30)
            valid = vr & vc
            g = rpb[:, dr, dc]
            cols.append(np.where(valid[None], g, np.float32(MASKV)).astype(np.float32))
    return np.ascontiguousarray(np.concatenate(cols, axis=2))


def _rope_tables(hf):
    t = np.arange(NTOK)
    row = np.array([_true_row(l, hf) for l in (t // 64)], dtype=np.float32)
    col = (t % 64).astype(np.float32)
    inv = np.power(np.float32(10000.0), -np.arange(0, 64, 2, dtype=np.float32) / np.float32(64)).astype(np.float32)
    d = np.arange(128)
    f = d % 32
    pos = np.where((d < 64)[:, None], row[None, :], col[None, :]).astype(np.float32)
    ang = (pos * inv[f][:, None]).astype(np.float32)
    cosT = np.cos(ang).astype(np.float32)
    sgn = np.where((d % 64) < 32, -1.0, 1.0).astype(np.float32)[:, None]
    sinT = (np.sin(ang).astype(np.float32) * sgn).astype(np.float32)
    return np.ascontiguousarray(cosT), np.ascontiguousarray(sinT)


def _fm(vec):
    return np.asarray(vec, dtype=np.float32).reshape(-1, 128).T


_CACHE = {}


def _get_nc(mode):
    if mode not in _CACHE:
        _CACHE[mode] = build(mode)
    return _CACHE[mode]


def kernel(x, mem, mem_norm, attn_norm, mlp_norm, a_w_in, a_rpb, b_w_in, b_q_norm, b_k_norm,
           w_mem_kv, w_o, w_up, w_down, final_norm, _mode="fused"):
    x = np.asarray(x, dtype=np.float32)
    mem = np.asarray(mem, dtype=np.float32)
    gains = np.concatenate([_fm(mem_norm), _fm(attn_norm[0]), _fm(mlp_norm[0]), _fm(attn_norm[1]), _fm(mlp_norm[1]),
                            _fm(final_norm), np.asarray(b_q_norm[0], np.float32)[:, None], np.asarray(b_k_norm[0], np.float32)[:, None]], axis=1)
    gains = np.ascontiguousarray(gains.astype(np.float32))
    d = np.arange(128)
    partner = np.where((d % 64) < 32, d + 32, d - 32)
    perm = np.zeros((128, 128), np.float32)
    perm[partner, d] = 1.0
    ident = np.eye(128, dtype=np.float32)
    rpb = np.asarray(a_rpb[0], np.float32)
    bias = [_bias_tables(rpb, 0), _bias_tables(rpb, 1)]
    rope = [_rope_tables(0), _rope_tables(1)]
    common = {
        "ident": ident, "gains": gains, "perm": perm,
        "b_w_in": np.ascontiguousarray(np.asarray(b_w_in[0], np.float32)),
        "w_mem_kv": np.asarray(w_mem_kv, np.float32).reshape(2 * D, 1024),
        "w_o": np.asarray(w_o, np.float32).reshape(2 * D, D),
        "w_up": np.asarray(w_up, np.float32).reshape(2 * D, 8192),
        "w_down": np.asarray(w_down, np.float32).reshape(2 * 8192, D),
    }
    a_in = np.ascontiguousarray(np.asarray(a_w_in[0], np.float32))
    maps1 = []
    for c in range(8):
        b, hf = c // 2, c % 2
        xb = x[b].reshape(32, 64, D)
        if hf:
            xb = xb[::-1]
        m = dict(common)
        m["x_ext"] = np.ascontiguousarray(xb[:20].reshape(NEXT, D))
        m["mem_b"] = np.ascontiguousarray(mem[b])
        m["bias0"] = bias[hf]
        m["a_w_in"] = a_in
        m["cosT"], m["sinT"] = rope[hf]
        maps1.append(m)
    if _mode == "fused":
        res = run_bass_kernel_spmd(_get_nc("fused"), maps1, core_ids=list(range(8)))
        outs = [r["out"] for r in res.results]
    else:
        res1 = run_bass_kernel_spmd(_get_nc("s1"), maps1, core_ids=list(range(8)))
        maps2 = []
        for c in range(8):
            b, hf = c // 2, c % 2
            m = dict(common)
            m["mem_b"] = maps1[c]["mem_b"]
            m["cosT"], m["sinT"] = rope[hf]
            m["h1"] = np.asarray(res1.results[c]["h1"])
            own = np.asarray(res1.results[c]["kv_own"])
            oth = np.asarray(res1.results[c ^ 1]["kv_own"])
            pair = [own, oth] if hf == 0 else [oth, own]
            m["kv_full"] = np.ascontiguousarray(np.concatenate(pair, axis=0))
            maps2.append(m)
        res2 = run_bass_kernel_spmd(_get_nc("s2"), maps2, core_ids=list(range(8)))
        outs = [r["out"] for r in res2.results]
    out = np.empty((4, 2048, D), np.float32)
    for c in range(8):
        b, hf = c // 2, c % 2
        ob = np.asarray(outs[c], np.float32).reshape(16, 64, D)
        if hf:
            ob = ob[::-1]
        out[b, hf * 1024:(hf + 1) * 1024] = ob.reshape(NTOK, D)
    return out
```

```python
import contextlib
import numpy as np
import ml_dtypes
import concourse.bass as bass
import concourse.mybir as mybir
from concourse.bass_utils import run_bass_kernel_spmd

F32 = mybir.dt.float32
BF16 = mybir.dt.bfloat16
ALU = mybir.AluOpType
AF = mybir.ActivationFunctionType

D = 2048
NTOK = 1024
NEXT = 1280
EPS = 1e-6
SCALE = 128 ** -0.5
MASKV = -30000.0

O_IDENT, O_GAINS, O_ONES, O_PERM, O_EPS = 0, 512, 1024, 1280, 1536
O_MT = 4096
O_KMT, O_VM = 12288, 14336
O_RING = 16384
O_QC = 49152
O_BIG = 81920
TOT = 208896


class Prog:
    ENGS = ("sync", "scalar", "vector", "gpsimd", "tensor")

    def __init__(self, nc, stack):
        self.nc = nc
        self.stack = stack
        self.ops = {e: [] for e in self.ENGS}
        self.sem = {}
        self.cnt = {}

    def _sem(self, key):
        if key not in self.sem:
            self.sem[key] = self.stack.enter_context(self.nc.semaphore(key))
            self.cnt[key] = 0
        return self.sem[key]

    def op(self, eng, fn, waits=(), sig=True):
        tok = None
        if sig:
            key = "e_" + eng
            self._sem(key)
            self.cnt[key] += 1
            tok = (key, self.cnt[key])
        self.ops[eng].append((fn, tuple(w for w in waits if w is not None), tok, 1))
        return tok

    def last(self, eng):
        key = "e_" + eng
        if key in self.cnt and self.cnt[key] > 0:
            return (key, self.cnt[key])
        return None

    def dma(self, eng, out, in_, semkey, waits=()):
        return self.custom(eng, lambda e, out=out, in_=in_: e.dma_start(out=out, in_=in_), semkey, waits)

    def custom(self, eng, fn, semkey, waits=(), inc=16):
        self._sem(semkey)
        self.cnt[semkey] += inc
        tok = (semkey, self.cnt[semkey])
        self.ops[eng].append((fn, tuple(w for w in waits if w is not None), tok, inc))
        return tok

    def wait_only(self, eng, waits):
        self.ops[eng].append((None, tuple(w for w in waits if w is not None), None, 0))

    def replay(self):
        with self.nc.Block() as block:
            for eng in self.ENGS:
                ops = self.ops[eng]
                if not ops:
                    continue

                def body(e, ops=ops):
                    seen = {}
                    for fn, waits, tok, inc in ops:
                        need = {}
                        for (k, v) in waits:
                            if seen.get(k, 0) < v:
                                need[k] = max(need.get(k, 0), v)
                        for k, v in need.items():
                            e.wait_ge(self.sem[k], v)
                            seen[k] = v
                        if fn is not None:
                            inst = fn(e)
                            if tok is not None:
                                inst.then_inc(self.sem[tok[0]], inc)

                getattr(block, eng)(body)


def build(mode):
    nc = bass.Bass("TRN2", target_bir_lowering=False)

    def din(name, shape, dt=F32):
        return nc.dram_tensor(name, shape, dt, kind="ExternalInput").ap()

    def dout(name, shape, dt=F32):
        return nc.dram_tensor(name, shape, dt, kind="ExternalOutput").ap()

    do1 = mode in ("s1", "fused")
    do2 = mode in ("s2", "fused")
    ident_d = din("ident", [128, 128])
    gains_d = din("gains", [128, 98])
    if do1:
        x_d = din("x_ext", [NEXT, D])
        bias_d = din("bias0", [12, 128, 1664])
        a_in = din("a_w_in", [D, 5120])
    mem_d = din("mem_b", [256, D])
    b_in = din("b_w_in", [D, 3072])
    perm_d = din("perm", [128, 128])
    cos_d = din("cosT", [128, NTOK])
    sin_d = din("sinT", [128, NTOK])
    wkv = din("w_mem_kv", [2 * D, 1024])
    wo = din("w_o", [2 * D, D])
    wup = din("w_up", [2 * D, 8192])
    wdn = din("w_down", [2 * 8192, D])
    if mode == "s1":
        h1_o = dout("h1", [NTOK, D])
        kv_own = dout("kv_own", [128, 8192], BF16)
    if mode == "s2":
        h1_i = din("h1", [NTOK, D])
        kv_full = din("kv_full", [256, 8192], BF16)
    if mode == "fused":
        kv_own = nc.dram_tensor("kv_own", [128, 8192], BF16, kind="Internal").ap()
        kv_full = nc.dram_tensor("kv_full", [256, 8192], BF16, kind="Internal").ap()
    if do2:
        out_d = dout("out", [NTOK, D])

    st = contextlib.ExitStack()
    with st:
        P = Prog(nc, st)
        arena = st.enter_context(nc.sbuf_tensor("arena", [128, TOT // 4], F32))
        abf = arena.bitcast(BF16)
        psum = st.enter_context(nc.psum_tensor("ps", [128, 4096], F32))

        def shp(ap, shape):
            if len(shape) == 1:
                return ap
            if len(shape) == 2:
                return ap.rearrange("p (a b) -> p a b", b=shape[1])
            return ap.rearrange("p (a b c) -> p a b c", b=shape[1], c=shape[2])

        def vf(off, *shape):
            n = int(np.prod(shape))
            return shp(arena[:, off // 4: off // 4 + n], shape)

        def vb(off, *shape):
            n = int(np.prod(shape))
            return shp(abf[:, off // 2: off // 2 + n], shape)

        ident = vf(O_IDENT, 128)
        gains = vf(O_GAINS, 98)
        ones = vb(O_ONES, 128)
        perm = vb(O_PERM, 128)
        epsT = vf(O_EPS, 1)
        scr = vf(2048, 512)
        mT = vb(O_MT, 16, 256)
        kmT = vb(O_KMT, 4, 256)
        vm = vb(O_VM, 2, 512)
        ring = [vb(O_RING + i * 16384, 16, 512) for i in range(2)]
        QC = vb(O_QC, 16, 1024)
        B = O_BIG

        class PSA:
            open = [False] * 8
            rel = [[] for _ in range(8)]
            relseq = list(range(8))
            seq = 8

            @classmethod
            def alloc(c):
                free = [b for b in range(8) if not c.open[b]]
                if not free:
                    raise RuntimeError("psum full")
                b = min(free, key=lambda x: c.relseq[x])
                c.open[b] = True
                return b

            @classmethod
            def alloc2(c):
                free = [b for b in range(0, 8, 2) if not c.open[b] and not c.open[b + 1]]
                if not free:
                    raise RuntimeError("psum full2")
                b = min(free, key=lambda x: max(c.relseq[x], c.relseq[x + 1]))
                c.open[b] = c.open[b + 1] = True
                return b

            @classmethod
            def release(c, b, toks):
                c.open[b] = False
                c.rel[b] = [t for t in toks if t is not None]
                c.relseq[b] = c.seq
                c.seq += 1

        def bank(b, n=512):
            return psum[:, b * 512: b * 512 + n]

        PEQ = []
        peq_busy = [False]

        def defer(n, fn):
            PEQ.append([n, fn])

        def pe_tick():
            if peq_busy[0]:
                return
            peq_busy[0] = True
            for ent in PEQ:
                ent[0] -= 1
            while PEQ and PEQ[0][0] <= 0:
                PEQ.pop(0)[1]()
            peq_busy[0] = False

        def pe_flush():
            peq_busy[0] = True
            while PEQ:
                PEQ.pop(0)[1]()
            peq_busy[0] = False

        def mm(out, pairs, waits):
            n = len(pairs)
            tok = None
            for i, (l, r) in enumerate(pairs):
                tok = P.op("tensor",
                           lambda e, l=l, r=r, i=i, out=out: e.matmul(out, lhsT=l, rhs=r, start=(i == 0), stop=(i == n - 1)),
                           waits=waits if i == 0 else (), sig=(i == n - 1))
                if i < n - 1:
                    pe_tick()
            return tok

        def bar():
            return [P.last(e) for e in ("tensor", "scalar", "vector")]

        def wblock(W, r0, c0):
            return W[r0:r0 + 2048, c0:c0 + 512].rearrange("(k p) n -> p k n", p=128)

        plan = []
        if do1:
            plan += [("aq%d" % i, a_in, 0, i * 512) for i in range(3)] + [("aqm", a_in, 0, 4608)]
            plan += [("kv0k", wkv, 0, 0), ("kv0v", wkv, 0, 512)]
            for g in range(3):
                plan += [("ak%d" % g, a_in, 0, 1536 + g * 512), ("av%d" % g, a_in, 0, 3072 + g * 512)]
            plan += [("wo0_%d" % i, wo, 0, i * 512) for i in range(4)]
            if mode == "fused":
                plan += [("kv1k", wkv, D, 0), ("kv1v", wkv, D, 512)]
            for kg in range(4):
                plan += [("up0_%d" % (kg * 4 + j), wup, 0, (kg * 4 + j) * 512) for j in range(4)]
                plan += [("dn0_%d_%d" % (kg, cb), wdn, kg * 2048, cb * 512) for cb in range(4)]
            plan += [("bv", b_in, 0, 2048), ("bk", b_in, 0, 1536)]
        if do2:
            if mode != "fused":
                plan += [("kv1k", wkv, D, 0), ("kv1v", wkv, D, 512)]
            plan += [("bq%d" % i, b_in, 0, i * 512) for i in range(3)] + [("bqm", b_in, 0, 2560)]
            plan += [("wo1_%d" % i, wo, D, i * 512) for i in range(4)]
            for kg in range(4):
                plan += [("up1_%d" % (kg * 4 + j), wup, D, (kg * 4 + j) * 512) for j in range(4)]
                plan += [("dn1_%d_%d" % (kg, cb), wdn, 8192 + kg * 2048, cb * 512) for cb in range(4)]

        class WS:
            nxt = 0
            cur = 0
            rel = {}
            loaded = {}

            @classmethod
            def pop(c, name):
                i = c.cur
                assert plan[i][0] == name, (plan[i][0], name)
                while c.nxt < len(plan) and c.nxt <= i + 1:
                    j = c.nxt
                    w = c.rel.get(j - 2, [])
                    assert j < 2 or (j - 2) in c.rel
                    if j < 2:
                        w = list(state["xdma"][:2])
                    _, W, r0, c0 = plan[j]
                    c.loaded[j] = P.dma("gpsimd", ring[j % 2], wblock(W, r0, c0), "w%d" % (j % 2), waits=w)
                    c.nxt += 1
                c.cur += 1
                return ring[i % 2], c.loaded[i], i

            @classmethod
            def release(c, i, toks):
                c.rel[i] = [t for t in toks if t is not None]

        t_ident = P.dma("sync", ident, ident_d, "c_id")
        t_gains = P.dma("sync", gains, gains_d, "c_g")
        t_perm = P.dma("gpsimd", perm, perm_d, "cstp")
        t_ones = P.op("vector", lambda e: e.memset(ones, 1.0))
        t_eps = P.op("vector", lambda e: e.memset(epsT, EPS))
        cst = [t_ident, t_gains, t_perm, t_ones, t_eps]

        sq = vb(B + 98304, 16, 256)
        rstd = vf(B + 106496, 256)
        tmpn = vf(B + 107520, 256)
        state = {"sq": [], "rstd": [], "tmpn": [], "xin_i": 0, "ev_i": 0, "xdma": []}

        def load_T(src, dstT, T, xin, xin_war, waits):
            toks = []
            for tt in range(T // 128):
                s = state["xin_i"] % len(xin)
                state["xin_i"] += 1
                tX = P.dma("sync", xin[s], src[tt * 128:(tt + 1) * 128, :], "x%d" % s, waits=xin_war[s])
                state["xdma"].append(tX)
                lastk = None
                for kq in range(4):
                    b = PSA.alloc()
                    pb = bank(b)
                    for i in range(4):
                        kc = kq * 4 + i
                        tk = P.op("tensor",
                                  lambda e, pb=pb, i=i, s=s, kc=kc: e.transpose(out=pb[:, i * 128:(i + 1) * 128], in_=xin[s][:, kc * 128:(kc + 1) * 128], identity=ident),
                                  waits=([tX, t_ident] + PSA.rel[b]) if i == 0 else (), sig=(i == 3))
                    state["ev_i"] += 1
                    dst = dstT[:, kq * 4:(kq + 1) * 4, tt * 128:(tt + 1) * 128]
                    src_ps = pb.rearrange("p (a b) -> p a b", b=128)
                    if state["ev_i"] % 2:
                        ev = P.op("vector", lambda e, dst=dst, src_ps=src_ps: e.tensor_copy(out=dst, in_=src_ps), waits=[tk] + list(waits))
                    else:
                        ev = P.op("scalar", lambda e, dst=dst, src_ps=src_ps: e.copy(out=dst, in_=src_ps), waits=[tk] + list(waits))
                    PSA.release(b, [ev])
                    toks.append(ev)
                    lastk = tk
                xin_war[s] = [lastk]
            return toks

        def norm_T(srcT, T, gcol, dst, src_waits, dst_waits, dmodel=2048):
            sqv = sq[:, :, :T]
            tsq = P.op("scalar", lambda e: e.activation(out=sqv, in_=srcT, func=AF.Square), waits=list(src_waits) + state["sq"])
            b = PSA.alloc()
            ps = bank(b, T)
            tss = mm(ps, [(ones, sq[:, kc, :T]) for kc in range(16)], [tsq, t_ones] + PSA.rel[b])
            state["sq"] = [tss]
            t1 = P.op("scalar", lambda e: e.activation(out=tmpn[:, :T], in_=ps, func=AF.Ln, bias=epsT, scale=1.0 / dmodel), waits=[tss, t_eps] + state["tmpn"])
            PSA.release(b, [t1])
            t2 = P.op("scalar", lambda e: e.activation(out=rstd[:, :T], in_=tmpn[:, :T], func=AF.Exp, scale=-0.5), waits=[t1] + state["rstd"])
            state["tmpn"] = [t2]
            toks = []
            for kc in range(16):
                toks.append(P.op("vector",
                                 lambda e, kc=kc: e.scalar_tensor_tensor(out=dst[:, kc, :], in0=srcT[:, kc, :], scalar=gains[:, gcol + kc:gcol + kc + 1], in1=rstd[:, :T], op0=ALU.mult, op1=ALU.mult),
                                 waits=[t2, t_gains] + (list(dst_waits) if kc == 0 else [])))
            state["rstd"] = [toks[-1]]
            return toks

        att = {"pT_war": [[] for _ in range(8)], "pT_i": 0, "NP": 2, "rl_war": [[], []], "rl_i": 0, "st_war": [[], [], []], "st_i": 0,
               "na_war": [[], [], []], "na_i": 0}

        def attn_unit(QT, tiles, out, NQ, q_waits, kv_waits, pT, rl, stmp=None, bias=None, bias_waits=(), split=False):
            nt = len(tiles)
            base_w = list(q_waits) + list(kv_waits) + [t_ones]
            ctx = {"tokPV": None}

            def open_acc():
                ctx["bO"] = PSA.alloc2()
                ctx["bL"] = ctx["bO"] + 1
                ctx["Oa"] = bank(ctx["bO"], NQ)
                ctx["La"] = bank(ctx["bL"], NQ)

            def issuePV(j, rhs, tP, slots):
                first = (j == 0)
                last = (j == nt - 1)
                Vj = tiles[j][1]
                Oa, La, bO, bL = ctx["Oa"], ctx["La"], ctx["bO"], ctx["bL"]
                P.op("tensor", lambda e, Vj=Vj, rhs=rhs: e.matmul(Oa, lhsT=Vj, rhs=rhs, start=first, stop=last),
                     waits=[tP] + (PSA.rel[bO] + PSA.rel[bL] if first else []), sig=False)
                ctx["tokPV"] = P.op("tensor", lambda e, rhs=rhs: e.matmul(La, lhsT=ones, rhs=rhs, start=first, stop=last), sig=True)
                for s_ in slots:
                    att["pT_war"][s_] = [ctx["tokPV"]]

            def finish_act():
                ri = att["rl_i"] % 2
                att["rl_i"] += 1
                ctx["ri"] = ri
                rv = rl[ri][:, :NQ]
                ctx["rv"] = rv
                tR0 = P.op("scalar", lambda e: e.activation(out=rv, in_=ctx["La"], func=AF.Ln), waits=[ctx["tokPV"]] + att["rl_war"][ri])
                ctx["tR0"] = tR0
                ctx["tR"] = P.op("scalar", lambda e: e.activation(out=rv, in_=rv, func=AF.Exp, scale=-1.0), waits=[tR0])

            def finish_mul():
                Oa, bO, bL, rv, ri = ctx["Oa"], ctx["bO"], ctx["bL"], ctx["rv"], ctx["ri"]
                tO = P.op("vector", lambda e: e.tensor_tensor(out=out, in0=Oa, in1=rv, op=ALU.mult), waits=[ctx["tR"]])
                PSA.release(bO, [tO])
                PSA.release(bL, [ctx["tR0"]])
                att["rl_war"][ri] = [tO]
                return tO

            def finish():
                finish_act()
                return finish_mul()

            if bias is None:
                if not split:
                    open_acc()
                assert NQ == 512 and nt % 2 == 0
                npairs = nt // 2
                q = []

                def issueS2(p):
                    b2 = PSA.alloc2()
                    tS = None
                    for k in range(2):
                        KTj = tiles[2 * p + k][0]
                        Sj = psum[:, (b2 + k) * 512:(b2 + k + 1) * 512]
                        tS = P.op("tensor", lambda e, KTj=KTj, Sj=Sj: e.matmul(Sj, lhsT=KTj, rhs=QT, start=True, stop=True),
                                  waits=(base_w + PSA.rel[b2] + PSA.rel[b2 + 1]) if k == 0 else (), sig=(k == 1))
                    half = att["pT_i"] % att["NP"]
                    att["pT_i"] += 1
                    src = psum[:, b2 * 512: b2 * 512 + 1024]
                    dst = pT_flat[:, half * 1024: half * 1024 + 1024]
                    tP = P.op("scalar", lambda e: e.activation(out=dst, in_=src, func=AF.Exp, scale=SCALE),
                              waits=[tS] + att["pT_war"][2 * half] + att["pT_war"][2 * half + 1])
                    PSA.release(b2, [tP])
                    PSA.release(b2 + 1, [tP])
                    q.append((tP, half))

                LOOKP = max(2, att["NP"] - 1)
                for p in range(min(LOOKP, npairs)):
                    issueS2(p)
                if split:
                    assert npairs == 1

                    def pv_only():
                        open_acc()
                        tP, half = q[0]
                        for k in range(2):
                            issuePV(k, pT_flat[:, half * 1024 + k * 512: half * 1024 + (k + 1) * 512], tP, [2 * half, 2 * half + 1])
                        return finish()

                    return pv_only
                for p in range(npairs):
                    tP, half = q[p]
                    for k in range(2):
                        issuePV(2 * p + k, pT_flat[:, half * 1024 + k * 512: half * 1024 + (k + 1) * 512], tP, [2 * half, 2 * half + 1])
                    if p + LOOKP < npairs:
                        issueS2(p + LOOKP)
                return finish()

            b2 = PSA.alloc2()
            S = psum[:, b2 * 512: b2 * 512 + nt * 128]
            tS = None
            for j in range(nt):
                KTj = tiles[j][0]
                Sj = psum[:, b2 * 512 + j * 128: b2 * 512 + (j + 1) * 128]
                tS = P.op("tensor", lambda e, KTj=KTj, Sj=Sj: e.matmul(Sj, lhsT=KTj, rhs=QT, start=True, stop=True),
                          waits=(base_w + PSA.rel[b2] + PSA.rel[b2 + 1]) if j == 0 else (), sig=(j == nt - 1))
            si = att["st_i"] % 3
            att["st_i"] += 1
            sv = stmp[si][:, :nt * 128]
            tB = P.op("vector", lambda e: e.scalar_tensor_tensor(out=sv, in0=S, scalar=SCALE, in1=bias, op0=ALU.mult, op1=ALU.add),
                      waits=[tS] + list(bias_waits) + att["st_war"][si])
            PSA.release(b2, [tB])
            PSA.release(b2 + 1, [tB])
            third = att["na_i"] % 3
            att["na_i"] += 1
            pfull = pT_na[:, third * 1024: third * 1024 + nt * 128]
            tP = P.op("scalar", lambda e: e.activation(out=pfull, in_=sv, func=AF.Exp),
                      waits=[tB] + att["na_war"][third])
            att["st_war"][si] = [tP]

            def pv_stage():
                open_acc()
                for j in range(nt):
                    issuePV(j, pT_na[:, third * 1024 + j * 128: third * 1024 + (j + 1) * 128], tP, [])
                att["na_war"][third] = [ctx["tokPV"]]
                finish_act()
                return finish_mul

            return pv_stage

        def evac_copy(eng, dst, ps, waits):
            if eng == "scalar":
                return P.op("scalar", lambda e: e.copy(out=dst, in_=ps), waits=waits)
            return P.op("vector", lambda e: e.tensor_copy(out=dst, in_=ps), waits=waits)

        def fpat(blk, btok, c, actT, t0, n, act_waits):
            b = PSA.alloc()
            ps = bank(b, n)
            tS = mm(ps, [(blk[:, kc, c * 128:(c + 1) * 128], actT[:, kc, t0:t0 + n]) for kc in range(16)],
                    [btok] + list(act_waits) + PSA.rel[b])
            return b, ps, tS

        def tpat(blk, btok, actT, t0, act_waits):
            b = PSA.alloc()
            ps = bank(b, 512)
            tS = mm(ps, [(actT[:, kc, t0:t0 + 128], blk[:, kc, :]) for kc in range(16)],
                    [btok] + list(act_waits) + PSA.rel[b])
            return b, ps, tS

        def mem_kv(layer, mT_waits, dst_waits):
            blk, btok, bi = WS.pop("kv%dk" % layer)
            toks = []
            last = None
            for c in range(4):
                b, ps, tS = fpat(blk, btok, c, mT, 0, 256, mT_waits)
                ev = evac_copy("scalar", kmT[:, c, :], ps, [tS] + list(dst_waits))
                PSA.release(b, [ev])
                toks.append(ev)
                last = tS
            WS.release(bi, [last])
            blk, btok, bi = WS.pop("kv%dv" % layer)
            for t in range(2):
                b, ps, tS = tpat(blk, btok, mT, t * 128, mT_waits)
                ev = evac_copy("vector", vm[:, t, :], ps, [tS] + list(dst_waits))
                PSA.release(b, [ev])
                toks.append(ev)
                last = tS
            WS.release(bi, [last])
            return toks

        def mem_attn(q_waits, kv_waits, pT, rl):
            toks = []
            pend = None
            for j in range(4):
                for tg in range(2):
                    QT = QC[:, 12 + j, tg * 512:(tg + 1) * 512]
                    tiles = [(kmT[:, j, t * 128:(t + 1) * 128], vm[:, t, j * 128:(j + 1) * 128]) for t in range(2)]
                    st = attn_unit(QT, tiles, QT, 512, q_waits, kv_waits, pT, rl, split=True)
                    if pend is not None:
                        toks.append(pend())
                    pend = st
            toks.append(pend())
            return toks

        def wo_phase(layer, hT, cat_waits, h_waits):
            toks = []
            for cb in range(4):
                blk, btok, bi = WS.pop("wo%d_%d" % (layer, cb))
                last = None
                for c in range(4):
                    for tg in range(2):
                        b, ps, tS = fpat(blk, btok, c, QC, tg * 512, 512, cat_waits)
                        hv = hT[:, cb * 4 + c, tg * 512:(tg + 1) * 512]
                        ev = P.op("vector", lambda e, hv=hv, ps=ps: e.tensor_tensor(out=hv, in0=ps, in1=hv, op=ALU.add), waits=[tS] + list(h_waits))
                        PSA.release(b, [ev])
                        toks.append(ev)
                        last = tS
                WS.release(bi, [last])
            return toks

        def mlp_phase(layer, hT, nT):
            aT = QC
            rt = [vf(B + 108544, 512), vf(B + 110592, 512)]
            rt_war = [[], []]
            ri = 0
            h_toks = []
            a_war = bar()
            for kg in range(4):
                a_toks = []
                for j in range(4):
                    blk, btok, bi = WS.pop("up%d_%d" % (layer, kg * 4 + j))
                    last = None
                    for tg in range(2):
                        for c in range(4):
                            b, ps, tS = fpat(blk, btok, c, nT, tg * 512, 512, nwaits(tg * 512, 512))
                            s = ri % 2
                            ri += 1
                            rv = rt[s]
                            t1 = P.op("scalar", lambda e, rv=rv, ps=ps: e.activation(out=rv, in_=ps, func=AF.Square), waits=[tS] + rt_war[s])
                            av = aT[:, j * 4 + c, tg * 512:(tg + 1) * 512]
                            t2 = P.op("vector", lambda e, rv=rv, ps=ps, av=av: e.scalar_tensor_tensor(out=av, in0=ps, scalar=0.0, in1=rv, op0=ALU.is_gt, op1=ALU.mult),
                                      waits=[t1] + a_war)
                            rt_war[s] = [t2]
                            PSA.release(b, [t2])
                            a_toks.append(t2)
                            last = tS
                    WS.release(bi, [last])
                lastdn = None
                for cb in range(4):
                    blk, btok, bi = WS.pop("dn%d_%d_%d" % (layer, kg, cb))
                    last = None
                    for c in range(4):
                        for tg in range(2):
                            b, ps, tS = fpat(blk, btok, c, aT, tg * 512, 512, a_toks)
                            hv = hT[:, cb * 4 + c, tg * 512:(tg + 1) * 512]
                            ev = P.op("vector", lambda e, hv=hv, ps=ps: e.tensor_tensor(out=hv, in0=ps, in1=hv, op=ALU.add), waits=[tS])
                            PSA.release(b, [ev])
                            h_toks.append(ev)
                            last = tS
                    WS.release(bi, [last])
                    lastdn = last
                a_war = [lastdn]
            return h_toks

        def emit_out(srcT_of, dst, waits_of, otile, ot_war):
            dts = []
            for tt in range(8):
                srcT = srcT_of(tt)
                s = tt % 2
                evs = []
                for kq in range(4):
                    b = PSA.alloc()
                    pb = bank(b)
                    tk = None
                    for i in range(4):
                        kc = kq * 4 + i
                        sv = srcT[:, kc, :]
                        tk = P.op("tensor", lambda e, pb=pb, i=i, sv=sv: e.transpose(out=pb[:, i * 128:(i + 1) * 128], in_=sv, identity=ident),
                                  waits=(list(waits_of(tt)) + [t_ident] + PSA.rel[b]) if i == 0 else (), sig=(i == 3))
                    dstv = otile[s][:, kq * 512:(kq + 1) * 512]
                    ev = evac_copy("scalar" if kq % 2 else "vector", dstv, pb, [tk] + ot_war[s])
                    PSA.release(b, [ev])
                    evs.append(ev)
                td = P.dma("sync", dst[tt * 128:(tt + 1) * 128, :], otile[s], "o%d" % s, waits=evs)
                ot_war[s] = [td]
                dts.append(td)
            return dts

        final_waits = []
        TN = {"g": []}

        def nwaits(t0, n):
            out = []
            for g in range(t0 // 256, (t0 + n - 1) // 256 + 1):
                out += TN["g"][g]
            return out

        def nall():
            out = []
            for g in TN["g"]:
                out += g
            return out

        xinA = [vf(B + 40960, 2048), vf(B + 49152, 2048)]
        if do1:
            xinA += [vf(O_QC + 16384, 2048), vf(O_QC + 24576, 2048)]
        xTa = vf(B + 57344, 16, 256)
        xw = [[] for _ in xinA]
        xTa_war = []

        def mem_norm():
            tl = load_T(mem_d, xTa, 256, xinA, xw, xTa_war)
            return norm_T(xTa, 256, 0, mT, tl, [])

        if not do1:
            t_mT = mem_norm()

        if do1:
            nT0 = vb(B, 16, NEXT)
            TN["g"] = []
            xTb = [xTa, vf(B + 73728, 16, 256)]
            xT_war = [[], []]
            srcs = [x_d[gi * 256:(gi + 1) * 256, :] for gi in range(5)] + [mem_d]
            loads = {}

            def do_load(i):
                loads[i] = load_T(srcs[i], xTb[i % 2], 256, xinA, xw, xT_war[i % 2])

            t_q = []
            qblk = {}
            qlast = {}

            def q_part(qi, tg, extra):
                if qi not in qblk:
                    qblk[qi] = WS.pop("aq%d" % qi if qi < 3 else "aqm")
                blk, btok, bi = qblk[qi]
                for c in range(4):
                    b, ps, tS = fpat(blk, btok, c, nT0, tg * 512, 512, nwaits(tg * 512, 512))
                    ev = evac_copy("scalar", QC[:, qi * 4 + c, tg * 512:(tg + 1) * 512], ps, [tS] + extra)
                    PSA.release(b, [ev])
                    t_q.append(ev)
                    qlast[qi] = tS
                if tg == 1:
                    WS.release(bi, [qlast[qi]])

            do_load(0)
            for i in range(6):
                if i + 1 < 6:
                    do_load(i + 1)
                if i < 5:
                    tn = norm_T(xTb[i % 2], 256, 16, nT0[:, :, i * 256:(i + 1) * 256], loads[i], [])
                    TN["g"].append(tn)
                else:
                    tn = norm_T(xTb[i % 2], 256, 0, mT, loads[i], [])
                    t_mT = tn
                xT_war[i % 2] = tn
                if i == 1:
                    q_part(0, 0, [])
                if i == 3:
                    q_part(0, 1, [])
            x_all = []
            for i in range(6):
                x_all += loads[i]
            for qi in range(1, 4):
                for tg in range(2):
                    q_part(qi, tg, x_all if qi >= 2 else [])
            t_kvm = mem_kv(0, t_mT, [])
            KTg = vb(B + 40960, 4, NEXT)
            Vg = vb(B + 51200, 10, 512)
            biasb = [vf(B + 61440, 1664), vf(B + 68096, 1664)]
            pT_flat = vb(B + 74752, 2048)
            pT = [pT_flat[:, i * 512:(i + 1) * 512] for i in range(4)]
            stmp = [vf(B + 78848, 640), vf(B + 81408, 640), vf(B + 94208, 640)]
            pT_na = vb(B + 88064, 3072)
            rl = [vf(B + 83968, 512), vf(B + 86016, 512)]
            kv_war = bar()
            bias_war = [list(kv_war), list(kv_war)]
            t_cat = mem_attn(t_q, t_kvm, pT, rl)
            for g in range(3):
                blk, btok, bi = WS.pop("ak%d" % g)
                t_k = []
                last = None
                for c in range(4):
                    for (t0, n) in ((0, 512), (512, 512), (1024, 256)):
                        b, ps, tS = fpat(blk, btok, c, nT0, t0, n, nwaits(t0, n))
                        ev = evac_copy("scalar", KTg[:, c, t0:t0 + n], ps, [tS] + kv_war)
                        PSA.release(b, [ev])
                        t_k.append(ev)
                        last = tS
                WS.release(bi, [last])
                blk, btok, bi = WS.pop("av%d" % g)
                for t in range(10):
                    b, ps, tS = tpat(blk, btok, nT0, t * 128, nwaits(t * 128, 128))
                    ev = evac_copy("vector" if t % 2 else "scalar", Vg[:, t, :], ps, [tS] + kv_war)
                    PSA.release(b, [ev])
                    t_k.append(ev)
                    last = tS
                WS.release(bi, [last])
                lastatt = None
                pend = []
                mul_q = []
                for hh in range(4):
                    h = g * 4 + hh
                    s = h % 2
                    t_b = P.dma("sync", biasb[s], bias_d[h], "bias%d" % s, waits=bias_war[s])
                    for m in range(8):
                        if m < 2:
                            tl_, boff = [0, 1, 2, 3], m * 512
                        else:
                            tl_, boff = list(range(m - 2, m + 3)), 1024
                        nt = len(tl_)
                        tiles = [(KTg[:, hh, t * 128:(t + 1) * 128], Vg[:, t, hh * 128:(hh + 1) * 128]) for t in tl_]
                        QT = QC[:, h, m * 128:(m + 1) * 128]
                        if len(pend) >= 3:
                            mul_q.append(pend.pop(0)())
                        if len(mul_q) >= 2:
                            lastatt = mul_q.pop(0)()
                            t_cat.append(lastatt)
                        pvs = attn_unit(QT, tiles, QT, 128, t_q, t_k, pT, rl, stmp=stmp,
                                        bias=biasb[s][:, boff:boff + nt * 128], bias_waits=[t_b])
                        pend.append(pvs)
                    bias_war[s] = [P.last("vector")]
                while pend:
                    mul_q.append(pend.pop(0)())
                while mul_q:
                    lastatt = mul_q.pop(0)()
                    t_cat.append(lastatt)
                kv_war = [P.last("tensor"), lastatt]
            hT = vf(B, 16, NTOK)
            xinC = [vf(B + 65536, 2048), vf(B + 73728, 2048)]
            bw = bar()
            xw = [list(bw), list(bw)]
            t_h = []
            for gi in range(4):
                t_h += load_T(x_d[gi * 256:(gi + 1) * 256, :], hT[:, :, gi * 256:(gi + 1) * 256], 256, xinC, xw, bw)
            t_h = wo_phase(0, hT, t_cat, t_h)
            nT = vb(B + 65536, 16, NTOK)
            nw = bar()
            TN["g"] = []
            for gi in range(4):
                TN["g"].append(norm_T(hT[:, :, gi * 256:(gi + 1) * 256], 256, 32, nT[:, :, gi * 256:(gi + 1) * 256], t_h, nw))
            if mode == "fused":
                t_kvm1 = mem_kv(1, t_mT, nw)
            t_h = mlp_phase(0, hT, nT)
            nw = bar()
            TN["g"] = []
            for gi in range(4):
                TN["g"].append(norm_T(hT[:, :, gi * 256:(gi + 1) * 256], 256, 48, nT[:, :, gi * 256:(gi + 1) * 256], t_h, nw))

        if mode == "s2":
            hT = vf(B, 16, NTOK)
            nT = vb(B + 65536, 16, NTOK)
            xinC = [vf(B + 65536, 2048), vf(B + 73728, 2048)]
            bw = bar()
            xw = [list(bw), list(bw)]
            t_h = []
            for gi in range(4):
                t_h += load_T(h1_i[gi * 256:(gi + 1) * 256, :], hT[:, :, gi * 256:(gi + 1) * 256], 256, xinC, xw, bw)
            nw = bar()
            TN["g"] = []
            for gi in range(4):
                TN["g"].append(norm_T(hT[:, :, gi * 256:(gi + 1) * 256], 256, 48, nT[:, :, gi * 256:(gi + 1) * 256], t_h, nw))

        cosT = vf(B + 108544, NTOK)
        sinT = vf(B + 112640, NTOK)
        sqq = vb(B + 116736, 512)
        rstq = vf(B + 117760, 512)
        tq = vf(B + 119808, 512)
        qh = vb(B + 121856, 512)
        t1b = vf(B + 122880, 512)
        t2b = vf(B + 124928, 512)
        qk = {"sqq": [], "rstq": [], "tq": [], "qh": [], "t1": [], "t2": []}
        bw = bar()
        t_cos = P.dma("sync", cosT, cos_d, "c_cos", waits=bw)
        t_sin = P.dma("sync", sinT, sin_d, "c_sin", waits=bw)

        def qk_post(b, ps, tS, gcol, t0, dst, dst_waits, done):
            t1 = P.op("scalar", lambda e: e.activation(out=sqq, in_=ps, func=AF.Square), waits=[tS] + qk["sqq"])
            res = {}

            def stage2():
                b2 = PSA.alloc()
                ps2 = bank(b2)
                t2 = mm(ps2, [(ones, sqq)], [t1, t_ones] + PSA.rel[b2])
                qk["sqq"] = [t2]
                t3 = P.op("scalar", lambda e: e.activation(out=tq, in_=ps2, func=AF.Ln, bias=epsT, scale=1.0 / 128), waits=[t2, t_eps] + qk["tq"])
                PSA.release(b2, [t3])
                t4 = P.op("scalar", lambda e: e.activation(out=rstq, in_=tq, func=AF.Exp, scale=-0.5), waits=[t3] + qk["rstq"])
                qk["tq"] = [t4]
                t5 = P.op("vector", lambda e: e.scalar_tensor_tensor(out=qh, in0=ps, scalar=gains[:, gcol:gcol + 1], in1=rstq, op0=ALU.mult, op1=ALU.mult),
                          waits=[t4, t_gains] + qk["qh"])
                PSA.release(b, [t5])
                qk["rstq"] = [t5]
                res["t5"] = t5

            def stage3():
                t5 = res["t5"]
                b3 = PSA.alloc()
                ps3 = bank(b3)
                t6 = mm(ps3, [(perm, qh)], [t5, t_perm] + PSA.rel[b3])
                cv = cosT[:, t0:t0 + 512]
                sv = sinT[:, t0:t0 + 512]
                t7 = P.op("vector", lambda e: e.tensor_tensor(out=t1b, in0=qh, in1=cv, op=ALU.mult), waits=[t5, t_cos] + qk["t1"])
                t8 = P.op("vector", lambda e: e.tensor_tensor(out=t2b, in0=ps3, in1=sv, op=ALU.mult), waits=[t6, t_sin] + qk["t2"])
                PSA.release(b3, [t8])
                t9 = P.op("vector", lambda e: e.tensor_tensor(out=dst, in0=t1b, in1=t2b, op=ALU.add), waits=[t7, t8] + list(dst_waits))
                qk["qh"] = [t6, t7]
                qk["t1"] = [t9]
                qk["t2"] = [t9]
                done(t9)

            defer(4, stage2)
            defer(20, stage3)

        if do1:
            kst = [vb(B + 98304, 1024), vb(B + 100352, 1024)]
            vst = [vb(B + 102400, 512), vb(B + 103424, 512)]
            kst_war = [nall(), nall()]
            vst_war = [nall(), nall()]
            kv_dmas = []
            blk, btok, bi = WS.pop("bv")
            last = None
            for t in range(8):
                s = t % 2
                b, ps, tS = tpat(blk, btok, nT, t * 128, nwaits(t * 128, 128))
                ev = evac_copy("scalar", vst[s], ps, [tS] + vst_war[s])
                PSA.release(b, [ev])
                td = P.dma("sync", kv_own[:, 4096 + t * 512: 4096 + (t + 1) * 512], vst[s], "kvo_v%d" % s, waits=[ev])
                vst_war[s] = [td]
                kv_dmas.append(td)
                last = tS
            WS.release(bi, [last])
            blk, btok, bi = WS.pop("bk")
            last = None
            for c in range(4):
                s = c % 2
                t9s = []

                def kdone(t9, c=c, s=s, t9s=t9s):
                    t9s.append(t9)
                    if len(t9s) == 2:
                        td = P.dma("sync", kv_own[:, c * 1024:(c + 1) * 1024], kst[s], "kvo_k%d" % s, waits=t9s)
                        kst_war[s] = [td]
                        kv_dmas.append(td)

                if c >= 2:
                    pe_flush()
                for tg in range(2):
                    b, ps, tS = fpat(blk, btok, c, nT, tg * 512, 512, nall())
                    qk_post(b, ps, tS, 97, tg * 512, kst[s][:, tg * 512:(tg + 1) * 512], list(kst_war[s]), kdone)
                    last = tS
            WS.release(bi, [last])
            pe_flush()
            final_waits += kv_dmas

        if mode == "s1":
            otile = [vf(B + 81920, 2048), vf(B + 90112, 2048)]
            bw = bar()
            ow = [list(bw), list(bw)]
            final_waits += emit_out(lambda tt: hT[:, :, tt * 128:(tt + 1) * 128], h1_o, lambda tt: t_h, otile, ow)

        kvfull_waits = []

        if do2:
            t_kvm = t_kvm1 if mode == "fused" else mem_kv(1, t_mT, bar())
            t_q = []
            q_war = bar()
            for qi in range(4):
                blk, btok, bi = WS.pop("bq%d" % qi if qi < 3 else "bqm")
                if qi == 0 and mode == "fused":
                    t_cc = P.custom("gpsimd",
                                    lambda e: e.collective_compute("AllGather", ALU.bypass, replica_groups=[[0, 1], [2, 3], [4, 5], [6, 7]],
                                                                   ins=[kv_own.opt()], outs=[kv_full.opt()]),
                                    "cc", waits=kv_dmas, inc=1)
                    kvfull_waits.append(t_cc)
                last = None
                for c in range(4):
                    for tg in range(2):
                        b, ps, tS = fpat(blk, btok, c, nT, tg * 512, 512, nall())
                        dst = QC[:, qi * 4 + c, tg * 512:(tg + 1) * 512]
                        if qi < 3:
                            qk_post(b, ps, tS, 96, tg * 512, dst, q_war, t_q.append)
                        else:
                            ev = evac_copy("scalar", dst, ps, [tS] + q_war)
                            PSA.release(b, [ev])
                            t_q.append(ev)
                        last = tS
                WS.release(bi, [last])
            pe_flush()
            KTf = vb(B + 65536, 4, 2048)
            Vf = vb(B + 81920, 16, 512)
            pT_flat = vb(B + 98304, 4096)
            pT = [pT_flat[:, i * 512:(i + 1) * 512] for i in range(8)]
            rl = [vf(B + 106496, 512), vf(B + 108544, 512)]
            bw = bar()
            if mode == "fused":
                bw = bw + kv_dmas
            att["NP"] = 4
            att["pT_war"] = [list(bw) for _ in range(8)]
            att["rl_war"] = [list(bw), list(bw)]
            t_kv = []
            for r in range(2):
                t_kv.append(P.dma("sync", KTf[:, :, r * 1024:(r + 1) * 1024],
                                  kv_full[r * 128:(r + 1) * 128, 0:4096].rearrange("p (h t) -> p h t", t=1024), "kvl", waits=bw + kvfull_waits))
                t_kv.append(P.dma("sync", Vf[:, r * 8:(r + 1) * 8, :],
                                  kv_full[r * 128:(r + 1) * 128, 4096:8192].rearrange("p (t n) -> p t n", n=512), "kvl", waits=bw + kvfull_waits))
            t_cat = mem_attn(t_q, t_kvm, pT, rl)
            for h in range(12):
                kvh = h // 3
                for tg in range(2):
                    QT = QC[:, h, tg * 512:(tg + 1) * 512]
                    tiles = [(KTf[:, kvh, kt * 128:(kt + 1) * 128], Vf[:, kt, kvh * 128:(kvh + 1) * 128]) for kt in range(16)]
                    t_cat.append(attn_unit(QT, tiles, QT, 512, t_q, t_kv, pT, rl))
            t_h = wo_phase(1, hT, t_cat, t_h)
            nw = bar()
            TN["g"] = []
            for gi in range(4):
                TN["g"].append(norm_T(hT[:, :, gi * 256:(gi + 1) * 256], 256, 64, nT[:, :, gi * 256:(gi + 1) * 256], t_h, nw))
            t_h = mlp_phase(1, hT, nT)
            yTs = [vf(B + 65536, 16, 256), vf(B + 81920, 16, 256)]
            otile = [vf(O_QC, 2048), vf(O_QC + 8192, 2048), vf(O_QC + 16384, 2048), vf(O_QC + 24576, 2048)]
            bw = bar()
            ow = [list(bw) for _ in range(4)]
            y_wars = [list(bw), list(bw)]
            tys = {}

            def fin_norm(gi):
                tys[gi] = norm_T(hT[:, :, gi * 256:(gi + 1) * 256], 256, 80, yTs[gi % 2], t_h, y_wars[gi % 2])

            fin_norm(0)
            for gi in range(4):
                yT = yTs[gi % 2]
                if gi + 1 < 4 and gi >= 1:
                    pass
                if gi + 1 < 4 and gi == 0:
                    fin_norm(1)
                ty = tys[gi]
                dts = []
                for tt in range(2):
                    srcT = yT[:, :, tt * 128:(tt + 1) * 128]
                    s = (gi * 2 + tt) % 4
                    evs = []
                    lastk = None
                    for kq in range(4):
                        b = PSA.alloc()
                        pb = bank(b)
                        tk = None
                        for i in range(4):
                            kc = kq * 4 + i
                            sv = srcT[:, kc, :]
                            tk = P.op("tensor", lambda e, pb=pb, i=i, sv=sv: e.transpose(out=pb[:, i * 128:(i + 1) * 128], in_=sv, identity=ident),
                                      waits=(list(ty) + [t_ident] + PSA.rel[b]) if i == 0 else (), sig=(i == 3))
                        dstv = otile[s][:, kq * 512:(kq + 1) * 512]
                        ev = evac_copy("scalar" if kq % 2 else "vector", dstv, pb, [tk] + ow[s])
                        PSA.release(b, [ev])
                        evs.append(ev)
                        lastk = tk
                    row = gi * 256 + tt * 128
                    td = P.dma("sync", out_d[row:row + 128, :], otile[s], "o%d" % s, waits=evs)
                    ow[s] = [td]
                    final_waits.append(td)
                y_wars[gi % 2] = [lastk]
                if gi + 2 < 4:
                    fin_norm(gi + 2)

        P.wait_only("sync", final_waits)
        P.replay()
    return nc


def _true_row(l, hf):
    return l if hf == 0 else 31 - l


def _bias_tables(rpb, hf):
    units = [(0, [0, 1, 2, 3]), (1, [0, 1, 2, 3]), (2, [0, 1, 2, 3, 4])]
    p = np.arange(128)
    ki, kc = p // 64, p % 64
    qi, qc = p // 64, p % 64
    cols = []
    for m, tl in units:
        for t in tl:
            kr = np.array([_true_row(2 * t + a, hf) for a in ki])[:, None]
            qr = np.array([_true_row(2 * m + a, hf) for a in qi])[None, :]
            r0 = np.clip(qr - 4, 0, 24)
            vr = (kr >= r0) & (kr < r0 + 8)
            c0 = np.clip(qc - 8, 0, 48)[None, :]
            vc = (kc[:, None] >= c0) & (kc[:, None] < c0 + 16)
            dr = np.clip(kr - qr + 7, 0, 14)
            dc = np.clip(kc[:, None] - qc[None, :] + 15, 0, 30)
            valid = vr & vc
            g = rpb[:, dr, dc]
            cols.append(np.where(valid[None], g, np.float32(MASKV)).astype(np.float32))
    return np.ascontiguousarray(np.concatenate(cols, axis=2))


def _rope_tables(hf):
    t = np.arange(NTOK)
    row = np.array([_true_row(l, hf) for l in (t // 64)], dtype=np.float32)
    col = (t % 64).astype(np.float32)
    inv = np.power(np.float32(10000.0), -np.arange(0, 64, 2, dtype=np.float32) / np.float32(64)).astype(np.float32)
    d = np.arange(128)
    f = d % 32
    pos = np.where((d < 64)[:, None], row[None, :], col[None, :]).astype(np.float32)
    ang = (pos * inv[f][:, None]).astype(np.float32)
    cosT = np.cos(ang).astype(np.float32)
    sgn = np.where((d % 64) < 32, -1.0, 1.0).astype(np.float32)[:, None]
    sinT = (np.sin(ang).astype(np.float32) * sgn).astype(np.float32)
    return np.ascontiguousarray(cosT), np.ascontiguousarray(sinT)


def _fm(vec):
    return np.asarray(vec, dtype=np.float32).reshape(-1, 128).T


_CACHE = {}


def _get_nc(mode):
    if mode not in _CACHE:
        _CACHE[mode] = build(mode)
    return _CACHE[mode]


def kernel(x, mem, mem_norm, attn_norm, mlp_norm, a_w_in, a_rpb, b_w_in, b_q_norm, b_k_norm,
           w_mem_kv, w_o, w_up, w_down, final_norm, _mode="fused"):
    x = np.asarray(x, dtype=np.float32)
    mem = np.asarray(mem, dtype=np.float32)
    gains = np.concatenate([_fm(mem_norm), _fm(attn_norm[0]), _fm(mlp_norm[0]), _fm(attn_norm[1]), _fm(mlp_norm[1]),
                            _fm(final_norm), np.asarray(b_q_norm[0], np.float32)[:, None], np.asarray(b_k_norm[0], np.float32)[:, None]], axis=1)
    gains = np.ascontiguousarray(gains.astype(np.float32))
    d = np.arange(128)
    partner = np.where((d % 64) < 32, d + 32, d - 32)
    perm = np.zeros((128, 128), np.float32)
    perm[partner, d] = 1.0
    ident = np.eye(128, dtype=np.float32)
    rpb = np.asarray(a_rpb[0], np.float32)
    bias = [_bias_tables(rpb, 0), _bias_tables(rpb, 1)]
    rope = [_rope_tables(0), _rope_tables(1)]
    common = {
        "ident": ident, "gains": gains, "perm": perm,
        "b_w_in": np.ascontiguousarray(np.asarray(b_w_in[0], np.float32)),
        "w_mem_kv": np.asarray(w_mem_kv, np.float32).reshape(2 * D, 1024),
        "w_o": np.asarray(w_o, np.float32).reshape(2 * D, D),
        "w_up": np.asarray(w_up, np.float32).reshape(2 * D, 8192),
        "w_down": np.asarray(w_down, np.float32).reshape(2 * 8192, D),
    }
    a_in = np.ascontiguousarray(np.asarray(a_w_in[0], np.float32))
    maps1 = []
    for c in range(8):
        b, hf = c // 2, c % 2
        xb = x[b].reshape(32, 64, D)
        if hf:
            xb = xb[::-1]
        m = dict(common)
        m["x_ext"] = np.ascontiguousarray(xb[:20].reshape(NEXT, D))
        m["mem_b"] = np.ascontiguousarray(mem[b])
        m["bias0"] = bias[hf]
        m["a_w_in"] = a_in
        m["cosT"], m["sinT"] = rope[hf]
        maps1.append(m)
    if _mode == "fused":
        res = run_bass_kernel_spmd(_get_nc("fused"), maps1, core_ids=list(range(8)))
        outs = [r["out"] for r in res.results]
    else:
        res1 = run_bass_kernel_spmd(_get_nc("s1"), maps1, core_ids=list(range(8)))
        maps2 = []
        for c in range(8):
            b, hf = c // 2, c % 2
            m = dict(common)
            m["mem_b"] = maps1[c]["mem_b"]
            m["cosT"], m["sinT"] = rope[hf]
            m["h1"] = np.asarray(res1.results[c]["h1"])
            own = np.asarray(res1.results[c]["kv_own"])
            oth = np.asarray(res1.results[c ^ 1]["kv_own"])
            pair = [own, oth] if hf == 0 else [oth, own]
            m["kv_full"] = np.ascontiguousarray(np.concatenate(pair, axis=0))
            maps2.append(m)
        res2 = run_bass_kernel_spmd(_get_nc("s2"), maps2, core_ids=list(range(8)))
        outs = [r["out"] for r in res2.results]
    out = np.empty((4, 2048, D), np.float32)
    for c in range(8):
        b, hf = c // 2, c % 2
        ob = np.asarray(outs[c], np.float32).reshape(16, 64, D)
        if hf:
            ob = ob[::-1]
        out[b, hf * 1024:(hf + 1) * 1024] = ob.reshape(NTOK, D)
    return out
```

```python
import contextlib
import numpy as np
import ml_dtypes
import concourse.bass as bass
import concourse.mybir as mybir
from concourse.bass_utils import run_bass_kernel_spmd

F32 = mybir.dt.float32
BF16 = mybir.dt.bfloat16
ALU = mybir.AluOpType
AF = mybir.ActivationFunctionType

D = 2048
NTOK = 1024
NEXT = 1280
EPS = 1e-6
SCALE = 128 ** -0.5
MASKV = -30000.0

O_IDENT, O_GAINS, O_ONES, O_PERM, O_EPS = 0, 512, 1024, 1280, 1536
O_MT = 4096
O_KMT, O_VM = 12288, 14336
O_RING = 16384
O_QC = 49152
O_BIG = 81920
TOT = 208896


class Prog:
    ENGS = ("sync", "scalar", "vector", "gpsimd", "tensor")

    def __init__(self, nc, stack):
        self.nc = nc
        self.stack = stack
        self.ops = {e: [] for e in self.ENGS}
        self.sem = {}
        self.cnt = {}

    def _sem(self, key):
        if key not in self.sem:
            self.sem[key] = self.stack.enter_context(self.nc.semaphore(key))
            self.cnt[key] = 0
        return self.sem[key]

    def op(self, eng, fn, waits=(), sig=True):
        tok = None
        if sig:
            key = "e_" + eng
            self._sem(key)
            self.cnt[key] += 1
            tok = (key, self.cnt[key])
        self.ops[eng].append((fn, tuple(w for w in waits if w is not None), tok, 1))
        return tok

    def last(self, eng):
        key = "e_" + eng
        if key in self.cnt and self.cnt[key] > 0:
            return (key, self.cnt[key])
        return None

    def dma(self, eng, out, in_, semkey, waits=()):
        return self.custom(eng, lambda e, out=out, in_=in_: e.dma_start(out=out, in_=in_), semkey, waits)

    def custom(self, eng, fn, semkey, waits=(), inc=16):
        self._sem(semkey)
        self.cnt[semkey] += inc
        tok = (semkey, self.cnt[semkey])
        self.ops[eng].append((fn, tuple(w for w in waits if w is not None), tok, inc))
        return tok

    def wait_only(self, eng, waits):
        self.ops[eng].append((None, tuple(w for w in waits if w is not None), None, 0))

    def replay(self):
        with self.nc.Block() as block:
            for eng in self.ENGS:
                ops = self.ops[eng]
                if not ops:
                    continue

                def body(e, ops=ops):
                    seen = {}
                    for fn, waits, tok, inc in ops:
                        need = {}
                        for (k, v) in waits:
                            if seen.get(k, 0) < v:
                                need[k] = max(need.get(k, 0), v)
                        for k, v in need.items():
                            e.wait_ge(self.sem[k], v)
                            seen[k] = v
                        if fn is not None:
                            inst = fn(e)
                            if tok is not None:
                                inst.then_inc(self.sem[tok[0]], inc)

                getattr(block, eng)(body)


def build(mode):
    nc = bass.Bass("TRN2", target_bir_lowering=False)

    def din(name, shape, dt=F32):
        return nc.dram_tensor(name, shape, dt, kind="ExternalInput").ap()

    def dout(name, shape, dt=F32):
        return nc.dram_tensor(name, shape, dt, kind="ExternalOutput").ap()

    do1 = mode in ("s1", "fused")
    do2 = mode in ("s2", "fused")
    ident_d = din("ident", [128, 128])
    gains_d = din("gains", [128, 98])
    if do1:
        x_d = din("x_ext", [NEXT, D])
        bias_d = din("bias0", [12, 128, 1664])
        a_in = din("a_w_in", [D, 5120])
    mem_d = din("mem_b", [256, D])
    b_in = din("b_w_in", [D, 3072])
    perm_d = din("perm", [128, 128])
    cos_d = din("cosT", [128, NTOK])
    sin_d = din("sinT", [128, NTOK])
    wkv = din("w_mem_kv", [2 * D, 1024])
    wo = din("w_o", [2 * D, D])
    wup = din("w_up", [2 * D, 8192])
    wdn = din("w_down", [2 * 8192, D])
    if mode == "s1":
        h1_o = dout("h1", [NTOK, D])
        kv_own = dout("kv_own", [128, 8192], BF16)
    if mode == "s2":
        h1_i = din("h1", [NTOK, D])
        kv_full = din("kv_full", [256, 8192], BF16)
    if mode == "fused":
        kv_own = nc.dram_tensor("kv_own", [128, 8192], BF16, kind="Internal").ap()
        kv_full = nc.dram_tensor("kv_full", [256, 8192], BF16, kind="Internal").ap()
    if do2:
        out_d = dout("out", [NTOK, D])

    st = contextlib.ExitStack()
    with st:
        P = Prog(nc, st)
        arena = st.enter_context(nc.sbuf_tensor("arena", [128, TOT // 4], F32))
        abf = arena.bitcast(BF16)
        psum = st.enter_context(nc.psum_tensor("ps", [128, 4096], F32))

        def shp(ap, shape):
            if len(shape) == 1:
                return ap
            if len(shape) == 2:
                return ap.rearrange("p (a b) -> p a b", b=shape[1])
            return ap.rearrange("p (a b c) -> p a b c", b=shape[1], c=shape[2])

        def vf(off, *shape):
            n = int(np.prod(shape))
            return shp(arena[:, off // 4: off // 4 + n], shape)

        def vb(off, *shape):
            n = int(np.prod(shape))
            return shp(abf[:, off // 2: off // 2 + n], shape)

        ident = vf(O_IDENT, 128)
        gains = vf(O_GAINS, 98)
        ones = vb(O_ONES, 128)
        perm = vb(O_PERM, 128)
        epsT = vf(O_EPS, 1)
        scr = vf(2048, 512)
        mT = vb(O_MT, 16, 256)
        kmT = vb(O_KMT, 4, 256)
        vm = vb(O_VM, 2, 512)
        ring = [vb(O_RING + i * 16384, 16, 512) for i in range(2)]
        QC = vb(O_QC, 16, 1024)
        B = O_BIG

        class PSA:
            open = [False] * 8
            rel = [[] for _ in range(8)]
            relseq = list(range(8))
            seq = 8

            @classmethod
            def alloc(c):
                free = [b for b in range(8) if not c.open[b]]
                if not free:
                    raise RuntimeError("psum full")
                b = min(free, key=lambda x: c.relseq[x])
                c.open[b] = True
                return b

            @classmethod
            def alloc2(c):
                free = [b for b in range(0, 8, 2) if not c.open[b] and not c.open[b + 1]]
                if not free:
                    raise RuntimeError("psum full2")
                b = min(free, key=lambda x: max(c.relseq[x], c.relseq[x + 1]))
                c.open[b] = c.open[b + 1] = True
                return b

            @classmethod
            def release(c, b, toks):
                c.open[b] = False
                c.rel[b] = [t for t in toks if t is not None]
                c.relseq[b] = c.seq
                c.seq += 1

        def bank(b, n=512):
            return psum[:, b * 512: b * 512 + n]

        PEQ = []
        peq_busy = [False]

        def defer(n, fn):
            PEQ.append([n, fn])

        def pe_tick():
            if peq_busy[0]:
                return
            peq_busy[0] = True
            for ent in PEQ:
                ent[0] -= 1
            while PEQ and PEQ[0][0] <= 0:
                PEQ.pop(0)[1]()
            peq_busy[0] = False

        def pe_flush():
            peq_busy[0] = True
            while PEQ:
                PEQ.pop(0)[1]()
            peq_busy[0] = False

        def mm(out, pairs, waits):
            n = len(pairs)
            tok = None
            for i, (l, r) in enumerate(pairs):
                tok = P.op("tensor",
                           lambda e, l=l, r=r, i=i, out=out: e.matmul(out, lhsT=l, rhs=r, start=(i == 0), stop=(i == n - 1)),
                           waits=waits if i == 0 else (), sig=(i == n - 1))
                if i < n - 1:
                    pe_tick()
            return tok

        def bar():
            return [P.last(e) for e in ("tensor", "scalar", "vector")]

        def wblock(W, r0, c0):
            return W[r0:r0 + 2048, c0:c0 + 512].rearrange("(k p) n -> p k n", p=128)

        plan = []
        if do1:
            plan += [("aq%d" % i, a_in, 0, i * 512) for i in range(3)] + [("aqm", a_in, 0, 4608)]
            plan += [("kv0k", wkv, 0, 0), ("kv0v", wkv, 0, 512)]
            for g in range(3):
                plan += [("ak%d" % g, a_in, 0, 1536 + g * 512), ("av%d" % g, a_in, 0, 3072 + g * 512)]
            plan += [("wo0_%d" % i, wo, 0, i * 512) for i in range(4)]
            if mode == "fused":
                plan += [("kv1k", wkv, D, 0), ("kv1v", wkv, D, 512)]
            for kg in range(4):
                plan += [("up0_%d" % (kg * 4 + j), wup, 0, (kg * 4 + j) * 512) for j in range(4)]
                plan += [("dn0_%d_%d" % (kg, cb), wdn, kg * 2048, cb * 512) for cb in range(4)]
            plan += [("bv", b_in, 0, 2048), ("bk", b_in, 0, 1536)]
        if do2:
            if mode != "fused":
                plan += [("kv1k", wkv, D, 0), ("kv1v", wkv, D, 512)]
            plan += [("bq%d" % i, b_in, 0, i * 512) for i in range(3)] + [("bqm", b_in, 0, 2560)]
            plan += [("wo1_%d" % i, wo, D, i * 512) for i in range(4)]
            for kg in range(4):
                plan += [("up1_%d" % (kg * 4 + j), wup, D, (kg * 4 + j) * 512) for j in range(4)]
                plan += [("dn1_%d_%d" % (kg, cb), wdn, 8192 + kg * 2048, cb * 512) for cb in range(4)]

        class WS:
            nxt = 0
            cur = 0
            rel = {}
            loaded = {}

            @classmethod
            def pop(c, name):
                i = c.cur
                assert plan[i][0] == name, (plan[i][0], name)
                while c.nxt < len(plan) and c.nxt <= i + 1:
                    j = c.nxt
                    w = c.rel.get(j - 2, [])
                    assert j < 2 or (j - 2) in c.rel
                    if j < 2:
                        w = list(state["xdma"][:4])
                    _, W, r0, c0 = plan[j]
                    c.loaded[j] = P.dma("gpsimd", ring[j % 2], wblock(W, r0, c0), "w%d" % (j % 2), waits=w)
                    c.nxt += 1
                c.cur += 1
                return ring[i % 2], c.loaded[i], i

            @classmethod
            def release(c, i, toks):
                c.rel[i] = [t for t in toks if t is not None]

        t_ident = P.dma("sync", ident, ident_d, "c_id")
        t_gains = P.dma("sync", gains, gains_d, "c_g")
        t_perm = P.dma("gpsimd", perm, perm_d, "cstp")
        t_ones = P.op("vector", lambda e: e.memset(ones, 1.0))
        t_eps = P.op("vector", lambda e: e.memset(epsT, EPS))
        cst = [t_ident, t_gains, t_perm, t_ones, t_eps]

        sq = vb(B + 98304, 16, 256)
        rstd = vf(B + 106496, 256)
        tmpn = vf(B + 107520, 256)
        state = {"sq": [], "rstd": [], "tmpn": [], "xin_i": 0, "ev_i": 0, "xdma": []}

        def load_T(src, dstT, T, xin, xin_war, waits):
            toks = []
            for tt in range(T // 128):
                s = state["xin_i"] % len(xin)
                state["xin_i"] += 1
                tX = P.dma("sync", xin[s], src[tt * 128:(tt + 1) * 128, :], "x%d" % s, waits=xin_war[s])
                state["xdma"].append(tX)
                lastk = None
                for kq in range(4):
                    b = PSA.alloc()
                    pb = bank(b)
                    for i in range(4):
                        kc = kq * 4 + i
                        tk = P.op("tensor",
                                  lambda e, pb=pb, i=i, s=s, kc=kc: e.transpose(out=pb[:, i * 128:(i + 1) * 128], in_=xin[s][:, kc * 128:(kc + 1) * 128], identity=ident),
                                  waits=([tX, t_ident] + PSA.rel[b]) if i == 0 else (), sig=(i == 3))
                    state["ev_i"] += 1
                    dst = dstT[:, kq * 4:(kq + 1) * 4, tt * 128:(tt + 1) * 128]
                    src_ps = pb.rearrange("p (a b) -> p a b", b=128)
                    if state["ev_i"] % 2:
                        ev = P.op("vector", lambda e, dst=dst, src_ps=src_ps: e.tensor_copy(out=dst, in_=src_ps), waits=[tk] + list(waits))
                    else:
                        ev = P.op("scalar", lambda e, dst=dst, src_ps=src_ps: e.copy(out=dst, in_=src_ps), waits=[tk] + list(waits))
                    PSA.release(b, [ev])
                    toks.append(ev)
                    lastk = tk
                xin_war[s] = [lastk]
            return toks

        def norm_T(srcT, T, gcol, dst, src_waits, dst_waits, dmodel=2048):
            sqv = sq[:, :, :T]
            tsq = P.op("scalar", lambda e: e.activation(out=sqv, in_=srcT, func=AF.Square), waits=list(src_waits) + state["sq"])
            b = PSA.alloc()
            ps = bank(b, T)
            tss = mm(ps, [(ones, sq[:, kc, :T]) for kc in range(16)], [tsq, t_ones] + PSA.rel[b])
            state["sq"] = [tss]
            t1 = P.op("scalar", lambda e: e.activation(out=tmpn[:, :T], in_=ps, func=AF.Ln, bias=epsT, scale=1.0 / dmodel), waits=[tss, t_eps] + state["tmpn"])
            PSA.release(b, [t1])
            t2 = P.op("scalar", lambda e: e.activation(out=rstd[:, :T], in_=tmpn[:, :T], func=AF.Exp, scale=-0.5), waits=[t1] + state["rstd"])
            state["tmpn"] = [t2]
            toks = []
            for kc in range(16):
                toks.append(P.op("vector",
                                 lambda e, kc=kc: e.scalar_tensor_tensor(out=dst[:, kc, :], in0=srcT[:, kc, :], scalar=gains[:, gcol + kc:gcol + kc + 1], in1=rstd[:, :T], op0=ALU.mult, op1=ALU.mult),
                                 waits=[t2, t_gains] + (list(dst_waits) if kc == 0 else [])))
            state["rstd"] = [toks[-1]]
            return toks

        att = {"pT_war": [[] for _ in range(8)], "pT_i": 0, "NP": 2, "rl_war": [[], []], "rl_i": 0, "st_war": [[], [], []], "st_i": 0,
               "na_war": [[], [], []], "na_i": 0}

        def attn_unit(QT, tiles, out, NQ, q_waits, kv_waits, pT, rl, stmp=None, bias=None, bias_waits=(), split=False):
            nt = len(tiles)
            base_w = list(q_waits) + list(kv_waits) + [t_ones]
            ctx = {"tokPV": None}

            def open_acc():
                ctx["bO"] = PSA.alloc2()
                ctx["bL"] = ctx["bO"] + 1
                ctx["Oa"] = bank(ctx["bO"], NQ)
                ctx["La"] = bank(ctx["bL"], NQ)

            def issuePV(j, rhs, tP, slots):
                first = (j == 0)
                last = (j == nt - 1)
                Vj = tiles[j][1]
                Oa, La, bO, bL = ctx["Oa"], ctx["La"], ctx["bO"], ctx["bL"]
                P.op("tensor", lambda e, Vj=Vj, rhs=rhs: e.matmul(Oa, lhsT=Vj, rhs=rhs, start=first, stop=last),
                     waits=[tP] + (PSA.rel[bO] + PSA.rel[bL] if first else []), sig=False)
                ctx["tokPV"] = P.op("tensor", lambda e, rhs=rhs: e.matmul(La, lhsT=ones, rhs=rhs, start=first, stop=last), sig=True)
                for s_ in slots:
                    att["pT_war"][s_] = [ctx["tokPV"]]

            def finish_act():
                ri = att["rl_i"] % 2
                att["rl_i"] += 1
                ctx["ri"] = ri
                rv = rl[ri][:, :NQ]
                ctx["rv"] = rv
                tR0 = P.op("scalar", lambda e: e.activation(out=rv, in_=ctx["La"], func=AF.Ln), waits=[ctx["tokPV"]] + att["rl_war"][ri])
                ctx["tR0"] = tR0
                ctx["tR"] = P.op("scalar", lambda e: e.activation(out=rv, in_=rv, func=AF.Exp, scale=-1.0), waits=[tR0])

            def finish_mul():
                Oa, bO, bL, rv, ri = ctx["Oa"], ctx["bO"], ctx["bL"], ctx["rv"], ctx["ri"]
                tO = P.op("vector", lambda e: e.tensor_tensor(out=out, in0=Oa, in1=rv, op=ALU.mult), waits=[ctx["tR"]])
                PSA.release(bO, [tO])
                PSA.release(bL, [ctx["tR0"]])
                att["rl_war"][ri] = [tO]
                return tO

            def finish():
                finish_act()
                return finish_mul()

            if bias is None:
                if not split:
                    open_acc()
                assert NQ == 512 and nt % 2 == 0
                npairs = nt // 2
                q = []

                def issueS2(p):
                    b2 = PSA.alloc2()
                    tS = None
                    for k in range(2):
                        KTj = tiles[2 * p + k][0]
                        Sj = psum[:, (b2 + k) * 512:(b2 + k + 1) * 512]
                        tS = P.op("tensor", lambda e, KTj=KTj, Sj=Sj: e.matmul(Sj, lhsT=KTj, rhs=QT, start=True, stop=True),
                                  waits=(base_w + PSA.rel[b2] + PSA.rel[b2 + 1]) if k == 0 else (), sig=(k == 1))
                    half = att["pT_i"] % att["NP"]
                    att["pT_i"] += 1
                    src = psum[:, b2 * 512: b2 * 512 + 1024]
                    dst = pT_flat[:, half * 1024: half * 1024 + 1024]
                    tP = P.op("scalar", lambda e: e.activation(out=dst, in_=src, func=AF.Exp, scale=SCALE),
                              waits=[tS] + att["pT_war"][2 * half] + att["pT_war"][2 * half + 1])
                    PSA.release(b2, [tP])
                    PSA.release(b2 + 1, [tP])
                    q.append((tP, half))

                LOOKP = 2
                for p in range(min(LOOKP, npairs)):
                    issueS2(p)
                if split:
                    assert npairs == 1

                    def pv_only():
                        open_acc()
                        tP, half = q[0]
                        for k in range(2):
                            issuePV(k, pT_flat[:, half * 1024 + k * 512: half * 1024 + (k + 1) * 512], tP, [2 * half, 2 * half + 1])
                        return finish()

                    return pv_only
                for p in range(npairs):
                    tP, half = q[p]
                    for k in range(2):
                        issuePV(2 * p + k, pT_flat[:, half * 1024 + k * 512: half * 1024 + (k + 1) * 512], tP, [2 * half, 2 * half + 1])
                    if p + LOOKP < npairs:
                        issueS2(p + LOOKP)
                return finish()

            b2 = PSA.alloc2()
            S = psum[:, b2 * 512: b2 * 512 + nt * 128]
            tS = None
            for j in range(nt):
                KTj = tiles[j][0]
                Sj = psum[:, b2 * 512 + j * 128: b2 * 512 + (j + 1) * 128]
                tS = P.op("tensor", lambda e, KTj=KTj, Sj=Sj: e.matmul(Sj, lhsT=KTj, rhs=QT, start=True, stop=True),
                          waits=(base_w + PSA.rel[b2] + PSA.rel[b2 + 1]) if j == 0 else (), sig=(j == nt - 1))
            si = att["st_i"] % 3
            att["st_i"] += 1
            sv = stmp[si][:, :nt * 128]
            tB = P.op("vector", lambda e: e.scalar_tensor_tensor(out=sv, in0=S, scalar=SCALE, in1=bias, op0=ALU.mult, op1=ALU.add),
                      waits=[tS] + list(bias_waits) + att["st_war"][si])
            PSA.release(b2, [tB])
            PSA.release(b2 + 1, [tB])
            third = att["na_i"] % 3
            att["na_i"] += 1
            pfull = pT_na[:, third * 1024: third * 1024 + nt * 128]
            tP = P.op("scalar", lambda e: e.activation(out=pfull, in_=sv, func=AF.Exp),
                      waits=[tB] + att["na_war"][third])
            att["st_war"][si] = [tP]

            def pv_stage():
                open_acc()
                for j in range(nt):
                    issuePV(j, pT_na[:, third * 1024 + j * 128: third * 1024 + (j + 1) * 128], tP, [])
                att["na_war"][third] = [ctx["tokPV"]]
                finish_act()
                return finish_mul

            return pv_stage

        def evac_copy(eng, dst, ps, waits):
            if eng == "scalar":
                return P.op("scalar", lambda e: e.copy(out=dst, in_=ps), waits=waits)
            return P.op("vector", lambda e: e.tensor_copy(out=dst, in_=ps), waits=waits)

        def fpat(blk, btok, c, actT, t0, n, act_waits):
            b = PSA.alloc()
            ps = bank(b, n)
            tS = mm(ps, [(blk[:, kc, c * 128:(c + 1) * 128], actT[:, kc, t0:t0 + n]) for kc in range(16)],
                    [btok] + list(act_waits) + PSA.rel[b])
            return b, ps, tS

        def tpat(blk, btok, actT, t0, act_waits):
            b = PSA.alloc()
            ps = bank(b, 512)
            tS = mm(ps, [(actT[:, kc, t0:t0 + 128], blk[:, kc, :]) for kc in range(16)],
                    [btok] + list(act_waits) + PSA.rel[b])
            return b, ps, tS

        def mem_kv(layer, mT_waits, dst_waits):
            blk, btok, bi = WS.pop("kv%dk" % layer)
            toks = []
            last = None
            for c in range(4):
                b, ps, tS = fpat(blk, btok, c, mT, 0, 256, mT_waits)
                ev = evac_copy("scalar", kmT[:, c, :], ps, [tS] + list(dst_waits))
                PSA.release(b, [ev])
                toks.append(ev)
                last = tS
            WS.release(bi, [last])
            blk, btok, bi = WS.pop("kv%dv" % layer)
            for t in range(2):
                b, ps, tS = tpat(blk, btok, mT, t * 128, mT_waits)
                ev = evac_copy("vector", vm[:, t, :], ps, [tS] + list(dst_waits))
                PSA.release(b, [ev])
                toks.append(ev)
                last = tS
            WS.release(bi, [last])
            return toks

        def mem_attn(q_waits, kv_waits, pT, rl):
            toks = []
            pend = None
            for j in range(4):
                for tg in range(2):
                    QT = QC[:, 12 + j, tg * 512:(tg + 1) * 512]
                    tiles = [(kmT[:, j, t * 128:(t + 1) * 128], vm[:, t, j * 128:(j + 1) * 128]) for t in range(2)]
                    st = attn_unit(QT, tiles, QT, 512, q_waits, kv_waits, pT, rl, split=True)
                    if pend is not None:
                        toks.append(pend())
                    pend = st
            toks.append(pend())
            return toks

        def wo_phase(layer, hT, cat_waits, h_waits):
            toks = []
            for cb in range(4):
                blk, btok, bi = WS.pop("wo%d_%d" % (layer, cb))
                last = None
                for c in range(4):
                    for tg in range(2):
                        b, ps, tS = fpat(blk, btok, c, QC, tg * 512, 512, cat_waits)
                        hv = hT[:, cb * 4 + c, tg * 512:(tg + 1) * 512]
                        ev = P.op("vector", lambda e, hv=hv, ps=ps: e.tensor_tensor(out=hv, in0=ps, in1=hv, op=ALU.add), waits=[tS] + list(h_waits))
                        PSA.release(b, [ev])
                        toks.append(ev)
                        last = tS
                WS.release(bi, [last])
            return toks

        def mlp_phase(layer, hT, nT):
            aT = QC
            rt = [vf(B + 108544, 512), vf(B + 110592, 512)]
            rt_war = [[], []]
            ri = 0
            h_toks = []
            a_war = bar()
            for kg in range(4):
                a_toks = []
                for j in range(4):
                    blk, btok, bi = WS.pop("up%d_%d" % (layer, kg * 4 + j))
                    last = None
                    for tg in range(2):
                        for c in range(4):
                            b, ps, tS = fpat(blk, btok, c, nT, tg * 512, 512, nwaits(tg * 512, 512))
                            s = ri % 2
                            ri += 1
                            rv = rt[s]
                            t1 = P.op("scalar", lambda e, rv=rv, ps=ps: e.activation(out=rv, in_=ps, func=AF.Square), waits=[tS] + rt_war[s])
                            av = aT[:, j * 4 + c, tg * 512:(tg + 1) * 512]
                            t2 = P.op("vector", lambda e, rv=rv, ps=ps, av=av: e.scalar_tensor_tensor(out=av, in0=ps, scalar=0.0, in1=rv, op0=ALU.is_gt, op1=ALU.mult),
                                      waits=[t1] + a_war)
                            rt_war[s] = [t2]
                            PSA.release(b, [t2])
                            a_toks.append(t2)
                            last = tS
                    WS.release(bi, [last])
                lastdn = None
                for cb in range(4):
                    blk, btok, bi = WS.pop("dn%d_%d_%d" % (layer, kg, cb))
                    last = None
                    for c in range(4):
                        for tg in range(2):
                            b, ps, tS = fpat(blk, btok, c, aT, tg * 512, 512, a_toks)
                            hv = hT[:, cb * 4 + c, tg * 512:(tg + 1) * 512]
                            ev = P.op("vector", lambda e, hv=hv, ps=ps: e.tensor_tensor(out=hv, in0=ps, in1=hv, op=ALU.add), waits=[tS])
                            PSA.release(b, [ev])
                            h_toks.append(ev)
                            last = tS
                    WS.release(bi, [last])
                    lastdn = last
                a_war = [lastdn]
            return h_toks

        def emit_out(srcT_of, dst, waits_of, otile, ot_war):
            dts = []
            for tt in range(8):
                srcT = srcT_of(tt)
                s = tt % 2
                evs = []
                for kq in range(4):
                    b = PSA.alloc()
                    pb = bank(b)
                    tk = None
                    for i in range(4):
                        kc = kq * 4 + i
                        sv = srcT[:, kc, :]
                        tk = P.op("tensor", lambda e, pb=pb, i=i, sv=sv: e.transpose(out=pb[:, i * 128:(i + 1) * 128], in_=sv, identity=ident),
                                  waits=(list(waits_of(tt)) + [t_ident] + PSA.rel[b]) if i == 0 else (), sig=(i == 3))
                    dstv = otile[s][:, kq * 512:(kq + 1) * 512]
                    ev = evac_copy("scalar" if kq % 2 else "vector", dstv, pb, [tk] + ot_war[s])
                    PSA.release(b, [ev])
                    evs.append(ev)
                td = P.dma("sync", dst[tt * 128:(tt + 1) * 128, :], otile[s], "o%d" % s, waits=evs)
                ot_war[s] = [td]
                dts.append(td)
            return dts

        final_waits = []
        TN = {"g": []}

        def nwaits(t0, n):
            out = []
            for g in range(t0 // 256, (t0 + n - 1) // 256 + 1):
                out += TN["g"][g]
            return out

        def nall():
            out = []
            for g in TN["g"]:
                out += g
            return out

        xinA = [vf(B + 40960, 2048), vf(B + 49152, 2048)]
        if do1:
            xinA += [vf(O_QC + 16384, 2048), vf(O_QC + 24576, 2048)]
        xTa = vf(B + 57344, 16, 256)
        xw = [[] for _ in xinA]
        xTa_war = []

        def mem_norm():
            tl = load_T(mem_d, xTa, 256, xinA, xw, xTa_war)
            return norm_T(xTa, 256, 0, mT, tl, [])

        if not do1:
            t_mT = mem_norm()

        if do1:
            nT0 = vb(B, 16, NEXT)
            TN["g"] = []
            xTb = [xTa, vf(B + 73728, 16, 256)]
            xT_war = [[], []]
            srcs = [x_d[gi * 256:(gi + 1) * 256, :] for gi in range(5)] + [mem_d]
            loads = {}

            def do_load(i):
                loads[i] = load_T(srcs[i], xTb[i % 2], 256, xinA, xw, xT_war[i % 2])

            do_load(0)
            for i in range(6):
                if i + 1 < 6:
                    do_load(i + 1)
                if i < 5:
                    tn = norm_T(xTb[i % 2], 256, 16, nT0[:, :, i * 256:(i + 1) * 256], loads[i], [])
                    TN["g"].append(tn)
                else:
                    tn = norm_T(xTb[i % 2], 256, 0, mT, loads[i], [])
                    t_mT = tn
                xT_war[i % 2] = tn
            t_q = []
            x_all = []
            for i in range(6):
                x_all += loads[i]
            for qi in range(4):
                blk, btok, bi = WS.pop("aq%d" % qi if qi < 3 else "aqm")
                last = None
                for tg in range(2):
                    for c in range(4):
                        b, ps, tS = fpat(blk, btok, c, nT0, tg * 512, 512, nwaits(tg * 512, 512))
                        ev = evac_copy("scalar", QC[:, qi * 4 + c, tg * 512:(tg + 1) * 512], ps, [tS] + (x_all if qi >= 2 else []))
                        PSA.release(b, [ev])
                        t_q.append(ev)
                        last = tS
                WS.release(bi, [last])
            t_kvm = mem_kv(0, t_mT, [])
            KTg = vb(B + 40960, 4, NEXT)
            Vg = vb(B + 51200, 10, 512)
            biasb = [vf(B + 61440, 1664), vf(B + 68096, 1664)]
            pT_flat = vb(B + 74752, 2048)
            pT = [pT_flat[:, i * 512:(i + 1) * 512] for i in range(4)]
            stmp = [vf(B + 78848, 640), vf(B + 81408, 640), vf(B + 94208, 640)]
            pT_na = vb(B + 88064, 3072)
            rl = [vf(B + 83968, 512), vf(B + 86016, 512)]
            kv_war = bar()
            bias_war = [list(kv_war), list(kv_war)]
            t_cat = mem_attn(t_q, t_kvm, pT, rl)
            for g in range(3):
                blk, btok, bi = WS.pop("ak%d" % g)
                t_k = []
                last = None
                for c in range(4):
                    for (t0, n) in ((0, 512), (512, 512), (1024, 256)):
                        b, ps, tS = fpat(blk, btok, c, nT0, t0, n, nwaits(t0, n))
                        ev = evac_copy("scalar", KTg[:, c, t0:t0 + n], ps, [tS] + kv_war)
                        PSA.release(b, [ev])
                        t_k.append(ev)
                        last = tS
                WS.release(bi, [last])
                blk, btok, bi = WS.pop("av%d" % g)
                for t in range(10):
                    b, ps, tS = tpat(blk, btok, nT0, t * 128, nwaits(t * 128, 128))
                    ev = evac_copy("vector" if t % 2 else "scalar", Vg[:, t, :], ps, [tS] + kv_war)
                    PSA.release(b, [ev])
                    t_k.append(ev)
                    last = tS
                WS.release(bi, [last])
                lastatt = None
                pend = []
                mul_q = []
                for hh in range(4):
                    h = g * 4 + hh
                    s = h % 2
                    t_b = P.dma("sync", biasb[s], bias_d[h], "bias%d" % s, waits=bias_war[s])
                    for m in range(8):
                        if m < 2:
                            tl_, boff = [0, 1, 2, 3], m * 512
                        else:
                            tl_, boff = list(range(m - 2, m + 3)), 1024
                        nt = len(tl_)
                        tiles = [(KTg[:, hh, t * 128:(t + 1) * 128], Vg[:, t, hh * 128:(hh + 1) * 128]) for t in tl_]
                        QT = QC[:, h, m * 128:(m + 1) * 128]
                        if len(pend) >= 3:
                            mul_q.append(pend.pop(0)())
                        if len(mul_q) >= 2:
                            lastatt = mul_q.pop(0)()
                            t_cat.append(lastatt)
                        pvs = attn_unit(QT, tiles, QT, 128, t_q, t_k, pT, rl, stmp=stmp,
                                        bias=biasb[s][:, boff:boff + nt * 128], bias_waits=[t_b])
                        pend.append(pvs)
                    bias_war[s] = [P.last("vector")]
                while pend:
                    mul_q.append(pend.pop(0)())
                while mul_q:
                    lastatt = mul_q.pop(0)()
                    t_cat.append(lastatt)
                kv_war = [P.last("tensor"), lastatt]
            hT = vf(B, 16, NTOK)
            xinC = [vf(B + 65536, 2048), vf(B + 73728, 2048)]
            bw = bar()
            xw = [list(bw), list(bw)]
            t_h = []
            for gi in range(4):
                t_h += load_T(x_d[gi * 256:(gi + 1) * 256, :], hT[:, :, gi * 256:(gi + 1) * 256], 256, xinC, xw, bw)
            t_h = wo_phase(0, hT, t_cat, t_h)
            nT = vb(B + 65536, 16, NTOK)
            nw = bar()
            TN["g"] = []
            for gi in range(4):
                TN["g"].append(norm_T(hT[:, :, gi * 256:(gi + 1) * 256], 256, 32, nT[:, :, gi * 256:(gi + 1) * 256], t_h, nw))
            if mode == "fused":
                t_kvm1 = mem_kv(1, t_mT, nw)
            t_h = mlp_phase(0, hT, nT)
            nw = bar()
            TN["g"] = []
            for gi in range(4):
                TN["g"].append(norm_T(hT[:, :, gi * 256:(gi + 1) * 256], 256, 48, nT[:, :, gi * 256:(gi + 1) * 256], t_h, nw))

        if mode == "s2":
            hT = vf(B, 16, NTOK)
            nT = vb(B + 65536, 16, NTOK)
            xinC = [vf(B + 65536, 2048), vf(B + 73728, 2048)]
            bw = bar()
            xw = [list(bw), list(bw)]
            t_h = []
            for gi in range(4):
                t_h += load_T(h1_i[gi * 256:(gi + 1) * 256, :], hT[:, :, gi * 256:(gi + 1) * 256], 256, xinC, xw, bw)
            nw = bar()
            TN["g"] = []
            for gi in range(4):
                TN["g"].append(norm_T(hT[:, :, gi * 256:(gi + 1) * 256], 256, 48, nT[:, :, gi * 256:(gi + 1) * 256], t_h, nw))

        cosT = vf(B + 108544, NTOK)
        sinT = vf(B + 112640, NTOK)
        sqq = vb(B + 116736, 512)
        rstq = vf(B + 117760, 512)
        tq = vf(B + 119808, 512)
        qh = vb(B + 121856, 512)
        t1b = vf(B + 122880, 512)
        t2b = vf(B + 124928, 512)
        qk = {"sqq": [], "rstq": [], "tq": [], "qh": [], "t1": [], "t2": []}
        bw = bar()
        t_cos = P.dma("sync", cosT, cos_d, "c_cos", waits=bw)
        t_sin = P.dma("sync", sinT, sin_d, "c_sin", waits=bw)

        def qk_post(b, ps, tS, gcol, t0, dst, dst_waits, done):
            t1 = P.op("scalar", lambda e: e.activation(out=sqq, in_=ps, func=AF.Square), waits=[tS] + qk["sqq"])
            res = {}

            def stage2():
                b2 = PSA.alloc()
                ps2 = bank(b2)
                t2 = mm(ps2, [(ones, sqq)], [t1, t_ones] + PSA.rel[b2])
                qk["sqq"] = [t2]
                t3 = P.op("scalar", lambda e: e.activation(out=tq, in_=ps2, func=AF.Ln, bias=epsT, scale=1.0 / 128), waits=[t2, t_eps] + qk["tq"])
                PSA.release(b2, [t3])
                t4 = P.op("scalar", lambda e: e.activation(out=rstq, in_=tq, func=AF.Exp, scale=-0.5), waits=[t3] + qk["rstq"])
                qk["tq"] = [t4]
                t5 = P.op("vector", lambda e: e.scalar_tensor_tensor(out=qh, in0=ps, scalar=gains[:, gcol:gcol + 1], in1=rstq, op0=ALU.mult, op1=ALU.mult),
                          waits=[t4, t_gains] + qk["qh"])
                PSA.release(b, [t5])
                qk["rstq"] = [t5]
                res["t5"] = t5

            def stage3():
                t5 = res["t5"]
                b3 = PSA.alloc()
                ps3 = bank(b3)
                t6 = mm(ps3, [(perm, qh)], [t5, t_perm] + PSA.rel[b3])
                cv = cosT[:, t0:t0 + 512]
                sv = sinT[:, t0:t0 + 512]
                t7 = P.op("vector", lambda e: e.tensor_tensor(out=t1b, in0=qh, in1=cv, op=ALU.mult), waits=[t5, t_cos] + qk["t1"])
                t8 = P.op("vector", lambda e: e.tensor_tensor(out=t2b, in0=ps3, in1=sv, op=ALU.mult), waits=[t6, t_sin] + qk["t2"])
                PSA.release(b3, [t8])
                t9 = P.op("vector", lambda e: e.tensor_tensor(out=dst, in0=t1b, in1=t2b, op=ALU.add), waits=[t7, t8] + list(dst_waits))
                qk["qh"] = [t6, t7]
                qk["t1"] = [t9]
                qk["t2"] = [t9]
                done(t9)

            defer(4, stage2)
            defer(20, stage3)

        if do1:
            kst = [vb(B + 98304, 1024), vb(B + 100352, 1024)]
            vst = [vb(B + 102400, 512), vb(B + 103424, 512)]
            kst_war = [nall(), nall()]
            vst_war = [nall(), nall()]
            kv_dmas = []
            blk, btok, bi = WS.pop("bv")
            last = None
            for t in range(8):
                s = t % 2
                b, ps, tS = tpat(blk, btok, nT, t * 128, nwaits(t * 128, 128))
                ev = evac_copy("scalar", vst[s], ps, [tS] + vst_war[s])
                PSA.release(b, [ev])
                td = P.dma("sync", kv_own[:, 4096 + t * 512: 4096 + (t + 1) * 512], vst[s], "kvo_v%d" % s, waits=[ev])
                vst_war[s] = [td]
                kv_dmas.append(td)
                last = tS
            WS.release(bi, [last])
            blk, btok, bi = WS.pop("bk")
            last = None
            for c in range(4):
                s = c % 2
                t9s = []

                def kdone(t9, c=c, s=s, t9s=t9s):
                    t9s.append(t9)
                    if len(t9s) == 2:
                        td = P.dma("sync", kv_own[:, c * 1024:(c + 1) * 1024], kst[s], "kvo_k%d" % s, waits=t9s)
                        kst_war[s] = [td]
                        kv_dmas.append(td)

                if c >= 2:
                    pe_flush()
                for tg in range(2):
                    b, ps, tS = fpat(blk, btok, c, nT, tg * 512, 512, nall())
                    qk_post(b, ps, tS, 97, tg * 512, kst[s][:, tg * 512:(tg + 1) * 512], list(kst_war[s]), kdone)
                    last = tS
            WS.release(bi, [last])
            pe_flush()
            final_waits += kv_dmas

        if mode == "s1":
            otile = [vf(B + 81920, 2048), vf(B + 90112, 2048)]
            bw = bar()
            ow = [list(bw), list(bw)]
            final_waits += emit_out(lambda tt: hT[:, :, tt * 128:(tt + 1) * 128], h1_o, lambda tt: t_h, otile, ow)

        kvfull_waits = []

        if do2:
            t_kvm = t_kvm1 if mode == "fused" else mem_kv(1, t_mT, bar())
            t_q = []
            q_war = bar()
            for qi in range(4):
                blk, btok, bi = WS.pop("bq%d" % qi if qi < 3 else "bqm")
                if qi == 0 and mode == "fused":
                    t_cc = P.custom("gpsimd",
                                    lambda e: e.collective_compute("AllGather", ALU.bypass, replica_groups=[[0, 1], [2, 3], [4, 5], [6, 7]],
                                                                   ins=[kv_own.opt()], outs=[kv_full.opt()]),
                                    "cc", waits=kv_dmas, inc=1)
                    kvfull_waits.append(t_cc)
                last = None
                for c in range(4):
                    for tg in range(2):
                        b, ps, tS = fpat(blk, btok, c, nT, tg * 512, 512, nall())
                        dst = QC[:, qi * 4 + c, tg * 512:(tg + 1) * 512]
                        if qi < 3:
                            qk_post(b, ps, tS, 96, tg * 512, dst, q_war, t_q.append)
                        else:
                            ev = evac_copy("scalar", dst, ps, [tS] + q_war)
                            PSA.release(b, [ev])
                            t_q.append(ev)
                        last = tS
                WS.release(bi, [last])
            pe_flush()
            KTf = vb(B + 65536, 4, 2048)
            Vf = vb(B + 81920, 16, 512)
            pT_flat = vb(B + 98304, 4096)
            pT = [pT_flat[:, i * 512:(i + 1) * 512] for i in range(8)]
            rl = [vf(B + 106496, 512), vf(B + 108544, 512)]
            bw = bar()
            if mode == "fused":
                bw = bw + kv_dmas
            att["NP"] = 4
            att["pT_war"] = [list(bw) for _ in range(8)]
            att["rl_war"] = [list(bw), list(bw)]
            t_kv = []
            for r in range(2):
                t_kv.append(P.dma("sync", KTf[:, :, r * 1024:(r + 1) * 1024],
                                  kv_full[r * 128:(r + 1) * 128, 0:4096].rearrange("p (h t) -> p h t", t=1024), "kvl", waits=bw + kvfull_waits))
                t_kv.append(P.dma("sync", Vf[:, r * 8:(r + 1) * 8, :],
                                  kv_full[r * 128:(r + 1) * 128, 4096:8192].rearrange("p (t n) -> p t n", n=512), "kvl", waits=bw + kvfull_waits))
            t_cat = mem_attn(t_q, t_kvm, pT, rl)
            for h in range(12):
                kvh = h // 3
                for tg in range(2):
                    QT = QC[:, h, tg * 512:(tg + 1) * 512]
                    tiles = [(KTf[:, kvh, kt * 128:(kt + 1) * 128], Vf[:, kt, kvh * 128:(kvh + 1) * 128]) for kt in range(16)]
                    t_cat.append(attn_unit(QT, tiles, QT, 512, t_q, t_kv, pT, rl))
            t_h = wo_phase(1, hT, t_cat, t_h)
            nw = bar()
            TN["g"] = []
            for gi in range(4):
                TN["g"].append(norm_T(hT[:, :, gi * 256:(gi + 1) * 256], 256, 64, nT[:, :, gi * 256:(gi + 1) * 256], t_h, nw))
            t_h = mlp_phase(1, hT, nT)
            yTs = [vf(B + 65536, 16, 256), vf(B + 81920, 16, 256)]
            otile = [vf(O_QC, 2048), vf(O_QC + 8192, 2048), vf(O_QC + 16384, 2048), vf(O_QC + 24576, 2048)]
            bw = bar()
            ow = [list(bw) for _ in range(4)]
            y_wars = [list(bw), list(bw)]
            tys = {}

            def fin_norm(gi):
                tys[gi] = norm_T(hT[:, :, gi * 256:(gi + 1) * 256], 256, 80, yTs[gi % 2], t_h, y_wars[gi % 2])

            fin_norm(0)
            for gi in range(4):
                yT = yTs[gi % 2]
                if gi + 1 < 4 and gi >= 1:
                    pass
                if gi + 1 < 4 and gi == 0:
                    fin_norm(1)
                ty = tys[gi]
                dts = []
                for tt in range(2):
                    srcT = yT[:, :, tt * 128:(tt + 1) * 128]
                    s = (gi * 2 + tt) % 4
                    evs = []
                    lastk = None
                    for kq in range(4):
                        b = PSA.alloc()
                        pb = bank(b)
                        tk = None
                        for i in range(4):
                            kc = kq * 4 + i
                            sv = srcT[:, kc, :]
                            tk = P.op("tensor", lambda e, pb=pb, i=i, sv=sv: e.transpose(out=pb[:, i * 128:(i + 1) * 128], in_=sv, identity=ident),
                                      waits=(list(ty) + [t_ident] + PSA.rel[b]) if i == 0 else (), sig=(i == 3))
                        dstv = otile[s][:, kq * 512:(kq + 1) * 512]
                        ev = evac_copy("scalar" if kq % 2 else "vector", dstv, pb, [tk] + ow[s])
                        PSA.release(b, [ev])
                        evs.append(ev)
                        lastk = tk
                    row = gi * 256 + tt * 128
                    td = P.dma("sync", out_d[row:row + 128, :], otile[s], "o%d" % s, waits=evs)
                    ow[s] = [td]
                    final_waits.append(td)
                y_wars[gi % 2] = [lastk]
                if gi + 2 < 4:
                    fin_norm(gi + 2)

        P.wait_only("sync", final_waits)
        P.replay()
    return nc


def _true_row(l, hf):
    return l if hf == 0 else 31 - l


def _bias_tables(rpb, hf):
    units = [(0, [0, 1, 2, 3]), (1, [0, 1, 2, 3]), (2, [0, 1, 2, 3, 4])]
    p = np.arange(128)
    ki, kc = p // 64, p % 64
    qi, qc = p // 64, p % 64
    cols = []
    for m, tl in units:
        for t in tl:
            kr = np.array([_true_row(2 * t + a, hf) for a in ki])[:, None]
            qr = np.array([_true_row(2 * m + a, hf) for a in qi])[None, :]
            r0 = np.clip(qr - 4, 0, 24)
            vr = (kr >= r0) & (kr < r0 + 8)
            c0 = np.clip(qc - 8, 0, 48)[None, :]
            vc = (kc[:, None] >= c0) & (kc[:, None] < c0 + 16)
            dr = np.clip(kr - qr + 7, 0, 14)
            dc = np.clip(kc[:, None] - qc[None, :] + 15, 0, 30)
            valid = vr & vc
            g = rpb[:, dr, dc]
            cols.append(np.where(valid[None], g, np.float32(MASKV)).astype(np.float32))
    return np.ascontiguousarray(np.concatenate(cols, axis=2))


def _rope_tables(hf):
    t = np.arange(NTOK)
    row = np.array([_true_row(l, hf) for l in (t // 64)], dtype=np.float32)
    col = (t % 64).astype(np.float32)
    inv = np.power(np.float32(10000.0), -np.arange(0, 64, 2, dtype=np.float32) / np.float32(64)).astype(np.float32)
    d = np.arange(128)
    f = d % 32
    pos = np.where((d < 64)[:, None], row[None, :], col[None, :]).astype(np.float32)
    ang = (pos * inv[f][:, None]).astype(np.float32)
    cosT = np.cos(ang).astype(np.float32)
    sgn = np.where((d % 64) < 32, -1.0, 1.0).astype(np.float32)[:, None]
    sinT = (np.sin(ang).astype(np.float32) * sgn).astype(np.float32)
    return np.ascontiguousarray(cosT), np.ascontiguousarray(sinT)


def _fm(vec):
    return np.asarray(vec, dtype=np.float32).reshape(-1, 128).T


_CACHE = {}


def _get_nc(mode):
    if mode not in _CACHE:
        _CACHE[mode] = build(mode)
    return _CACHE[mode]


def kernel(x, mem, mem_norm, attn_norm, mlp_norm, a_w_in, a_rpb, b_w_in, b_q_norm, b_k_norm,
           w_mem_kv, w_o, w_up, w_down, final_norm, _mode="fused"):
    x = np.asarray(x, dtype=np.float32)
    mem = np.asarray(mem, dtype=np.float32)
    gains = np.concatenate([_fm(mem_norm), _fm(attn_norm[0]), _fm(mlp_norm[0]), _fm(attn_norm[1]), _fm(mlp_norm[1]),
                            _fm(final_norm), np.asarray(b_q_norm[0], np.float32)[:, None], np.asarray(b_k_norm[0], np.float32)[:, None]], axis=1)
    gains = np.ascontiguousarray(gains.astype(np.float32))
    d = np.arange(128)
    partner = np.where((d % 64) < 32, d + 32, d - 32)
    perm = np.zeros((128, 128), np.float32)
    perm[partner, d] = 1.0
    ident = np.eye(128, dtype=np.float32)
    rpb = np.asarray(a_rpb[0], np.float32)
    bias = [_bias_tables(rpb, 0), _bias_tables(rpb, 1)]
    rope = [_rope_tables(0), _rope_tables(1)]
    common = {
        "ident": ident, "gains": gains, "perm": perm,
        "b_w_in": np.ascontiguousarray(np.asarray(b_w_in[0], np.float32)),
        "w_mem_kv": np.asarray(w_mem_kv, np.float32).reshape(2 * D, 1024),
        "w_o": np.asarray(w_o, np.float32).reshape(2 * D, D),
        "w_up": np.asarray(w_up, np.float32).reshape(2 * D, 8192),
        "w_down": np.asarray(w_down, np.float32).reshape(2 * 8192, D),
    }
    a_in = np.ascontiguousarray(np.asarray(a_w_in[0], np.float32))
    maps1 = []
    for c in range(8):
        b, hf = c // 2, c % 2
        xb = x[b].reshape(32, 64, D)
        if hf:
            xb = xb[::-1]
        m = dict(common)
        m["x_ext"] = np.ascontiguousarray(xb[:20].reshape(NEXT, D))
        m["mem_b"] = np.ascontiguousarray(mem[b])
        m["bias0"] = bias[hf]
        m["a_w_in"] = a_in
        m["cosT"], m["sinT"] = rope[hf]
        maps1.append(m)
    if _mode == "fused":
        res = run_bass_kernel_spmd(_get_nc("fused"), maps1, core_ids=list(range(8)))
        outs = [r["out"] for r in res.results]
    else:
        res1 = run_bass_kernel_spmd(_get_nc("s1"), maps1, core_ids=list(range(8)))
        maps2 = []
        for c in range(8):
            b, hf = c // 2, c % 2
            m = dict(common)
            m["mem_b"] = maps1[c]["mem_b"]
            m["cosT"], m["sinT"] = rope[hf]
            m["h1"] = np.asarray(res1.results[c]["h1"])
            own = np.asarray(res1.results[c]["kv_own"])
            oth = np.asarray(res1.results[c ^ 1]["kv_own"])
            pair = [own, oth] if hf == 0 else [oth, own]
            m["kv_full"] = np.ascontiguousarray(np.concatenate(pair, axis=0))
            maps2.append(m)
        res2 = run_bass_kernel_spmd(_get_nc("s2"), maps2, core_ids=list(range(8)))
        outs = [r["out"] for r in res2.results]
    out = np.empty((4, 2048, D), np.float32)
    for c in range(8):
        b, hf = c // 2, c % 2
        ob = np.asarray(outs[c], np.float32).reshape(16, 64, D)
        if hf:
            ob = ob[::-1]
        out[b, hf * 1024:(hf + 1) * 1024] = ob.reshape(NTOK, D)
    return out
```

```python
import contextlib
import numpy as np
import ml_dtypes
import concourse.bass as bass
import concourse.mybir as mybir
from concourse.bass_utils import run_bass_kernel_spmd

F32 = mybir.dt.float32
BF16 = mybir.dt.bfloat16
ALU = mybir.AluOpType
AF = mybir.ActivationFunctionType

D = 2048
NTOK = 1024
NEXT = 1280
EPS = 1e-6
SCALE = 128 ** -0.5
MASKV = -30000.0

O_IDENT, O_GAINS, O_ONES, O_PERM, O_EPS = 0, 512, 1024, 1280, 1536
O_MT = 4096
O_KMT, O_VM = 12288, 14336
O_RING = 16384
O_QC = 49152
O_BIG = 81920
TOT = 208896


class Prog:
    ENGS = ("sync", "scalar", "vector", "gpsimd", "tensor")

    def __init__(self, nc, stack):
        self.nc = nc
        self.stack = stack
        self.ops = {e: [] for e in self.ENGS}
        self.sem = {}
        self.cnt = {}

    def _sem(self, key):
        if key not in self.sem:
            self.sem[key] = self.stack.enter_context(self.nc.semaphore(key))
            self.cnt[key] = 0
        return self.sem[key]

    def op(self, eng, fn, waits=(), sig=True):
        tok = None
        if sig:
            key = "e_" + eng
            self._sem(key)
            self.cnt[key] += 1
            tok = (key, self.cnt[key])
        self.ops[eng].append((fn, tuple(w for w in waits if w is not None), tok, 1))
        return tok

    def last(self, eng):
        key = "e_" + eng
        if key in self.cnt and self.cnt[key] > 0:
            return (key, self.cnt[key])
        return None

    def dma(self, eng, out, in_, semkey, waits=()):
        return self.custom(eng, lambda e, out=out, in_=in_: e.dma_start(out=out, in_=in_), semkey, waits)

    def custom(self, eng, fn, semkey, waits=(), inc=16):
        self._sem(semkey)
        self.cnt[semkey] += inc
        tok = (semkey, self.cnt[semkey])
        self.ops[eng].append((fn, tuple(w for w in waits if w is not None), tok, inc))
        return tok

    def wait_only(self, eng, waits):
        self.ops[eng].append((None, tuple(w for w in waits if w is not None), None, 0))

    def replay(self):
        with self.nc.Block() as block:
            for eng in self.ENGS:
                ops = self.ops[eng]
                if not ops:
                    continue

                def body(e, ops=ops):
                    seen = {}
                    for fn, waits, tok, inc in ops:
                        need = {}
                        for (k, v) in waits:
                            if seen.get(k, 0) < v:
                                need[k] = max(need.get(k, 0), v)
                        for k, v in need.items():
                            e.wait_ge(self.sem[k], v)
                            seen[k] = v
                        if fn is not None:
                            inst = fn(e)
                            if tok is not None:
                                inst.then_inc(self.sem[tok[0]], inc)

                getattr(block, eng)(body)


def build(mode):
    nc = bass.Bass("TRN2", target_bir_lowering=False)

    def din(name, shape, dt=F32):
        return nc.dram_tensor(name, shape, dt, kind="ExternalInput").ap()

    def dout(name, shape, dt=F32):
        return nc.dram_tensor(name, shape, dt, kind="ExternalOutput").ap()

    do1 = mode in ("s1", "fused")
    do2 = mode in ("s2", "fused")
    ident_d = din("ident", [128, 128])
    gains_d = din("gains", [128, 98])
    if do1:
        x_d = din("x_ext", [NEXT, D])
        bias_d = din("bias0", [12, 128, 1664])
        a_in = din("a_w_in", [D, 5120])
    mem_d = din("mem_b", [256, D])
    b_in = din("b_w_in", [D, 3072])
    perm_d = din("perm", [128, 128])
    cos_d = din("cosT", [128, NTOK])
    sin_d = din("sinT", [128, NTOK])
    wkv = din("w_mem_kv", [2 * D, 1024])
    wo = din("w_o", [2 * D, D])
    wup = din("w_up", [2 * D, 8192])
    wdn = din("w_down", [2 * 8192, D])
    if mode == "s1":
        h1_o = dout("h1", [NTOK, D])
        kv_own = dout("kv_own", [128, 8192], BF16)
    if mode == "s2":
        h1_i = din("h1", [NTOK, D])
        kv_full = din("kv_full", [256, 8192], BF16)
    if mode == "fused":
        kv_own = nc.dram_tensor("kv_own", [128, 8192], BF16, kind="Internal").ap()
        kv_full = nc.dram_tensor("kv_full", [256, 8192], BF16, kind="Internal").ap()
    if do2:
        out_d = dout("out", [NTOK, D])

    st = contextlib.ExitStack()
    with st:
        P = Prog(nc, st)
        arena = st.enter_context(nc.sbuf_tensor("arena", [128, TOT // 4], F32))
        abf = arena.bitcast(BF16)
        psum = st.enter_context(nc.psum_tensor("ps", [128, 4096], F32))

        def shp(ap, shape):
            if len(shape) == 1:
                return ap
            if len(shape) == 2:
                return ap.rearrange("p (a b) -> p a b", b=shape[1])
            return ap.rearrange("p (a b c) -> p a b c", b=shape[1], c=shape[2])

        def vf(off, *shape):
            n = int(np.prod(shape))
            return shp(arena[:, off // 4: off // 4 + n], shape)

        def vb(off, *shape):
            n = int(np.prod(shape))
            return shp(abf[:, off // 2: off // 2 + n], shape)

        ident = vf(O_IDENT, 128)
        gains = vf(O_GAINS, 98)
        ones = vb(O_ONES, 128)
        perm = vb(O_PERM, 128)
        epsT = vf(O_EPS, 1)
        scr = vf(2048, 512)
        mT = vb(O_MT, 16, 256)
        kmT = vb(O_KMT, 4, 256)
        vm = vb(O_VM, 2, 512)
        ring = [vb(O_RING + i * 16384, 16, 512) for i in range(2)]
        QC = vb(O_QC, 16, 1024)
        B = O_BIG

        class PSA:
            open = [False] * 8
            rel = [[] for _ in range(8)]
            relseq = list(range(8))
            seq = 8

            @classmethod
            def alloc(c):
                free = [b for b in range(8) if not c.open[b]]
                if not free:
                    raise RuntimeError("psum full")
                b = min(free, key=lambda x: c.relseq[x])
                c.open[b] = True
                return b

            @classmethod
            def alloc2(c):
                free = [b for b in range(0, 8, 2) if not c.open[b] and not c.open[b + 1]]
                if not free:
                    raise RuntimeError("psum full2")
                b = min(free, key=lambda x: max(c.relseq[x], c.relseq[x + 1]))
                c.open[b] = c.open[b + 1] = True
                return b

            @classmethod
            def release(c, b, toks):
                c.open[b] = False
                c.rel[b] = [t for t in toks if t is not None]
                c.relseq[b] = c.seq
                c.seq += 1

        def bank(b, n=512):
            return psum[:, b * 512: b * 512 + n]

        PEQ = []
        peq_busy = [False]

        def defer(n, fn):
            PEQ.append([n, fn])

        def pe_tick():
            if peq_busy[0]:
                return
            peq_busy[0] = True
            for ent in PEQ:
                ent[0] -= 1
            while PEQ and PEQ[0][0] <= 0:
                PEQ.pop(0)[1]()
            peq_busy[0] = False

        def pe_flush():
            peq_busy[0] = True
            while PEQ:
                PEQ.pop(0)[1]()
            peq_busy[0] = False

        def mm(out, pairs, waits):
            n = len(pairs)
            tok = None
            for i, (l, r) in enumerate(pairs):
                tok = P.op("tensor",
                           lambda e, l=l, r=r, i=i, out=out: e.matmul(out, lhsT=l, rhs=r, start=(i == 0), stop=(i == n - 1)),
                           waits=waits if i == 0 else (), sig=(i == n - 1))
                if i < n - 1:
                    pe_tick()
            return tok

        def bar():
            return [P.last(e) for e in ("tensor", "scalar", "vector")]

        def wblock(W, r0, c0):
            return W[r0:r0 + 2048, c0:c0 + 512].rearrange("(k p) n -> p k n", p=128)

        plan = []
        if do1:
            plan += [("aq%d" % i, a_in, 0, i * 512) for i in range(3)] + [("aqm", a_in, 0, 4608)]
            plan += [("kv0k", wkv, 0, 0), ("kv0v", wkv, 0, 512)]
            for g in range(3):
                plan += [("ak%d" % g, a_in, 0, 1536 + g * 512), ("av%d" % g, a_in, 0, 3072 + g * 512)]
            plan += [("wo0_%d" % i, wo, 0, i * 512) for i in range(4)]
            if mode == "fused":
                plan += [("kv1k", wkv, D, 0), ("kv1v", wkv, D, 512)]
            for kg in range(4):
                plan += [("up0_%d" % (kg * 4 + j), wup, 0, (kg * 4 + j) * 512) for j in range(4)]
                plan += [("dn0_%d_%d" % (kg, cb), wdn, kg * 2048, cb * 512) for cb in range(4)]
            plan += [("bv", b_in, 0, 2048), ("bk", b_in, 0, 1536)]
        if do2:
            if mode != "fused":
                plan += [("kv1k", wkv, D, 0), ("kv1v", wkv, D, 512)]
            plan += [("bq%d" % i, b_in, 0, i * 512) for i in range(3)] + [("bqm", b_in, 0, 2560)]
            plan += [("wo1_%d" % i, wo, D, i * 512) for i in range(4)]
            for kg in range(4):
                plan += [("up1_%d" % (kg * 4 + j), wup, D, (kg * 4 + j) * 512) for j in range(4)]
                plan += [("dn1_%d_%d" % (kg, cb), wdn, 8192 + kg * 2048, cb * 512) for cb in range(4)]

        class WS:
            nxt = 0
            cur = 0
            rel = {}
            loaded = {}

            @classmethod
            def pop(c, name):
                i = c.cur
                assert plan[i][0] == name, (plan[i][0], name)
                while c.nxt < len(plan) and c.nxt <= i + 1:
                    j = c.nxt
                    w = c.rel.get(j - 2, [])
                    assert j < 2 or (j - 2) in c.rel
                    if j < 2:
                        w = list(state["xdma"][:4])
                    _, W, r0, c0 = plan[j]
                    c.loaded[j] = P.dma("gpsimd", ring[j % 2], wblock(W, r0, c0), "w%d" % (j % 2), waits=w)
                    c.nxt += 1
                c.cur += 1
                return ring[i % 2], c.loaded[i], i

            @classmethod
            def release(c, i, toks):
                c.rel[i] = [t for t in toks if t is not None]

        t_ident = P.dma("sync", ident, ident_d, "c_id")
        t_gains = P.dma("sync", gains, gains_d, "c_g")
        t_perm = P.dma("gpsimd", perm, perm_d, "cstp")
        t_ones = P.op("vector", lambda e: e.memset(ones, 1.0))
        t_eps = P.op("vector", lambda e: e.memset(epsT, EPS))
        cst = [t_ident, t_gains, t_perm, t_ones, t_eps]

        sq = vb(B + 98304, 16, 256)
        rstd = vf(B + 106496, 256)
        tmpn = vf(B + 107520, 256)
        state = {"sq": [], "rstd": [], "tmpn": [], "xin_i": 0, "ev_i": 0, "xdma": []}

        def load_T(src, dstT, T, xin, xin_war, waits):
            toks = []
            for tt in range(T // 128):
                s = state["xin_i"] % len(xin)
                state["xin_i"] += 1
                tX = P.dma("sync", xin[s], src[tt * 128:(tt + 1) * 128, :], "x%d" % s, waits=xin_war[s])
                state["xdma"].append(tX)
                lastk = None
                for kq in range(4):
                    b = PSA.alloc()
                    pb = bank(b)
                    for i in range(4):
                        kc = kq * 4 + i
                        tk = P.op("tensor",
                                  lambda e, pb=pb, i=i, s=s, kc=kc: e.transpose(out=pb[:, i * 128:(i + 1) * 128], in_=xin[s][:, kc * 128:(kc + 1) * 128], identity=ident),
                                  waits=([tX, t_ident] + PSA.rel[b]) if i == 0 else (), sig=(i == 3))
                    state["ev_i"] += 1
                    dst = dstT[:, kq * 4:(kq + 1) * 4, tt * 128:(tt + 1) * 128]
                    src_ps = pb.rearrange("p (a b) -> p a b", b=128)
                    if state["ev_i"] % 2:
                        ev = P.op("vector", lambda e, dst=dst, src_ps=src_ps: e.tensor_copy(out=dst, in_=src_ps), waits=[tk] + list(waits))
                    else:
                        ev = P.op("scalar", lambda e, dst=dst, src_ps=src_ps: e.copy(out=dst, in_=src_ps), waits=[tk] + list(waits))
                    PSA.release(b, [ev])
                    toks.append(ev)
                    lastk = tk
                xin_war[s] = [lastk]
            return toks

        def norm_T(srcT, T, gcol, dst, src_waits, dst_waits, dmodel=2048):
            sqv = sq[:, :, :T]
            tsq = P.op("scalar", lambda e: e.activation(out=sqv, in_=srcT, func=AF.Square), waits=list(src_waits) + state["sq"])
            b = PSA.alloc()
            ps = bank(b, T)
            tss = mm(ps, [(ones, sq[:, kc, :T]) for kc in range(16)], [tsq, t_ones] + PSA.rel[b])
            state["sq"] = [tss]
            t1 = P.op("scalar", lambda e: e.activation(out=tmpn[:, :T], in_=ps, func=AF.Ln, bias=epsT, scale=1.0 / dmodel), waits=[tss, t_eps] + state["tmpn"])
            PSA.release(b, [t1])
            t2 = P.op("scalar", lambda e: e.activation(out=rstd[:, :T], in_=tmpn[:, :T], func=AF.Exp, scale=-0.5), waits=[t1] + state["rstd"])
            state["tmpn"] = [t2]
            toks = []
            for kc in range(16):
                toks.append(P.op("vector",
                                 lambda e, kc=kc: e.scalar_tensor_tensor(out=dst[:, kc, :], in0=srcT[:, kc, :], scalar=gains[:, gcol + kc:gcol + kc + 1], in1=rstd[:, :T], op0=ALU.mult, op1=ALU.mult),
                                 waits=[t2, t_gains] + (list(dst_waits) if kc == 0 else [])))
            state["rstd"] = [toks[-1]]
            return toks

        att = {"pT_war": [[] for _ in range(8)], "pT_i": 0, "NP": 2, "rl_war": [[], []], "rl_i": 0, "st_war": [[], [], []], "st_i": 0,
               "na_war": [[], [], []], "na_i": 0}

        def attn_unit(QT, tiles, out, NQ, q_waits, kv_waits, pT, rl, stmp=None, bias=None, bias_waits=(), split=False):
            nt = len(tiles)
            base_w = list(q_waits) + list(kv_waits) + [t_ones]
            ctx = {"tokPV": None}

            def open_acc():
                ctx["bO"] = PSA.alloc2()
                ctx["bL"] = ctx["bO"] + 1
                ctx["Oa"] = bank(ctx["bO"], NQ)
                ctx["La"] = bank(ctx["bL"], NQ)

            def issuePV(j, rhs, tP, slots):
                first = (j == 0)
                last = (j == nt - 1)
                Vj = tiles[j][1]
                Oa, La, bO, bL = ctx["Oa"], ctx["La"], ctx["bO"], ctx["bL"]
                P.op("tensor", lambda e, Vj=Vj, rhs=rhs: e.matmul(Oa, lhsT=Vj, rhs=rhs, start=first, stop=last),
                     waits=[tP] + (PSA.rel[bO] + PSA.rel[bL] if first else []), sig=False)
                ctx["tokPV"] = P.op("tensor", lambda e, rhs=rhs: e.matmul(La, lhsT=ones, rhs=rhs, start=first, stop=last), sig=True)
                for s_ in slots:
                    att["pT_war"][s_] = [ctx["tokPV"]]

            def finish_act():
                ri = att["rl_i"] % 2
                att["rl_i"] += 1
                ctx["ri"] = ri
                rv = rl[ri][:, :NQ]
                ctx["rv"] = rv
                tR0 = P.op("scalar", lambda e: e.activation(out=rv, in_=ctx["La"], func=AF.Ln), waits=[ctx["tokPV"]] + att["rl_war"][ri])
                ctx["tR0"] = tR0
                ctx["tR"] = P.op("scalar", lambda e: e.activation(out=rv, in_=rv, func=AF.Exp, scale=-1.0), waits=[tR0])

            def finish_mul():
                Oa, bO, bL, rv, ri = ctx["Oa"], ctx["bO"], ctx["bL"], ctx["rv"], ctx["ri"]
                tO = P.op("vector", lambda e: e.tensor_tensor(out=out, in0=Oa, in1=rv, op=ALU.mult), waits=[ctx["tR"]])
                PSA.release(bO, [tO])
                PSA.release(bL, [ctx["tR0"]])
                att["rl_war"][ri] = [tO]
                return tO

            def finish():
                finish_act()
                return finish_mul()

            if bias is None:
                if not split:
                    open_acc()
                assert NQ == 512 and nt % 2 == 0
                npairs = nt // 2
                q = []

                def issueS2(p):
                    b2 = PSA.alloc2()
                    tS = None
                    for k in range(2):
                        KTj = tiles[2 * p + k][0]
                        Sj = psum[:, (b2 + k) * 512:(b2 + k + 1) * 512]
                        tS = P.op("tensor", lambda e, KTj=KTj, Sj=Sj: e.matmul(Sj, lhsT=KTj, rhs=QT, start=True, stop=True),
                                  waits=(base_w + PSA.rel[b2] + PSA.rel[b2 + 1]) if k == 0 else (), sig=(k == 1))
                    half = att["pT_i"] % att["NP"]
                    att["pT_i"] += 1
                    src = psum[:, b2 * 512: b2 * 512 + 1024]
                    dst = pT_flat[:, half * 1024: half * 1024 + 1024]
                    tP = P.op("scalar", lambda e: e.activation(out=dst, in_=src, func=AF.Exp, scale=SCALE),
                              waits=[tS] + att["pT_war"][2 * half] + att["pT_war"][2 * half + 1])
                    PSA.release(b2, [tP])
                    PSA.release(b2 + 1, [tP])
                    q.append((tP, half))

                LOOKP = max(2, att["NP"] - 1)
                for p in range(min(LOOKP, npairs)):
                    issueS2(p)
                if split:
                    assert npairs == 1

                    def pv_only():
                        open_acc()
                        tP, half = q[0]
                        for k in range(2):
                            issuePV(k, pT_flat[:, half * 1024 + k * 512: half * 1024 + (k + 1) * 512], tP, [2 * half, 2 * half + 1])
                        return finish()

                    return pv_only
                for p in range(npairs):
                    tP, half = q[p]
                    for k in range(2):
                        issuePV(2 * p + k, pT_flat[:, half * 1024 + k * 512: half * 1024 + (k + 1) * 512], tP, [2 * half, 2 * half + 1])
                    if p + LOOKP < npairs:
                        issueS2(p + LOOKP)
                return finish()

            b2 = PSA.alloc2()
            S = psum[:, b2 * 512: b2 * 512 + nt * 128]
            tS = None
            for j in range(nt):
                KTj = tiles[j][0]
                Sj = psum[:, b2 * 512 + j * 128: b2 * 512 + (j + 1) * 128]
                tS = P.op("tensor", lambda e, KTj=KTj, Sj=Sj: e.matmul(Sj, lhsT=KTj, rhs=QT, start=True, stop=True),
                          waits=(base_w + PSA.rel[b2] + PSA.rel[b2 + 1]) if j == 0 else (), sig=(j == nt - 1))
            si = att["st_i"] % 3
            att["st_i"] += 1
            sv = stmp[si][:, :nt * 128]
            tB = P.op("vector", lambda e: e.scalar_tensor_tensor(out=sv, in0=S, scalar=SCALE, in1=bias, op0=ALU.mult, op1=ALU.add),
                      waits=[tS] + list(bias_waits) + att["st_war"][si])
            PSA.release(b2, [tB])
            PSA.release(b2 + 1, [tB])
            third = att["na_i"] % 3
            att["na_i"] += 1
            pfull = pT_na[:, third * 1024: third * 1024 + nt * 128]
            tP = P.op("scalar", lambda e: e.activation(out=pfull, in_=sv, func=AF.Exp),
                      waits=[tB] + att["na_war"][third])
            att["st_war"][si] = [tP]

            def pv_stage():
                open_acc()
                for j in range(nt):
                    issuePV(j, pT_na[:, third * 1024 + j * 128: third * 1024 + (j + 1) * 128], tP, [])
                att["na_war"][third] = [ctx["tokPV"]]
                finish_act()
                return finish_mul

            return pv_stage

        def evac_copy(eng, dst, ps, waits):
            if eng == "scalar":
                return P.op("scalar", lambda e: e.copy(out=dst, in_=ps), waits=waits)
            return P.op("vector", lambda e: e.tensor_copy(out=dst, in_=ps), waits=waits)

        def fpat(blk, btok, c, actT, t0, n, act_waits):
            b = PSA.alloc()
            ps = bank(b, n)
            tS = mm(ps, [(blk[:, kc, c * 128:(c + 1) * 128], actT[:, kc, t0:t0 + n]) for kc in range(16)],
                    [btok] + list(act_waits) + PSA.rel[b])
            return b, ps, tS

        def tpat(blk, btok, actT, t0, act_waits):
            b = PSA.alloc()
            ps = bank(b, 512)
            tS = mm(ps, [(actT[:, kc, t0:t0 + 128], blk[:, kc, :]) for kc in range(16)],
                    [btok] + list(act_waits) + PSA.rel[b])
            return b, ps, tS

        def mem_kv(layer, mT_waits, dst_waits):
            blk, btok, bi = WS.pop("kv%dk" % layer)
            toks = []
            last = None
            for c in range(4):
                b, ps, tS = fpat(blk, btok, c, mT, 0, 256, mT_waits)
                ev = evac_copy("scalar", kmT[:, c, :], ps, [tS] + list(dst_waits))
                PSA.release(b, [ev])
                toks.append(ev)
                last = tS
            WS.release(bi, [last])
            blk, btok, bi = WS.pop("kv%dv" % layer)
            for t in range(2):
                b, ps, tS = tpat(blk, btok, mT, t * 128, mT_waits)
                ev = evac_copy("vector", vm[:, t, :], ps, [tS] + list(dst_waits))
                PSA.release(b, [ev])
                toks.append(ev)
                last = tS
            WS.release(bi, [last])
            return toks

        def mem_attn(q_waits, kv_waits, pT, rl):
            toks = []
            pend = None
            for j in range(4):
                for tg in range(2):
                    QT = QC[:, 12 + j, tg * 512:(tg + 1) * 512]
                    tiles = [(kmT[:, j, t * 128:(t + 1) * 128], vm[:, t, j * 128:(j + 1) * 128]) for t in range(2)]
                    st = attn_unit(QT, tiles, QT, 512, q_waits, kv_waits, pT, rl, split=True)
                    if pend is not None:
                        toks.append(pend())
                    pend = st
            toks.append(pend())
            return toks

        def wo_phase(layer, hT, cat_waits, h_waits):
            toks = []
            for cb in range(4):
                blk, btok, bi = WS.pop("wo%d_%d" % (layer, cb))
                last = None
                for c in range(4):
                    for tg in range(2):
                        b, ps, tS = fpat(blk, btok, c, QC, tg * 512, 512, cat_waits)
                        hv = hT[:, cb * 4 + c, tg * 512:(tg + 1) * 512]
                        ev = P.op("vector", lambda e, hv=hv, ps=ps: e.tensor_tensor(out=hv, in0=ps, in1=hv, op=ALU.add), waits=[tS] + list(h_waits))
                        PSA.release(b, [ev])
                        toks.append(ev)
                        last = tS
                WS.release(bi, [last])
            return toks

        def mlp_phase(layer, hT, nT):
            aT = QC
            rt = [vf(B + 108544, 512), vf(B + 110592, 512)]
            rt_war = [[], []]
            ri = 0
            h_toks = []
            a_war = bar()
            for kg in range(4):
                a_toks = []
                for j in range(4):
                    blk, btok, bi = WS.pop("up%d_%d" % (layer, kg * 4 + j))
                    last = None
                    for tg in range(2):
                        for c in range(4):
                            b, ps, tS = fpat(blk, btok, c, nT, tg * 512, 512, nwaits(tg * 512, 512))
                            s = ri % 2
                            ri += 1
                            rv = rt[s]
                            t1 = P.op("scalar", lambda e, rv=rv, ps=ps: e.activation(out=rv, in_=ps, func=AF.Square), waits=[tS] + rt_war[s])
                            av = aT[:, j * 4 + c, tg * 512:(tg + 1) * 512]
                            t2 = P.op("vector", lambda e, rv=rv, ps=ps, av=av: e.scalar_tensor_tensor(out=av, in0=ps, scalar=0.0, in1=rv, op0=ALU.is_gt, op1=ALU.mult),
                                      waits=[t1] + a_war)
                            rt_war[s] = [t2]
                            PSA.release(b, [t2])
                            a_toks.append(t2)
                            last = tS
                    WS.release(bi, [last])
                lastdn = None
                for cb in range(4):
                    blk, btok, bi = WS.pop("dn%d_%d_%d" % (layer, kg, cb))
                    last = None
                    for c in range(4):
                        for tg in range(2):
                            b, ps, tS = fpat(blk, btok, c, aT, tg * 512, 512, a_toks)
                            hv = hT[:, cb * 4 + c, tg * 512:(tg + 1) * 512]
                            ev = P.op("vector", lambda e, hv=hv, ps=ps: e.tensor_tensor(out=hv, in0=ps, in1=hv, op=ALU.add), waits=[tS])
                            PSA.release(b, [ev])
                            h_toks.append(ev)
                            last = tS
                    WS.release(bi, [last])
                    lastdn = last
                a_war = [lastdn]
            return h_toks

        def emit_out(srcT_of, dst, waits_of, otile, ot_war):
            dts = []
            for tt in range(8):
                srcT = srcT_of(tt)
                s = tt % 2
                evs = []
                for kq in range(4):
                    b = PSA.alloc()
                    pb = bank(b)
                    tk = None
                    for i in range(4):
                        kc = kq * 4 + i
                        sv = srcT[:, kc, :]
                        tk = P.op("tensor", lambda e, pb=pb, i=i, sv=sv: e.transpose(out=pb[:, i * 128:(i + 1) * 128], in_=sv, identity=ident),
                                  waits=(list(waits_of(tt)) + [t_ident] + PSA.rel[b]) if i == 0 else (), sig=(i == 3))
                    dstv = otile[s][:, kq * 512:(kq + 1) * 512]
                    ev = evac_copy("scalar" if kq % 2 else "vector", dstv, pb, [tk] + ot_war[s])
                    PSA.release(b, [ev])
                    evs.append(ev)
                td = P.dma("sync", dst[tt * 128:(tt + 1) * 128, :], otile[s], "o%d" % s, waits=evs)
                ot_war[s] = [td]
                dts.append(td)
            return dts

        final_waits = []
        TN = {"g": []}

        def nwaits(t0, n):
            out = []
            for g in range(t0 // 256, (t0 + n - 1) // 256 + 1):
                out += TN["g"][g]
            return out

        def nall():
            out = []
            for g in TN["g"]:
                out += g
            return out

        xinA = [vf(B + 40960, 2048), vf(B + 49152, 2048)]
        if do1:
            xinA += [vf(O_QC + 16384, 2048), vf(O_QC + 24576, 2048)]
        xTa = vf(B + 57344, 16, 256)
        xw = [[] for _ in xinA]
        xTa_war = []

        def mem_norm():
            tl = load_T(mem_d, xTa, 256, xinA, xw, xTa_war)
            return norm_T(xTa, 256, 0, mT, tl, [])

        if not do1:
            t_mT = mem_norm()

        if do1:
            nT0 = vb(B, 16, NEXT)
            TN["g"] = []
            xTb = [xTa, vf(B + 73728, 16, 256)]
            xT_war = [[], []]
            srcs = [x_d[gi * 256:(gi + 1) * 256, :] for gi in range(5)] + [mem_d]
            loads = {}

            def do_load(i):
                loads[i] = load_T(srcs[i], xTb[i % 2], 256, xinA, xw, xT_war[i % 2])

            do_load(0)
            for i in range(6):
                if i + 1 < 6:
                    do_load(i + 1)
                if i < 5:
                    tn = norm_T(xTb[i % 2], 256, 16, nT0[:, :, i * 256:(i + 1) * 256], loads[i], [])
                    TN["g"].append(tn)
                else:
                    tn = norm_T(xTb[i % 2], 256, 0, mT, loads[i], [])
                    t_mT = tn
                xT_war[i % 2] = tn
            t_q = []
            x_all = []
            for i in range(6):
                x_all += loads[i]
            for qi in range(4):
                blk, btok, bi = WS.pop("aq%d" % qi if qi < 3 else "aqm")
                last = None
                for tg in range(2):
                    for c in range(4):
                        b, ps, tS = fpat(blk, btok, c, nT0, tg * 512, 512, nwaits(tg * 512, 512))
                        ev = evac_copy("scalar", QC[:, qi * 4 + c, tg * 512:(tg + 1) * 512], ps, [tS] + (x_all if qi >= 2 else []))
                        PSA.release(b, [ev])
                        t_q.append(ev)
                        last = tS
                WS.release(bi, [last])
            t_kvm = mem_kv(0, t_mT, [])
            KTg = vb(B + 40960, 4, NEXT)
            Vg = vb(B + 51200, 10, 512)
            biasb = [vf(B + 61440, 1664), vf(B + 68096, 1664)]
            pT_flat = vb(B + 74752, 2048)
            pT = [pT_flat[:, i * 512:(i + 1) * 512] for i in range(4)]
            stmp = [vf(B + 78848, 640), vf(B + 81408, 640), vf(B + 94208, 640)]
            pT_na = vb(B + 88064, 3072)
            rl = [vf(B + 83968, 512), vf(B + 86016, 512)]
            kv_war = bar()
            bias_war = [list(kv_war), list(kv_war)]
            t_cat = mem_attn(t_q, t_kvm, pT, rl)
            for g in range(3):
                blk, btok, bi = WS.pop("ak%d" % g)
                t_k = []
                last = None
                for c in range(4):
                    for (t0, n) in ((0, 512), (512, 512), (1024, 256)):
                        b, ps, tS = fpat(blk, btok, c, nT0, t0, n, nwaits(t0, n))
                        ev = evac_copy("scalar", KTg[:, c, t0:t0 + n], ps, [tS] + kv_war)
                        PSA.release(b, [ev])
                        t_k.append(ev)
                        last = tS
                WS.release(bi, [last])
                blk, btok, bi = WS.pop("av%d" % g)
                for t in range(10):
                    b, ps, tS = tpat(blk, btok, nT0, t * 128, nwaits(t * 128, 128))
                    ev = evac_copy("vector" if t % 2 else "scalar", Vg[:, t, :], ps, [tS] + kv_war)
                    PSA.release(b, [ev])
                    t_k.append(ev)
                    last = tS
                WS.release(bi, [last])
                lastatt = None
                pend = []
                mul_q = []
                for hh in range(4):
                    h = g * 4 + hh
                    s = h % 2
                    t_b = P.dma("sync", biasb[s], bias_d[h], "bias%d" % s, waits=bias_war[s])
                    for m in range(8):
                        if m < 2:
                            tl_, boff = [0, 1, 2, 3], m * 512
                        else:
                            tl_, boff = list(range(m - 2, m + 3)), 1024
                        nt = len(tl_)
                        tiles = [(KTg[:, hh, t * 128:(t + 1) * 128], Vg[:, t, hh * 128:(hh + 1) * 128]) for t in tl_]
                        QT = QC[:, h, m * 128:(m + 1) * 128]
                        if len(pend) >= 3:
                            mul_q.append(pend.pop(0)())
                        if len(mul_q) >= 2:
                            lastatt = mul_q.pop(0)()
                            t_cat.append(lastatt)
                        pvs = attn_unit(QT, tiles, QT, 128, t_q, t_k, pT, rl, stmp=stmp,
                                        bias=biasb[s][:, boff:boff + nt * 128], bias_waits=[t_b])
                        pend.append(pvs)
                    bias_war[s] = [P.last("vector")]
                while pend:
                    mul_q.append(pend.pop(0)())
                    if len(mul_q) >= 2:
                        lastatt = mul_q.pop(0)()
                        t_cat.append(lastatt)
                while mul_q:
                    lastatt = mul_q.pop(0)()
                    t_cat.append(lastatt)
                kv_war = [P.last("tensor"), lastatt]
            hT = vf(B, 16, NTOK)
            xinC = [vf(B + 65536, 2048), vf(B + 73728, 2048)]
            bw = bar()
            xw = [list(bw), list(bw)]
            t_h = []
            for gi in range(4):
                t_h += load_T(x_d[gi * 256:(gi + 1) * 256, :], hT[:, :, gi * 256:(gi + 1) * 256], 256, xinC, xw, bw)
            t_h = wo_phase(0, hT, t_cat, t_h)
            nT = vb(B + 65536, 16, NTOK)
            nw = bar()
            TN["g"] = []
            for gi in range(4):
                TN["g"].append(norm_T(hT[:, :, gi * 256:(gi + 1) * 256], 256, 32, nT[:, :, gi * 256:(gi + 1) * 256], t_h, nw))
            if mode == "fused":
                t_kvm1 = mem_kv(1, t_mT, nw)
            t_h = mlp_phase(0, hT, nT)
            nw = bar()
            TN["g"] = []
            for gi in range(4):
                TN["g"].append(norm_T(hT[:, :, gi * 256:(gi + 1) * 256], 256, 48, nT[:, :, gi * 256:(gi + 1) * 256], t_h, nw))

        if mode == "s2":
            hT = vf(B, 16, NTOK)
            nT = vb(B + 65536, 16, NTOK)
            xinC = [vf(B + 65536, 2048), vf(B + 73728, 2048)]
            bw = bar()
            xw = [list(bw), list(bw)]
            t_h = []
            for gi in range(4):
                t_h += load_T(h1_i[gi * 256:(gi + 1) * 256, :], hT[:, :, gi * 256:(gi + 1) * 256], 256, xinC, xw, bw)
            nw = bar()
            TN["g"] = []
            for gi in range(4):
                TN["g"].append(norm_T(hT[:, :, gi * 256:(gi + 1) * 256], 256, 48, nT[:, :, gi * 256:(gi + 1) * 256], t_h, nw))

        cosT = vf(B + 108544, NTOK)
        sinT = vf(B + 112640, NTOK)
        sqq = vb(B + 116736, 512)
        rstq = vf(B + 117760, 512)
        tq = vf(B + 119808, 512)
        qh = vb(B + 121856, 512)
        t1b = vf(B + 122880, 512)
        t2b = vf(B + 124928, 512)
        qk = {"sqq": [], "rstq": [], "tq": [], "qh": [], "t1": [], "t2": []}
        bw = bar()
        t_cos = P.dma("sync", cosT, cos_d, "c_cos", waits=bw)
        t_sin = P.dma("sync", sinT, sin_d, "c_sin", waits=bw)

        def qk_post(b, ps, tS, gcol, t0, dst, dst_waits, done):
            t1 = P.op("scalar", lambda e: e.activation(out=sqq, in_=ps, func=AF.Square), waits=[tS] + qk["sqq"])
            res = {}

            def stage2():
                b2 = PSA.alloc()
                ps2 = bank(b2)
                t2 = mm(ps2, [(ones, sqq)], [t1, t_ones] + PSA.rel[b2])
                qk["sqq"] = [t2]
                t3 = P.op("scalar", lambda e: e.activation(out=tq, in_=ps2, func=AF.Ln, bias=epsT, scale=1.0 / 128), waits=[t2, t_eps] + qk["tq"])
                PSA.release(b2, [t3])
                t4 = P.op("scalar", lambda e: e.activation(out=rstq, in_=tq, func=AF.Exp, scale=-0.5), waits=[t3] + qk["rstq"])
                qk["tq"] = [t4]
                t5 = P.op("vector", lambda e: e.scalar_tensor_tensor(out=qh, in0=ps, scalar=gains[:, gcol:gcol + 1], in1=rstq, op0=ALU.mult, op1=ALU.mult),
                          waits=[t4, t_gains] + qk["qh"])
                PSA.release(b, [t5])
                qk["rstq"] = [t5]
                res["t5"] = t5

            def stage3():
                t5 = res["t5"]
                b3 = PSA.alloc()
                ps3 = bank(b3)
                t6 = mm(ps3, [(perm, qh)], [t5, t_perm] + PSA.rel[b3])
                cv = cosT[:, t0:t0 + 512]
                sv = sinT[:, t0:t0 + 512]
                t7 = P.op("vector", lambda e: e.tensor_tensor(out=t1b, in0=qh, in1=cv, op=ALU.mult), waits=[t5, t_cos] + qk["t1"])
                t8 = P.op("vector", lambda e: e.tensor_tensor(out=t2b, in0=ps3, in1=sv, op=ALU.mult), waits=[t6, t_sin] + qk["t2"])
                PSA.release(b3, [t8])
                t9 = P.op("vector", lambda e: e.tensor_tensor(out=dst, in0=t1b, in1=t2b, op=ALU.add), waits=[t7, t8] + list(dst_waits))
                qk["qh"] = [t6, t7]
                qk["t1"] = [t9]
                qk["t2"] = [t9]
                done(t9)

            defer(4, stage2)
            defer(20, stage3)

        if do1:
            kst = [vb(B + 98304, 1024), vb(B + 100352, 1024)]
            vst = [vb(B + 102400, 512), vb(B + 103424, 512)]
            kst_war = [nall(), nall()]
            vst_war = [nall(), nall()]
            kv_dmas = []
            blk, btok, bi = WS.pop("bv")
            last = None
            for t in range(8):
                s = t % 2
                b, ps, tS = tpat(blk, btok, nT, t * 128, nwaits(t * 128, 128))
                ev = evac_copy("scalar", vst[s], ps, [tS] + vst_war[s])
                PSA.release(b, [ev])
                td = P.dma("sync", kv_own[:, 4096 + t * 512: 4096 + (t + 1) * 512], vst[s], "kvo_v%d" % s, waits=[ev])
                vst_war[s] = [td]
                kv_dmas.append(td)
                last = tS
            WS.release(bi, [last])
            blk, btok, bi = WS.pop("bk")
            last = None
            for c in range(4):
                s = c % 2
                t9s = []

                def kdone(t9, c=c, s=s, t9s=t9s):
                    t9s.append(t9)
                    if len(t9s) == 2:
                        td = P.dma("sync", kv_own[:, c * 1024:(c + 1) * 1024], kst[s], "kvo_k%d" % s, waits=t9s)
                        kst_war[s] = [td]
                        kv_dmas.append(td)

                if c >= 2:
                    pe_flush()
                for tg in range(2):
                    b, ps, tS = fpat(blk, btok, c, nT, tg * 512, 512, nall())
                    qk_post(b, ps, tS, 97, tg * 512, kst[s][:, tg * 512:(tg + 1) * 512], list(kst_war[s]), kdone)
                    last = tS
            WS.release(bi, [last])
            pe_flush()
            final_waits += kv_dmas

        if mode == "s1":
            otile = [vf(B + 81920, 2048), vf(B + 90112, 2048)]
            bw = bar()
            ow = [list(bw), list(bw)]
            final_waits += emit_out(lambda tt: hT[:, :, tt * 128:(tt + 1) * 128], h1_o, lambda tt: t_h, otile, ow)

        kvfull_waits = []

        if do2:
            t_kvm = t_kvm1 if mode == "fused" else mem_kv(1, t_mT, bar())
            t_q = []
            q_war = bar()
            for qi in range(4):
                blk, btok, bi = WS.pop("bq%d" % qi if qi < 3 else "bqm")
                if qi == 0 and mode == "fused":
                    t_cc = P.custom("gpsimd",
                                    lambda e: e.collective_compute("AllGather", ALU.bypass, replica_groups=[[0, 1], [2, 3], [4, 5], [6, 7]],
                                                                   ins=[kv_own.opt()], outs=[kv_full.opt()]),
                                    "cc", waits=kv_dmas, inc=1)
                    kvfull_waits.append(t_cc)
                last = None
                for c in range(4):
                    for tg in range(2):
                        b, ps, tS = fpat(blk, btok, c, nT, tg * 512, 512, nall())
                        dst = QC[:, qi * 4 + c, tg * 512:(tg + 1) * 512]
                        if qi < 3:
                            qk_post(b, ps, tS, 96, tg * 512, dst, q_war, t_q.append)
                        else:
                            ev = evac_copy("scalar", dst, ps, [tS] + q_war)
                            PSA.release(b, [ev])
                            t_q.append(ev)
                        last = tS
                WS.release(bi, [last])
            pe_flush()
            KTf = vb(B + 65536, 4, 2048)
            Vf = vb(B + 81920, 16, 512)
            pT_flat = vb(B + 98304, 4096)
            pT = [pT_flat[:, i * 512:(i + 1) * 512] for i in range(8)]
            rl = [vf(B + 106496, 512), vf(B + 108544, 512)]
            bw = bar()
            if mode == "fused":
                bw = bw + kv_dmas
            att["NP"] = 4
            att["pT_war"] = [list(bw) for _ in range(8)]
            att["rl_war"] = [list(bw), list(bw)]
            t_kv = []
            for r in range(2):
                t_kv.append(P.dma("sync", KTf[:, :, r * 1024:(r + 1) * 1024],
                                  kv_full[r * 128:(r + 1) * 128, 0:4096].rearrange("p (h t) -> p h t", t=1024), "kvl", waits=bw + kvfull_waits))
                t_kv.append(P.dma("sync", Vf[:, r * 8:(r + 1) * 8, :],
                                  kv_full[r * 128:(r + 1) * 128, 4096:8192].rearrange("p (t n) -> p t n", n=512), "kvl", waits=bw + kvfull_waits))
            t_cat = mem_attn(t_q, t_kvm, pT, rl)
            for h in range(12):
                kvh = h // 3
                for tg in range(2):
                    QT = QC[:, h, tg * 512:(tg + 1) * 512]
                    tiles = [(KTf[:, kvh, kt * 128:(kt + 1) * 128], Vf[:, kt, kvh * 128:(kvh + 1) * 128]) for kt in range(16)]
                    t_cat.append(attn_unit(QT, tiles, QT, 512, t_q, t_kv, pT, rl))
            t_h = wo_phase(1, hT, t_cat, t_h)
            nw = bar()
            TN["g"] = []
            for gi in range(4):
                TN["g"].append(norm_T(hT[:, :, gi * 256:(gi + 1) * 256], 256, 64, nT[:, :, gi * 256:(gi + 1) * 256], t_h, nw))
            t_h = mlp_phase(1, hT, nT)
            yTs = [vf(B + 65536, 16, 256), vf(B + 81920, 16, 256)]
            otile = [vf(O_QC, 2048), vf(O_QC + 8192, 2048), vf(O_QC + 16384, 2048), vf(O_QC + 24576, 2048)]
            bw = bar()
            ow = [list(bw) for _ in range(4)]
            y_wars = [list(bw), list(bw)]
            tys = {}

            def fin_norm(gi):
                tys[gi] = norm_T(hT[:, :, gi * 256:(gi + 1) * 256], 256, 80, yTs[gi % 2], t_h, y_wars[gi % 2])

            fin_norm(0)
            for gi in range(4):
                yT = yTs[gi % 2]
                if gi + 1 < 4 and gi >= 1:
                    pass
                if gi + 1 < 4 and gi == 0:
                    fin_norm(1)
                ty = tys[gi]
                dts = []
                for tt in range(2):
                    srcT = yT[:, :, tt * 128:(tt + 1) * 128]
                    s = (gi * 2 + tt) % 4
                    evs = []
                    lastk = None
                    for kq in range(4):
                        b = PSA.alloc()
                        pb = bank(b)
                        tk = None
                        for i in range(4):
                            kc = kq * 4 + i
                            sv = srcT[:, kc, :]
                            tk = P.op("tensor", lambda e, pb=pb, i=i, sv=sv: e.transpose(out=pb[:, i * 128:(i + 1) * 128], in_=sv, identity=ident),
                                      waits=(list(ty) + [t_ident] + PSA.rel[b]) if i == 0 else (), sig=(i == 3))
                        dstv = otile[s][:, kq * 512:(kq + 1) * 512]
                        ev = evac_copy("scalar" if kq % 2 else "vector", dstv, pb, [tk] + ow[s])
                        PSA.release(b, [ev])
                        evs.append(ev)
                        lastk = tk
                    row = gi * 256 + tt * 128
                    td = P.dma("sync", out_d[row:row + 128, :], otile[s], "o%d" % s, waits=evs)
                    ow[s] = [td]
                    final_waits.append(td)
                y_wars[gi % 2] = [lastk]
                if gi + 2 < 4:
                    fin_norm(gi + 2)

        P.wait_only("sync", final_waits)
        P.replay()
    return nc


def _true_row(l, hf):
    return l if hf == 0 else 31 - l


def _bias_tables(rpb, hf):
    units = [(0, [0, 1, 2, 3]), (1, [0, 1, 2, 3]), (2, [0, 1, 2, 3, 4])]
    p = np.arange(128)
    ki, kc = p // 64, p % 64
    qi, qc = p // 64, p % 64
    cols = []
    for m, tl in units:
        for t in tl:
            kr = np.array([_true_row(2 * t + a, hf) for a in ki])[:, None]
            qr = np.array([_true_row(2 * m + a, hf) for a in qi])[None, :]
            r0 = np.clip(qr - 4, 0, 24)
            vr = (kr >= r0) & (kr < r0 + 8)
            c0 = np.clip(qc - 8, 0, 48)[None, :]
            vc = (kc[:, None] >= c0) & (kc[:, None] < c0 + 16)
            dr = np.clip(kr - qr + 7, 0, 14)
            dc = np.clip(kc[:, None] - qc[None, :] + 15, 0, 30)
            valid = vr & vc
            g = rpb[:, dr, dc]
            cols.append(np.where(valid[None], g, np.float32(MASKV)).astype(np.float32))
    return np.ascontiguousarray(np.concatenate(cols, axis=2))


def _rope_tables(hf):
    t = np.arange(NTOK)
    row = np.array([_true_row(l, hf) for l in (t // 64)], dtype=np.float32)
    col = (t % 64).astype(np.float32)
    inv = np.power(np.float32(10000.0), -np.arange(0, 64, 2, dtype=np.float32) / np.float32(64)).astype(np.float32)
    d = np.arange(128)
    f = d % 32
    pos = np.where((d < 64)[:, None], row[None, :], col[None, :]).astype(np.float32)
    ang = (pos * inv[f][:, None]).astype(np.float32)
    cosT = np.cos(ang).astype(np.float32)
    sgn = np.where((d % 64) < 32, -1.0, 1.0).astype(np.float32)[:, None]
    sinT = (np.sin(ang).astype(np.float32) * sgn).astype(np.float32)
    return np.ascontiguousarray(cosT), np.ascontiguousarray(sinT)


def _fm(vec):
    return np.asarray(vec, dtype=np.float32).reshape(-1, 128).T


_CACHE = {}


def _get_nc(mode):
    if mode not in _CACHE:
        _CACHE[mode] = build(mode)
    return _CACHE[mode]


def kernel(x, mem, mem_norm, attn_norm, mlp_norm, a_w_in, a_rpb, b_w_in, b_q_norm, b_k_norm,
           w_mem_kv, w_o, w_up, w_down, final_norm, _mode="fused"):
    x = np.asarray(x, dtype=np.float32)
    mem = np.asarray(mem, dtype=np.float32)
    gains = np.concatenate([_fm(mem_norm), _fm(attn_norm[0]), _fm(mlp_norm[0]), _fm(attn_norm[1]), _fm(mlp_norm[1]),
                            _fm(final_norm), np.asarray(b_q_norm[0], np.float32)[:, None], np.asarray(b_k_norm[0], np.float32)[:, None]], axis=1)
    gains = np.ascontiguousarray(gains.astype(np.float32))
    d = np.arange(128)
    partner = np.where((d % 64) < 32, d + 32, d - 32)
    perm = np.zeros((128, 128), np.float32)
    perm[partner, d] = 1.0
    ident = np.eye(128, dtype=np.float32)
    rpb = np.asarray(a_rpb[0], np.float32)
    bias = [_bias_tables(rpb, 0), _bias_tables(rpb, 1)]
    rope = [_rope_tables(0), _rope_tables(1)]
    common = {
        "ident": ident, "gains": gains, "perm": perm,
        "b_w_in": np.ascontiguousarray(np.asarray(b_w_in[0], np.float32)),
        "w_mem_kv": np.asarray(w_mem_kv, np.float32).reshape(2 * D, 1024),
        "w_o": np.asarray(w_o, np.float32).reshape(2 * D, D),
        "w_up": np.asarray(w_up, np.float32).reshape(2 * D, 8192),
        "w_down": np.asarray(w_down, np.float32).reshape(2 * 8192, D),
    }
    a_in = np.ascontiguousarray(np.asarray(a_w_in[0], np.float32))
    maps1 = []
    for c in range(8):
        b, hf = c // 2, c % 2
        xb = x[b].reshape(32, 64, D)
        if hf:
            xb = xb[::-1]
        m = dict(common)
        m["x_ext"] = np.ascontiguousarray(xb[:20].reshape(NEXT, D))
        m["mem_b"] = np.ascontiguousarray(mem[b])
        m["bias0"] = bias[hf]
        m["a_w_in"] = a_in
        m["cosT"], m["sinT"] = rope[hf]
        maps1.append(m)
    if _mode == "fused":
        res = run_bass_kernel_spmd(_get_nc("fused"), maps1, core_ids=list(range(8)))
        outs = [r["out"] for r in res.results]
    else:
        res1 = run_bass_kernel_spmd(_get_nc("s1"), maps1, core_ids=list(range(8)))
        maps2 = []
        for c in range(8):
            b, hf = c // 2, c % 2
            m = dict(common)
            m["mem_b"] = maps1[c]["mem_b"]
            m["cosT"], m["sinT"] = rope[hf]
            m["h1"] = np.asarray(res1.results[c]["h1"])
            own = np.asarray(res1.results[c]["kv_own"])
            oth = np.asarray(res1.results[c ^ 1]["kv_own"])
            pair = [own, oth] if hf == 0 else [oth, own]
            m["kv_full"] = np.ascontiguousarray(np.concatenate(pair, axis=0))
            maps2.append(m)
        res2 = run_bass_kernel_spmd(_get_nc("s2"), maps2, core_ids=list(range(8)))
        outs = [r["out"] for r in res2.results]
    out = np.empty((4, 2048, D), np.float32)
    for c in range(8):
        b, hf = c // 2, c % 2
        ob = np.asarray(outs[c], np.float32).reshape(16, 64, D)
        if hf:
            ob = ob[::-1]
        out[b, hf * 1024:(hf + 1) * 1024] = ob.reshape(NTOK, D)
    return out
```
